# Optimizing a Trainium2 kernel written in Bass

```python
import math
import jax, jax.numpy as jnp
from jax import lax
import numpy as np

D_MODEL = 1024
BATCH = 4
SEQ = 8192
DEPTH = 2

N_META = 16
CONV_WIDTH = 512
CONV_GROUPS = 8
CONV_K = 3
GDN_HEADS = 4
GDN_DK = 128
GDN_DV = 128
GDN_CONV_K = 4
GDN_CHUNK = 64
D_MIX = CONV_WIDTH + GDN_HEADS * GDN_DV
QKV_COLS = GDN_HEADS * (2 * GDN_DK + GDN_DV)
PROJ_COLS = 3 * CONV_WIDTH + QKV_COLS + GDN_HEADS * GDN_DV + 2 * GDN_HEADS
PEER_HEADS = 8
PEER_NKEYS = 128
PEER_N_EXPERTS = PEER_NKEYS * PEER_NKEYS
PEER_DKEY = 256
PEER_TOPK = 16
PEER_BLOCK = 256
DEEPNORM_ALPHA = (2.0 * DEPTH) ** 0.25
DEEPNORM_BETA = (8.0 * DEPTH) ** -0.25
LN_EPS = 1e-5
RMS_EPS = 1e-6

kernel_name = "hymba_conv_gdn_peer_deepnorm"


def layer_norm(x, g, b):
    xf = x.astype(jnp.float32)
    mu = xf.mean(-1, keepdims=True)
    var = jnp.square(xf - mu).mean(-1, keepdims=True)
    return ((xf - mu) * lax.rsqrt(var + LN_EPS) * g.astype(jnp.float32) + b.astype(jnp.float32)).astype(x.dtype)


def causal_dwconv(x, w):
    K = w.shape[0]
    L = x.shape[1]
    xp = jnp.pad(x, ((0, 0), (K - 1, 0), (0, 0)))
    y = xp[:, 0:L] * w[0]
    for j in range(1, K):
        y = y + xp[:, j:j + L] * w[j]
    return y


def l2norm(x):
    return x * lax.rsqrt(jnp.sum(x * x, axis=-1, keepdims=True) + RMS_EPS)


def gated_delta_rule_chunked(q, k, v, g, beta):
    Bb, H, T, DK = q.shape
    DV = v.shape[-1]
    C = GDN_CHUNK
    N = T // C
    q = q * (DK ** -0.5)
    q = q.reshape(Bb, H, N, C, DK)
    k = k.reshape(Bb, H, N, C, DK)
    v = v.reshape(Bb, H, N, C, DV)
    g = jnp.cumsum(g.reshape(Bb, H, N, C), axis=-1)
    beta = beta.reshape(Bb, H, N, C)
    causal = jnp.tril(jnp.ones((C, C), dtype=bool))
    strict = jnp.tril(jnp.ones((C, C), dtype=bool), -1)
    gdiff = g[..., :, None] - g[..., None, :]
    decay = jnp.where(causal, jnp.exp(jnp.where(causal, gdiff, 0.0)), 0.0)
    k_beta = k * beta[..., None]
    v_beta = v * beta[..., None]
    A = jnp.where(strict, jnp.einsum('bhncd,bhnsd->bhncs', k_beta, k) * decay, 0.0)
    eye = jnp.eye(C, dtype=q.dtype)
    t_inv = lax.linalg.triangular_solve(eye + A, jnp.broadcast_to(eye, A.shape),
                                        left_side=True, lower=True, unit_diagonal=True)
    u = jnp.einsum('bhncs,bhnsv->bhncv', t_inv, v_beta)
    w = jnp.einsum('bhncs,bhnsd->bhncd', t_inv, k_beta * jnp.exp(g)[..., None])
    attn_intra = jnp.where(causal, jnp.einsum('bhncd,bhnsd->bhncs', q, k) * decay, 0.0)
    g_last = g[..., -1]
    k_dec = k * jnp.exp(g_last[..., None] - g)[..., None]
    q_dec = q * jnp.exp(g)[..., None]

    def step(S, inp):
        q_c, k_c, u_c, w_c, a_c, gl_c = inp
        v_new = u_c - jnp.einsum('bhcd,bhdv->bhcv', w_c, S)
        o_c = jnp.einsum('bhcd,bhdv->bhcv', q_c, S) + jnp.einsum('bhcs,bhsv->bhcv', a_c, v_new)
        S = S * jnp.exp(gl_c)[..., None, None] + jnp.einsum('bhcd,bhcv->bhdv', k_c, v_new)
        return S, o_c

    to_front = lambda t: jnp.moveaxis(t, 2, 0)
    xs = (to_front(q_dec), to_front(k_dec), to_front(u), to_front(w), to_front(attn_intra), to_front(g_last))
    S0 = jnp.zeros((Bb, H, DK, DV), dtype=q.dtype)
    _, o = lax.scan(step, S0, xs)
    return jnp.moveaxis(o, 0, 2).reshape(Bb, H, T, DV)


def hybrid_mixer(x, w_in, conv_w, gdn_conv_w, a_log, dt_bias, gdn_norm_w, w_out):
    Bb, L, _ = x.shape
    proj = x @ w_in
    s0 = CONV_WIDTH
    s1 = 2 * CONV_WIDTH
    s2 = 3 * CONV_WIDTH
    s3 = s2 + QKV_COLS
    s4 = s3 + GDN_HEADS * GDN_DV
    s5 = s4 + GDN_HEADS
    cb, cc, ch, qkv, z, a, b = jnp.split(proj, [s0, s1, s2, s3, s4, s5], axis=-1)

    y_conv = cb * causal_dwconv(cc * ch, conv_w)

    qkv = jax.nn.silu(causal_dwconv(qkv, gdn_conv_w)).astype(jnp.float32)
    q, k, v = jnp.split(qkv, [GDN_HEADS * GDN_DK, 2 * GDN_HEADS * GDN_DK], axis=-1)
    q = l2norm(q.reshape(Bb, L, GDN_HEADS, GDN_DK))
    k = l2norm(k.reshape(Bb, L, GDN_HEADS, GDN_DK))
    v = v.reshape(Bb, L, GDN_HEADS, GDN_DV)
    g = -jnp.exp(a_log.astype(jnp.float32)) * jax.nn.softplus(a.astype(jnp.float32) + dt_bias.astype(jnp.float32))
    beta = jax.nn.sigmoid(b.astype(jnp.float32))
    pad = (-N_META) % GDN_CHUNK
    p4 = ((0, 0), (pad, 0), (0, 0), (0, 0))
    p3 = ((0, 0), (pad, 0), (0, 0))
    qh = jnp.pad(q, p4).transpose(0, 2, 1, 3)
    kh = jnp.pad(k, p4).transpose(0, 2, 1, 3)
    vh = jnp.pad(v, p4).transpose(0, 2, 1, 3)
    gh = jnp.pad(g, p3).transpose(0, 2, 1)
    bh = jnp.pad(beta, p3).transpose(0, 2, 1)
    o = gated_delta_rule_chunked(qh, kh, vh, gh, bh)[:, :, pad:]
    o = o.transpose(0, 2, 1, 3)
    zf = z.astype(jnp.float32).reshape(Bb, L, GDN_HEADS, GDN_DV)
    o = o * lax.rsqrt(jnp.mean(o * o, axis=-1, keepdims=True) + RMS_EPS) * gdn_norm_w.astype(jnp.float32) * jax.nn.silu(zf)
    y_gdn = o.reshape(Bb, L, GDN_HEADS * GDN_DV).astype(x.dtype)

    return jnp.concatenate([y_conv.astype(x.dtype), y_gdn], axis=-1) @ w_out


def peer_ffn(x, w_q, sub_k1, sub_k2, expert_u, expert_v):
    Bb, L, D = x.shape
    T = Bb * L
    n_blk = -(-T // PEER_BLOCK)
    xt = jnp.pad(x.reshape(T, D), ((0, n_blk * PEER_BLOCK - T), (0, 0))).reshape(n_blk, PEER_BLOCK, D)
    half = PEER_DKEY // 2

    def block(xb):
        q = (xb @ w_q).astype(jnp.float32).reshape(PEER_BLOCK, PEER_HEADS, 2, half)
        s1 = jnp.einsum('thd,kd->thk', q[:, :, 0], sub_k1.astype(jnp.float32))
        s2 = jnp.einsum('thd,kd->thk', q[:, :, 1], sub_k2.astype(jnp.float32))
        t1, i1 = lax.top_k(s1, PEER_TOPK)
        t2, i2 = lax.top_k(s2, PEER_TOPK)
        cand = (t1[..., :, None] + t2[..., None, :]).reshape(PEER_BLOCK, PEER_HEADS, PEER_TOPK * PEER_TOPK)
        top, ci = lax.top_k(cand, PEER_TOPK)
        e1 = jnp.take_along_axis(i1, ci // PEER_TOPK, axis=-1)
        e2 = jnp.take_along_axis(i2, ci % PEER_TOPK, axis=-1)
        idx = (e1 * PEER_NKEYS + e2).reshape(PEER_BLOCK, PEER_HEADS * PEER_TOPK)
        gate = jax.nn.softmax(top, axis=-1).reshape(PEER_BLOCK, PEER_HEADS * PEER_TOPK)
        u = expert_u[idx]
        act = jax.nn.gelu(jnp.einsum('td,ted->te', xb, u).astype(jnp.float32), approximate=False)
        coef = (gate * act).astype(xb.dtype)
        v = expert_v[idx]
        return jnp.einsum('te,ted->td', coef, v)

    y = lax.map(block, xt)
    return y.reshape(n_blk * PEER_BLOCK, D)[:T].reshape(Bb, L, D)


def setup_inputs(seed: int = 0) -> dict:
    key = jax.random.key(seed)
    ks = jax.random.split(key, 24)
    f32 = jnp.float32
    nrm = lambda k, shape, s: jax.random.normal(k, shape, f32) * s
    dt = jnp.exp(jax.random.uniform(ks[7], (DEPTH, GDN_HEADS), f32, math.log(1e-3), math.log(1e-1)))
    return {
        "x": jax.random.normal(ks[0], (BATCH, SEQ, D_MODEL), f32),
        "meta_tokens": nrm(ks[1], (N_META, D_MODEL), 1.0),
        "ln_in_g": 1.0 + nrm(ks[2], (D_MODEL,), 0.02),
        "ln_in_b": nrm(ks[3], (D_MODEL,), 0.02),
        "w_in": nrm(ks[4], (DEPTH, D_MODEL, PROJ_COLS), D_MODEL ** -0.5),
        "conv_w": nrm(ks[5], (DEPTH, CONV_K, CONV_WIDTH), CONV_K ** -0.5),
        "gdn_conv_w": nrm(ks[6], (DEPTH, GDN_CONV_K, QKV_COLS), GDN_CONV_K ** -0.5),
        "a_log": jnp.log(jax.random.uniform(ks[8], (DEPTH, GDN_HEADS), f32, 1.0, 16.0)),
        "dt_bias": dt + jnp.log(-jnp.expm1(-dt)),
        "gdn_norm_w": 1.0 + nrm(ks[9], (DEPTH, GDN_DV), 0.02),
        "w_out": nrm(ks[10], (DEPTH, D_MIX, D_MODEL), D_MIX ** -0.5 * DEEPNORM_BETA),
        "ln1_g": 1.0 + nrm(ks[11], (DEPTH, D_MODEL), 0.02),
        "ln1_b": nrm(ks[12], (DEPTH, D_MODEL), 0.02),
        "peer_w_q": nrm(ks[13], (DEPTH, D_MODEL, PEER_HEADS * PEER_DKEY), D_MODEL ** -0.5),
        "peer_k1": nrm(ks[14], (DEPTH, PEER_NKEYS, PEER_DKEY // 2), (PEER_DKEY // 2) ** -0.5),
        "peer_k2": nrm(ks[15], (DEPTH, PEER_NKEYS, PEER_DKEY // 2), (PEER_DKEY // 2) ** -0.5),
        "peer_u": nrm(ks[16], (DEPTH, PEER_N_EXPERTS, D_MODEL), D_MODEL ** -0.5),
        "peer_v": nrm(ks[17], (DEPTH, PEER_N_EXPERTS, D_MODEL), (PEER_HEADS * PEER_TOPK) ** -0.5 * DEEPNORM_BETA),
        "ln2_g": 1.0 + nrm(ks[18], (DEPTH, D_MODEL), 0.02),
        "ln2_b": nrm(ks[19], (DEPTH, D_MODEL), 0.02),
    }


def reference(x, meta_tokens, ln_in_g, ln_in_b, w_in, conv_w, gdn_conv_w, a_log, dt_bias, gdn_norm_w,
              w_out, ln1_g, ln1_b, peer_w_q, peer_k1, peer_k2, peer_u, peer_v, ln2_g, ln2_b):
    Bb = x.shape[0]
    meta = jnp.broadcast_to(meta_tokens[None].astype(x.dtype), (Bb, N_META, D_MODEL))
    h = jnp.concatenate([meta, x], axis=1)
    h = layer_norm(h, ln_in_g, ln_in_b)
    for l in range(DEPTH):
        mix = hybrid_mixer(h, w_in[l], conv_w[l], gdn_conv_w[l], a_log[l], dt_bias[l], gdn_norm_w[l], w_out[l])
        h = layer_norm(DEEPNORM_ALPHA * h + mix, ln1_g[l], ln1_b[l])
        ffn = peer_ffn(h, peer_w_q[l], peer_k1[l], peer_k2[l], peer_u[l], peer_v[l])
        h = layer_norm(DEEPNORM_ALPHA * h + ffn, ln2_g[l], ln2_b[l])
    return h[:, N_META:]
```

```python
import numpy as np
from contextlib import ExitStack
import concourse.bass as bass
import concourse.mybir as mybir
from concourse.bass_utils import run_bass_kernel_spmd

F32 = mybir.dt.float32
F32R = mybir.dt.float32r
BF16 = mybir.dt.bfloat16
U32 = mybir.dt.uint32
I32 = mybir.dt.int32
AF = mybir.ActivationFunctionType
ALU = mybir.AluOpType
AX = mybir.AxisListType

D = 1024
PC = 3592
NMETA = 16
SEQ = 8192
DEPTH = 2
ALPHA = (2.0 * DEPTH) ** 0.25
BIG = 30000.0
NEXP = 16384
G_TOK = 4
NSLOT = 16
NZ = 4


class Buf:
    def __init__(self, name):
        self.name = name
        self.w = None
        self.r = []
        self.dsem = None
        self.dcnt = 0
        self.excl = False


class T:
    def __init__(self, t, name):
        self.t = t
        self.b = Buf(name)


class Sched:
    def __init__(self, nc, stack):
        self.nc = nc
        self.stack = stack
        self.eng = {}
        for nm, e in [("pe", nc.tensor), ("act", nc.scalar), ("dve", nc.vector), ("pool", nc.gpsimd), ("sp", nc.sync)]:
            sem = stack.enter_context(nc.semaphore("s_" + nm))
            self.eng[nm] = dict(e=e, sem=sem, cnt=0, waited={}, name=nm)
        self.pool = []
        self.ninst = 0

    def _wait(self, E, deps):
        for (sem, val) in deps:
            key = id(sem)
            if E["waited"].get(key, 0) < val:
                E["e"].wait_ge(sem, val)
                E["waited"][key] = val
                self.ninst += 1

    def _deps(self, E, reads, writes):
        deps = []
        for b in reads:
            if b.w is not None:
                deps.append(b.w)
        for b in writes:
            if b.w is not None:
                deps.append(b.w)
            deps.extend(b.r)
        if E["name"] == "pe":
            deps = [d for d in deps if d[0] is not E["sem"]]
        return deps

    def _mark(self, tok, reads, writes):
        for b in reads:
            b.r = [x for x in b.r if x[0] is not tok[0]]
            b.r.append(tok)
        for b in writes:
            b.w = tok
            b.r = []

    def op(self, en, fn, reads=(), writes=()):
        E = self.eng[en]
        reads = [x.b if isinstance(x, T) else x for x in reads]
        writes = [x.b if isinstance(x, T) else x for x in writes]
        writes = writes + [b for b in reads if b.excl and b not in writes]
        reads = [b for b in reads if not b.excl]
        self._wait(E, self._deps(E, reads, writes))
        inst = fn(E["e"])
        E["cnt"] += 1
        inst.then_inc(E["sem"], 1)
        self.ninst += 1
        self._mark((E["sem"], E["cnt"]), reads, writes)
        return inst

    def dma(self, en, fn, owner, reads=(), writes=()):
        E = self.eng[en]
        owner = owner.b if isinstance(owner, T) else owner
        reads = [x.b if isinstance(x, T) else x for x in reads]
        writes = [x.b if isinstance(x, T) else x for x in writes]
        self._wait(E, self._deps(E, reads, writes))
        kind = "sw" if en == "pool" else "hw"
        if owner.dsem is None:
            owner.dsem = {}
        if kind not in owner.dsem:
            free = [p for p in self.pool if p["owner"] is None and p["kind"] == kind]
            if free:
                ent = free[0]
            else:
                ent = dict(sem=self.stack.enter_context(self.nc.semaphore("dq%d" % len(self.pool))), cnt=0, owner=None, kind=kind)
                self.pool.append(ent)
            ent["owner"] = owner
            owner.dsem[kind] = ent
        ent = owner.dsem[kind]
        inst = fn(E["e"])
        ent["cnt"] += 16
        inst.then_inc(ent["sem"], 16)
        self.ninst += 1
        self._mark((ent["sem"], ent["cnt"]), reads, writes)
        return inst

    def barrier(self, release=True):
        toks = [(E["sem"], E["cnt"]) for E in self.eng.values() if E["cnt"] > 0]
        toks += [(p["sem"], p["cnt"]) for p in self.pool if p["cnt"] > 0]
        for E in self.eng.values():
            self._wait(E, [t for t in toks if t[0] is not E["sem"]])
        if release:
            for p in self.pool:
                if p["owner"] is not None:
                    p["owner"].dsem = None
                    p["owner"] = None


def fr(ap):
    return ap.bitcast(F32R)


def bc(ap, shape):
    return ap.to_broadcast(list(shape))


def build(NT=64, depth=DEPTH, dbg=False, skip_p2=False, skip_p1=False, cut=99):
    nc = bass.Bass("TRN2", target_bir_lowering=False)
    LTOK = 128 * (NT + 1)
    dr = {}

    def din(name, shape, dt=F32):
        dr[name] = nc.dram_tensor(name, list(shape), dt, kind="ExternalInput").ap()
        return dr[name]

    x = din("x", [128 * NT, D])
    meta = din("meta_tokens", [NMETA, D])
    ln_in_g = din("ln_in_g", [D]); ln_in_b = din("ln_in_b", [D])
    w_in = din("w_in", [DEPTH, D, PC])
    conv_w = din("conv_w", [DEPTH, 3, 512])
    gdn_conv_w = din("gdn_conv_w", [DEPTH, 4, 1536])
    a_log = din("a_log", [DEPTH, 4]); dt_bias = din("dt_bias", [DEPTH, 4])
    gdn_norm_w = din("gdn_norm_w", [DEPTH, 128])
    w_out = din("w_out", [DEPTH, D, D])
    ln1_g = din("ln1_g", [DEPTH, D]); ln1_b = din("ln1_b", [DEPTH, D])
    peer_w_q = din("peer_w_q", [DEPTH, D, 2048])
    peer_k1 = din("peer_k1", [DEPTH, 128, 128]); peer_k2 = din("peer_k2", [DEPTH, 128, 128])
    peer_u = din("peer_u", [DEPTH, NEXP, D]); peer_v = din("peer_v", [DEPTH, NEXP, D])
    ln2_g = din("ln2_g", [DEPTH, D]); ln2_b = din("ln2_b", [DEPTH, D])
    out = nc.dram_tensor("out", [128 * NT, D], F32, kind="ExternalOutput").ap()
    hA = nc.dram_tensor("hA", [LTOK, D], F32, kind="Internal").ap()
    hB = nc.dram_tensor("hB", [LTOK, D], F32, kind="Internal").ap()
    TAB = nc.dram_tensor("TAB", [DEPTH * NEXP, 2 * D], BF16, kind="Internal").ap()
    dbg_out = {}
    if dbg:
        dbg_out["d_h0"] = nc.dram_tensor("d_h0", [LTOK, D], F32, kind="ExternalOutput").ap()
        dbg_out["d_h2"] = nc.dram_tensor("d_h2", [LTOK, D], F32, kind="ExternalOutput").ap()
        dbg_out["d_h3"] = nc.dram_tensor("d_h3", [LTOK, D], F32, kind="ExternalOutput").ap()

    tiles = [(128 * i, 128) for i in range(NT + 1)]

    top = ExitStack()
    with top:
        S = Sched(nc, top)

        uniq = [0]

        def mk(st, name, shape, dt=F32):
            uniq[0] += 1
            name = "%s_%d" % (name, uniq[0])
            return T(st.enter_context(nc.sbuf_tensor(name, list(shape), dt)), name)

        PSB = top.enter_context(nc.psum_tensor("psb", [128, 4096], F32))
        PS = [T(PSB[:, 512 * i:512 * (i + 1)], "ps%d" % i) for i in range(8)]
        XBV = [PSB[:, 2048 + 1024 * j:2048 + 1024 * (j + 1)] for j in range(2)]
        for p_ in PS:
            p_.b.excl = True
        psctr = [0]

        def nps(lo=0, hi=8):
            i = lo + psctr[0] % (hi - lo)
            psctr[0] += 1
            return PS[i]

        def v4(p):
            return p.t[:].rearrange("p (a b) -> p a b", a=4)

        ident = mk(top, "ident", [128, 128])
        identb = mk(top, "identb", [128, 128], BF16)
        ones = mk(top, "ones", [128, 128])
        LT = mk(top, "LT", [128, 128])
        UTS = mk(top, "UTS", [128, 128])
        MPOS = mk(top, "MPOS", [128, 128])
        MNEG = mk(top, "MNEG", [128, 128])
        iot = mk(top, "iot", [128, 128], I32)
        iof = mk(top, "iof", [128, 128])
        iota16 = mk(top, "iota16", [128, 16])
        S.op("pool", lambda e: e.iota(iot.t[:], pattern=[[1, 128]], base=0, channel_multiplier=-1), writes=[iot])
        S.op("dve", lambda e: e.tensor_copy(out=iof.t[:], in_=iot.t[:]), reads=[iot], writes=[iof])
        S.op("dve", lambda e: e.tensor_scalar(out=ident.t[:], in0=iof.t[:], scalar1=0.0, scalar2=None, op0=ALU.is_equal), reads=[iof], writes=[ident])
        S.op("dve", lambda e: e.tensor_copy(out=identb.t[:], in_=ident.t[:]), reads=[ident], writes=[identb])
        S.op("dve", lambda e: e.memset(ones.t[:], 1.0), writes=[ones])
        onesr = mk(top, "onesr", [128, 128])
        S.op("dve", lambda e: e.tensor_scalar(out=fr(onesr.t[:]), in0=iof.t[:], scalar1=0.0, scalar2=1.0, op0=ALU.mult, op1=ALU.add), reads=[iof], writes=[onesr])
        S.op("dve", lambda e: e.tensor_scalar(out=LT.t[:], in0=iof.t[:], scalar1=0.0, scalar2=None, op0=ALU.is_ge), reads=[iof], writes=[LT])
        S.op("dve", lambda e: e.tensor_scalar(out=UTS.t[:], in0=iof.t[:], scalar1=0.0, scalar2=None, op0=ALU.is_lt), reads=[iof], writes=[UTS])
        S.op("dve", lambda e: e.tensor_scalar(out=MPOS.t[:], in0=iof.t[:], scalar1=0.0, scalar2=BIG, op0=ALU.is_ge, op1=ALU.mult), reads=[iof], writes=[MPOS])
        S.op("dve", lambda e: e.tensor_scalar(out=MNEG.t[:], in0=iof.t[:], scalar1=0.0, scalar2=-BIG, op0=ALU.is_lt, op1=ALU.mult), reads=[iof], writes=[MNEG])
        padmask = mk(top, "padmask", [128, 1])
        S.op("dve", lambda e: e.tensor_scalar(out=padmask.t[:], in0=iof.t[:, 0:1], scalar1=-111.5, scalar2=None, op0=ALU.is_lt), reads=[iof], writes=[padmask])
        iot2 = mk(top, "iot2", [128, 16], I32)
        S.op("pool", lambda e: e.iota(iot2.t[:], pattern=[[1, 16]], base=0, channel_multiplier=0), writes=[iot2])
        S.op("dve", lambda e: e.tensor_copy(out=iota16.t[:], in_=iot2.t[:]), reads=[iot2], writes=[iota16])

        def layer_norm(st_tiles, X, nt, Gt, Bt, eng2="pool"):
            stats, mv, rstd = st_tiles
            for c in range(2):
                S.op("dve", lambda e: e.bn_stats(out=stats.t[:nt, c, :], in_=X.t[:nt, c * 512:(c + 1) * 512]), reads=[X], writes=[stats])
            S.op("dve", lambda e: e.bn_aggr(out=mv.t[:nt, :], in_=stats.t[:nt].rearrange("p a b -> p (a b)")), reads=[stats], writes=[mv])
            S.op("act", lambda e: e.activation(out=rstd.t[:nt, :], in_=mv.t[:nt, 1:2], func=AF.Sqrt, bias=1e-5, scale=1.0), reads=[mv], writes=[rstd])
            S.op("dve", lambda e: e.reciprocal(out=rstd.t[:nt, :], in_=rstd.t[:nt, :]), reads=[rstd], writes=[rstd])
            S.op("dve", lambda e: e.tensor_scalar(out=X.t[:nt, :], in0=X.t[:nt, :], scalar1=mv.t[:nt, 0:1], scalar2=rstd.t[:nt, 0:1],
                                                  op0=ALU.subtract, op1=ALU.mult), reads=[X, mv, rstd], writes=[X])
            S.op(eng2, lambda e: e.tensor_tensor(out=X.t[:nt, :], in0=X.t[:nt, :], in1=Gt.t[:nt, :], op=ALU.mult), reads=[X, Gt], writes=[X])
            S.op(eng2, lambda e: e.tensor_tensor(out=X.t[:nt, :], in0=X.t[:nt, :], in1=Bt.t[:nt, :], op=ALU.add), reads=[X, Bt], writes=[X])

        def make_T(X, nt, HT, lo=0, hi=8):
            for half in range(2):
                p = nps(lo, hi)
                pv = v4(p)
                for c in range(4):
                    k = half * 4 + c
                    S.op("pe", lambda e: e.transpose(out=pv[:, c, :nt], in_=X.t[:nt, k * 128:(k + 1) * 128], identity=ident.t[:nt, :nt]),
                         reads=[X, ident], writes=[p])
                S.op("act", lambda e: e.copy(out=HT.t[:, half * 4:(half + 1) * 4, :nt], in_=pv[:, :, :nt]), reads=[p], writes=[HT])

        with ExitStack() as st:
            g0 = mk(st, "lnin_g", [128, D]); b0 = mk(st, "lnin_b", [128, D])
            S.dma("sp", lambda e: e.dma_start(out=g0.t[:], in_=ln_in_g.partition_broadcast(128)), g0, writes=[g0])
            S.dma("sp", lambda e: e.dma_start(out=b0.t[:], in_=ln_in_b.partition_broadcast(128)), b0, writes=[b0])
            XI = [mk(st, "p0x%d" % i, [128, D]) for i in range(3)]
            STG = [mk(st, "stg%d" % i, [128, 8, D], BF16) for i in range(3)]
            tabv = TAB.rearrange("(l p r) w -> l p r w", l=DEPTH, p=128)
            ci_ = 0
            for l in range(depth):
                for src, off in ((peer_u, 0), (peer_v, D)):
                    srcv = src[l].rearrange("(p r) d -> p r d", p=128)
                    for c in range(16):
                        sg = STG[ci_ % 3]
                        ci_ += 1
                        S.dma("pool", lambda e: e.dma_start(out=sg.t[:], in_=srcv[:, 8 * c:8 * c + 8, :]), sg, writes=[sg])
                        S.dma("act", lambda e: e.dma_start(out=tabv[l][:, 8 * c:8 * c + 8, off:off + D], in_=sg.t[:]), sg, reads=[sg])
            lnst = (mk(st, "p0stats", [128, 2, 6]), mk(st, "p0mv", [128, 2]), mk(st, "p0rstd", [128, 1]))
            for ti, (row0, nt) in enumerate(tiles):
                X = XI[ti % 3]
                if ti == 0:
                    S.op("dve", lambda e: e.memset(X.t[:], 0.0), writes=[X])
                    S.dma("sp", lambda e: e.dma_start(out=X.t[112:128, :], in_=meta[:, :]), X, writes=[X])
                else:
                    S.dma("sp", lambda e: e.dma_start(out=X.t[:nt, :], in_=x[(ti - 1) * 128:ti * 128, :]), X, writes=[X])
                layer_norm(lnst, X, nt, g0, b0)
                if ti == 0:
                    S.op("dve", lambda e: e.tensor_scalar(out=X.t[:], in0=X.t[:], scalar1=padmask.t[:, 0:1], scalar2=None, op0=ALU.mult), reads=[X, padmask], writes=[X])
                S.dma("sp", lambda e: e.dma_start(out=hA[row0:row0 + nt, :], in_=X.t[:nt, :]), X, reads=[X])
                if dbg:
                    S.dma("sp", lambda e: e.dma_start(out=dbg_out["d_h0"][row0:row0 + nt, :], in_=X.t[:nt, :]), X, reads=[X])
            S.barrier()

        for l in range(depth):
            with ExitStack() as st:
                if skip_p1:
                    break
                WIN = mk(st, "WIN", [128, 8, PC], BF16)
                WOUT = mk(st, "WOUT", [128, 8, D], BF16)
                for k in range(8):
                    S.dma("pool", lambda e: e.dma_start(out=WIN.t[:, k, :], in_=w_in[l, k * 128:(k + 1) * 128, :]), WIN, writes=[WIN])
                    S.dma("pool", lambda e: e.dma_start(out=WOUT.t[:, k, :], in_=w_out[l, k * 128:(k + 1) * 128, :]), WOUT, writes=[WOUT])
                G1 = mk(st, "ln1g", [128, D]); B1 = mk(st, "ln1b", [128, D])
                S.dma("sp", lambda e: e.dma_start(out=G1.t[:], in_=ln1_g[l].partition_broadcast(128)), G1, writes=[G1])
                S.dma("sp", lambda e: e.dma_start(out=B1.t[:], in_=ln1_b[l].partition_broadcast(128)), B1, writes=[B1])
                CW = mk(st, "CW", [128, 4, 3]); GW = mk(st, "GW", [128, 12, 4])
                for j in range(3):
                    S.dma("sp", lambda e: e.dma_start(out=CW.t[:, :, j], in_=conv_w[l, j].rearrange("(b p) -> p b", p=128), allow_slow_non_contiguous=True), CW, writes=[CW])
                for j in range(4):
                    S.dma("sp", lambda e: e.dma_start(out=GW.t[:, :, j], in_=gdn_conv_w[l, j].rearrange("(b p) -> p b", p=128), allow_slow_non_contiguous=True), GW, writes=[GW])
                GNW = mk(st, "GNW", [128, 1])
                S.dma("sp", lambda e: e.dma_start(out=GNW.t[:], in_=gdn_norm_w[l].rearrange("(p o) -> p o", o=1)), GNW, writes=[GNW])
                NEGA = mk(st, "NEGA", [128, 4]); DTB = mk(st, "DTB", [128, 4])
                S.dma("sp", lambda e: e.dma_start(out=NEGA.t[:], in_=a_log[l].partition_broadcast(128)), NEGA, writes=[NEGA])
                S.dma("sp", lambda e: e.dma_start(out=DTB.t[:], in_=dt_bias[l].partition_broadcast(128)), DTB, writes=[DTB])
                S.op("act", lambda e: e.activation(out=NEGA.t[:], in_=NEGA.t[:], func=AF.Exp), reads=[NEGA], writes=[NEGA])
                S.op("dve", lambda e: e.tensor_scalar(out=NEGA.t[:], in0=NEGA.t[:], scalar1=-1.0, scalar2=None, op0=ALU.mult), reads=[NEGA], writes=[NEGA])

                XIN = [mk(st, "xin%d" % i, [128, D]) for i in range(2)]
                R = mk(st, "R", [128, D])
                HT = mk(st, "HT", [128, 8, 128], BF16)
                PJ = mk(st, "PJ", [128, 16, 128])
                QKVP = mk(st, "QKVP", [128, 12, 131])
                UC = mk(st, "UC", [128, 4, 130])
                CV = mk(st, "CV", [128, 4, 128])
                CA = mk(st, "CA", [128, 12, 128])
                CT = mk(st, "CT", [128, 12, 128])
                SQ = mk(st, "SQ", [128, 8, 128])
                QKN = mk(st, "QKN", [128, 8, 128])
                KV = mk(st, "KV", [128, 8, 128])
                KVS = mk(st, "KVS", [128, 12, 128])
                DD = mk(st, "DD", [128, 8, 128])
                GH = mk(st, "GH", [128, 4, 128])
                EGB = mk(st, "EGB", [128, 4, 128])
                Pm = [mk(st, "Pm%d" % i, [128, 4, 128]) for i in range(2)]
                PTm = [mk(st, "PTm%d" % i, [128, 4, 128]) for i in range(2)]
                XT = mk(st, "XT", [128, 4, 128])
                NWT = mk(st, "NWT", [128, 4, 128])
                VNEW = mk(st, "VNEW", [128, 4, 128])
                ATT = mk(st, "ATT", [128, 4, 128])
                QD = mk(st, "QD", [128, 4, 128])
                Sst = mk(st, "Sst", [128, 4, 128])
                ZS = mk(st, "ZS", [128, 4, 128])
                T1 = mk(st, "T1", [128, 4, 128])
                YM = mk(st, "YM", [128, 8, 128], BF16)
                SM = mk(st, "SM", [128, 12])
                SM2 = mk(st, "SM2", [128, 16])
                SM3 = mk(st, "SM3", [128, 4])
                lnst = (mk(st, "p1stats", [128, 2, 6]), mk(st, "p1mv", [128, 2]), mk(st, "p1rstd", [128, 1]))
                for h in range(4):
                    S.op("dve", lambda e: e.tensor_scalar(out=fr(Sst.t[:, h, :]), in0=iof.t[:, :], scalar1=0.0, scalar2=None, op0=ALU.mult), reads=[iof], writes=[Sst])
                S.op("dve", lambda e: e.memset(QKVP.t[:], 0.0), writes=[QKVP])
                S.op("dve", lambda e: e.memset(UC.t[:], 0.0), writes=[UC])

                PJ2 = [PJ, mk(st, "PJb", [128, 16, 128])]
                QKVP2 = [QKVP, mk(st, "QKVPb", [128, 12, 131])]
                HT2 = [HT, mk(st, "HTb", [128, 8, 128], BF16)]
                PAB = [PS[7], PS[7]]
                S.op("dve", lambda e: e.memset(QKVP2[1].t[:], 0.0), writes=[QKVP2[1]])

                def front1(ti):
                    row0, nt = tiles[ti]
                    X = XIN[ti % 2]
                    HTc = HT2[ti % 2]; PJc = PJ2[ti % 2]; QKVPc = QKVP2[ti % 2]; pab = PAB[ti % 2]
                    S.dma("sp", lambda e: e.dma_start(out=X.t[:nt, :], in_=hA[row0:row0 + nt, :]), X, writes=[X])
                    yield
                    make_T(X, nt, HTc, lo=5, hi=7)
                    yield
                    for g in range(7):
                        p = nps(5, 7); pv = v4(p)
                        for c in range(4):
                            blk = g * 4 + c
                            for k in range(8):
                                S.op("pe", lambda e: e.matmul(pv[:, c, :nt], lhsT=WIN.t[:, k, blk * 128:(blk + 1) * 128], rhs=HTc.t[:, k, :nt],
                                                              start=(k == 0), stop=(k == 7)), reads=[WIN, HTc], writes=[p])
                            yield
                        if g < 3:
                            dst, dT = PJc.t[:, 4 * g:4 * g + 4, :nt], PJc
                        elif g < 6:
                            dst, dT = QKVPc.t[:, 4 * (g - 3):4 * (g - 3) + 4, 3:3 + nt], QKVPc
                        else:
                            dst, dT = PJc.t[:, 12:16, :nt], PJc
                        if g % 2 == 0:
                            S.op("act", lambda e: e.copy(out=dst, in_=pv[:, :, :nt]), reads=[p], writes=[dT])
                        else:
                            S.op("dve", lambda e: e.tensor_copy(out=dst, in_=pv[:, :, :nt]), reads=[p], writes=[dT])
                    for k in range(8):
                        S.op("pe", lambda e: e.matmul(pab.t[:nt, 0:8], lhsT=HTc.t[:, k, :nt], rhs=WIN.t[:, k, 3584:3592], start=(k == 0), stop=(k == 7)),
                             reads=[WIN, HTc], writes=[pab])
                    if ti > 0:
                        QKVPp = QKVP2[(ti - 1) % 2]
                        S.op("act", lambda e: e.copy(out=QKVPc.t[:, :, 0:3], in_=QKVPp.t[:, :, nt:nt + 3]), reads=[QKVPp], writes=[QKVPc])
                    yield

                f0_ = front1(0)
                for _ in f0_:
                    pass
                for ti, (row0, nt) in enumerate(tiles):
                    X = XIN[ti % 2]
                    PJ = PJ2[ti % 2]; QKVP = QKVP2[ti % 2]; pab = PAB[ti % 2]
                    fnext = front1(ti + 1) if ti + 1 < len(tiles) else None

                    def pump(n=2):
                        if fnext is not None:
                            for _ in range(n):
                                next(fnext, None)
                    S.op("dve", lambda e: e.tensor_tensor(out=UC.t[:, :, 2:2 + nt], in0=PJ.t[:, 4:8, :nt], in1=PJ.t[:, 8:12, :nt], op=ALU.mult), reads=[PJ], writes=[UC])
                    for j in range(3):
                        if j == 0:
                            S.op("dve", lambda e: e.tensor_tensor(out=CV.t[:, :, :nt], in0=UC.t[:, :, 0:nt], in1=bc(CW.t[:, :, 0:1], [128, 4, nt]), op=ALU.mult),
                                 reads=[UC, CW], writes=[CV])
                        else:
                            S.op("dve", lambda e: e.tensor_tensor(out=CT.t[:, 0:4, :nt], in0=UC.t[:, :, j:j + nt], in1=bc(CW.t[:, :, j:j + 1], [128, 4, nt]), op=ALU.mult),
                                 reads=[UC, CW], writes=[CT])
                            S.op("dve", lambda e: e.tensor_tensor(out=CV.t[:, :, :nt], in0=CV.t[:, :, :nt], in1=CT.t[:, 0:4, :nt], op=ALU.add), reads=[CV, CT], writes=[CV])
                    S.op("dve", lambda e: e.tensor_tensor(out=YM.t[:, 0:4, :nt], in0=PJ.t[:, 0:4, :nt], in1=CV.t[:, :, :nt], op=ALU.mult), reads=[PJ, CV], writes=[YM])
                    S.op("dve", lambda e: e.tensor_copy(out=UC.t[:, :, 0:2], in_=UC.t[:, :, nt:nt + 2]), reads=[UC], writes=[UC])
                    if cut <= 3:
                        continue
                    pump()
                    for j in range(4):
                        if j == 0:
                            S.op("dve", lambda e: e.tensor_tensor(out=CA.t[:, :, :nt], in0=QKVP.t[:, :, 0:nt], in1=bc(GW.t[:, :, 0:1], [128, 12, nt]), op=ALU.mult),
                                 reads=[QKVP, GW], writes=[CA])
                        else:
                            S.op("dve", lambda e: e.tensor_tensor(out=CT.t[:, :, :nt], in0=QKVP.t[:, :, j:j + nt], in1=bc(GW.t[:, :, j:j + 1], [128, 12, nt]), op=ALU.mult),
                                 reads=[QKVP, GW], writes=[CT])
                            S.op("dve", lambda e: e.tensor_tensor(out=CA.t[:, :, :nt], in0=CA.t[:, :, :nt], in1=CT.t[:, :, :nt], op=ALU.add), reads=[CA, CT], writes=[CA])
                    S.op("act", lambda e: e.activation(out=CA.t[:, :, :nt], in_=CA.t[:, :, :nt], func=AF.Silu), reads=[CA], writes=[CA])
                    if cut <= 4:
                        continue
                    pump()
                    S.op("act", lambda e: e.activation(out=fr(SQ.t[:, 0:8, :nt]), in_=CA.t[:, 0:8, :nt], func=AF.Square), reads=[CA], writes=[SQ])
                    for half in range(2):
                        p = nps(0, 5); pv = v4(p)
                        S.op("pe", lambda e: e.matmul(pv[:, :, :nt], lhsT=fr(onesr.t[:, :]), rhs=fr(SQ.t[:, 4 * half:4 * half + 4, :nt]), start=True, stop=True),
                             reads=[onesr, SQ], writes=[p])
                        S.op("act", lambda e: e.activation(out=CT.t[:, 4 * half:4 * half + 4, :nt], in_=pv[:, :, :nt], func=AF.Ln, bias=1e-6, scale=1.0),
                             reads=[p], writes=[CT])
                    S.op("act", lambda e: e.activation(out=CT.t[:, 0:8, :nt], in_=CT.t[:, 0:8, :nt], func=AF.Exp, scale=-0.5), reads=[CT], writes=[CT])
                    S.op("dve", lambda e: e.scalar_tensor_tensor(out=fr(QKN.t[:, 0:4, :nt]), in0=CA.t[:, 0:4, :nt], scalar=128.0 ** -0.5, in1=CT.t[:, 0:4, :nt],
                                                                 op0=ALU.mult, op1=ALU.mult), reads=[CA, CT], writes=[QKN])
                    S.op("dve", lambda e: e.tensor_tensor(out=fr(QKN.t[:, 4:8, :nt]), in0=CA.t[:, 4:8, :nt], in1=CT.t[:, 4:8, :nt], op=ALU.mult), reads=[CA, CT], writes=[QKN])
                    if cut <= 5:
                        continue
                    pump()
                    for which in range(2):
                        p = nps(0, 5); pv = v4(p)
                        for h in range(4):
                            src = QKN.t[:, 4 + h, :nt] if which == 0 else CA.t[:, 8 + h, :nt]
                            S.op("pe", lambda e: e.transpose(out=pv[:nt, h, :], in_=src, identity=ident.t[:, :]), reads=[QKN, CA, ident], writes=[p])
                        S.op("act", lambda e: e.copy(out=KV.t[:nt, 4 * which:4 * which + 4, :], in_=pv[:nt, :, :]), reads=[p], writes=[KV])
                    if cut <= 6:
                        continue
                    pump()
                    S.op("dve", lambda e: e.tensor_tensor(out=SM.t[:nt, 0:4], in0=pab.t[:nt, 0:4], in1=DTB.t[:nt, :], op=ALU.add), reads=[pab, DTB], writes=[SM])
                    S.op("act", lambda e: e.activation(out=SM.t[:nt, 0:4], in_=SM.t[:nt, 0:4], func=AF.Exp), reads=[SM], writes=[SM])
                    S.op("act", lambda e: e.activation(out=SM.t[:nt, 0:4], in_=SM.t[:nt, 0:4], func=AF.Ln, bias=1.0, scale=1.0), reads=[SM], writes=[SM])
                    S.op("dve", lambda e: e.tensor_tensor(out=SM.t[:nt, 4:8], in0=SM.t[:nt, 0:4], in1=NEGA.t[:nt, :], op=ALU.mult), reads=[SM, NEGA], writes=[SM])
                    S.op("act", lambda e: e.activation(out=SM.t[:nt, 8:12], in_=pab.t[:nt, 4:8], func=AF.Sigmoid), reads=[pab], writes=[SM])
                    pg = nps(0, 5)
                    S.op("pe", lambda e: e.matmul(pg.t[:nt, 0:4], lhsT=LT.t[:nt, :nt], rhs=SM.t[:nt, 4:8], start=True, stop=True), reads=[LT, SM], writes=[pg])
                    S.op("pe", lambda e: e.matmul(pg.t[:nt, 4:8], lhsT=UTS.t[:nt, :nt], rhs=SM.t[:nt, 4:8], start=True, stop=True), reads=[UTS, SM], writes=[pg])
                    S.op("pe", lambda e: e.matmul(pg.t[:, 8:12], lhsT=ones.t[:nt, :], rhs=SM.t[:nt, 4:8], start=True, stop=True), reads=[ones, SM], writes=[pg])
                    S.op("dve", lambda e: e.tensor_copy(out=SM2.t[:nt, 0:4], in_=pg.t[:nt, 0:4]), reads=[pg], writes=[SM2])
                    S.op("act", lambda e: e.activation(out=SM2.t[:nt, 4:12], in_=pg.t[:nt, 0:8], func=AF.Exp), reads=[pg], writes=[SM2])
                    S.op("act", lambda e: e.activation(out=SM3.t[:, 0:4], in_=pg.t[:, 8:12], func=AF.Exp), reads=[pg], writes=[SM3])
                    S.op("dve", lambda e: e.tensor_tensor(out=SM2.t[:nt, 12:16], in0=SM.t[:nt, 8:12], in1=SM2.t[:nt, 4:8], op=ALU.mult), reads=[SM, SM2], writes=[SM2])
                    if cut <= 7:
                        continue
                    S.op("dve", lambda e: e.tensor_tensor(out=GH.t[:nt, :, :nt], in0=bc(LT.t[:nt, :nt].unsqueeze(1), [nt, 4, nt]),
                                                          in1=bc(SM.t[:nt, 4:8].unsqueeze(2), [nt, 4, nt]), op=ALU.mult), reads=[LT, SM], writes=[GH])
                    pc_ = nps(0, 5); pcv = v4(pc_)
                    S.op("pe", lambda e: e.matmul(pcv[:, :, :nt], lhsT=ones.t[:nt, :], rhs=GH.t[:nt, :, :nt], start=True, stop=True), reads=[ones, GH], writes=[pc_])
                    S.op("act", lambda e: e.activation(out=EGB.t[:, :, :nt], in_=pcv[:, :, :nt], func=AF.Exp), reads=[pc_], writes=[EGB])
                    for h in range(4):
                        S.op("dve", lambda e: e.scalar_tensor_tensor(out=DD.t[:nt, h, :nt], in0=pcv[:nt, h, :nt], scalar=SM2.t[:nt, h:h + 1], in1=MPOS.t[:nt, :nt],
                                                                     op0=ALU.subtract, op1=ALU.add), reads=[pc_, SM2, MPOS], writes=[DD])
                        S.op("dve", lambda e: e.scalar_tensor_tensor(out=DD.t[:nt, 4 + h, :nt], in0=pcv[:nt, h, :nt], scalar=SM2.t[:nt, h:h + 1], in1=MNEG.t[:nt, :nt],
                                                                     op0=ALU.subtract, op1=ALU.add), reads=[pc_, SM2, MNEG], writes=[DD])
                    S.op("act", lambda e: e.activation(out=DD.t[:nt, 0:4, :nt], in_=DD.t[:nt, 0:4, :nt], func=AF.Exp, scale=-1.0), reads=[DD], writes=[DD])
                    S.op("act", lambda e: e.activation(out=DD.t[:nt, 4:8, :nt], in_=DD.t[:nt, 4:8, :nt], func=AF.Exp), reads=[DD], writes=[DD])
                    if cut <= 8:
                        continue
                    pump()
                    pk = nps(0, 5); pkv = v4(pk)
                    for h in range(4):
                        S.op("pe", lambda e: e.matmul(pkv[:nt, h, :nt], lhsT=fr(QKN.t[:, 4 + h, :nt]), rhs=fr(QKN.t[:, 4 + h, :nt]), start=True, stop=True), reads=[QKN], writes=[pk])
                    for h in range(4):
                        S.op("dve", lambda e: e.scalar_tensor_tensor(out=fr(Pm[0].t[:nt, h, :nt]), in0=pkv[:nt, h, :nt], scalar=SM.t[:nt, 8 + h:9 + h], in1=DD.t[:nt, h, :nt],
                                                                     op0=ALU.mult, op1=ALU.mult), reads=[pk, SM, DD], writes=[Pm[0]])
                    pt = nps(0, 5); ptv = v4(pt)
                    for h in range(4):
                        S.op("pe", lambda e: e.transpose(out=ptv[:nt, h, :nt], in_=Pm[0].t[:nt, h, :nt], identity=ident.t[:nt, :nt]), reads=[Pm[0], ident], writes=[pt])
                    S.op("act", lambda e: e.copy(out=fr(PTm[0].t[:nt, :, :nt]), in_=ptv[:nt, :, :nt]), reads=[pt], writes=[PTm[0]])
                    S.op("dve", lambda e: e.tensor_tensor(out=fr(XT.t[:nt, :, :nt]), in0=bc(ident.t[:nt, :nt].unsqueeze(1), [nt, 4, nt]), in1=ptv[:nt, :, :nt], op=ALU.subtract),
                         reads=[ident, pt], writes=[XT])
                    smax = 6 if nt == 128 else 3
                    for s in range(1, smax + 1):
                        cur, nxt = (s - 1) % 2, s % 2
                        pump(2)
                        pp = nps(0, 5); ppv = v4(pp)
                        for h in range(4):
                            S.op("pe", lambda e: e.matmul(ppv[:nt, h, :nt], lhsT=fr(PTm[cur].t[:nt, h, :nt]), rhs=fr(Pm[cur].t[:nt, h, :nt]), start=True, stop=True),
                                 reads=[PTm[cur], Pm[cur]], writes=[pp])
                        S.op("act", lambda e: e.copy(out=fr(Pm[nxt].t[:nt, :, :nt]), in_=ppv[:nt, :, :nt]), reads=[pp], writes=[Pm[nxt]])
                        if s < smax:
                            pq = nps(0, 5); pqv = v4(pq)
                            for h in range(4):
                                S.op("pe", lambda e: e.matmul(pqv[:nt, h, :nt], lhsT=fr(Pm[cur].t[:nt, h, :nt]), rhs=fr(PTm[cur].t[:nt, h, :nt]), start=True, stop=True),
                                     reads=[PTm[cur], Pm[cur]], writes=[pq])
                            S.op("dve", lambda e: e.tensor_copy(out=fr(PTm[nxt].t[:nt, :, :nt]), in_=pqv[:nt, :, :nt]), reads=[pq], writes=[PTm[nxt]])
                        px = nps(0, 5); pxv = v4(px)
                        for h in range(4):
                            S.op("pe", lambda e: e.matmul(pxv[:nt, h, :nt], lhsT=fr(Pm[nxt].t[:nt, h, :nt]), rhs=fr(XT.t[:nt, h, :nt]), start=True, stop=True),
                                 reads=[Pm[nxt], XT], writes=[px])
                        S.op("dve", lambda e: e.tensor_tensor(out=fr(XT.t[:nt, :, :nt]), in0=pxv[:nt, :, :nt], in1=XT.t[:nt, :, :nt], op=ALU.add), reads=[px, XT], writes=[XT])
                    if cut <= 9:
                        continue
                    pump()
                    S.op("dve", lambda e: e.tensor_tensor(out=fr(KVS.t[:nt, 0:4, :]), in0=KV.t[:nt, 4:8, :], in1=bc(SM.t[:nt, 8:12].unsqueeze(2), [nt, 4, 128]), op=ALU.mult),
                         reads=[KV, SM], writes=[KVS])
                    S.op("dve", lambda e: e.tensor_tensor(out=fr(KVS.t[:nt, 4:8, :]), in0=KV.t[:nt, 0:4, :], in1=bc(SM2.t[:nt, 12:16].unsqueeze(2), [nt, 4, 128]), op=ALU.mult),
                         reads=[KV, SM2], writes=[KVS])
                    S.op("dve", lambda e: e.tensor_tensor(out=fr(KVS.t[:nt, 8:12, :]), in0=KV.t[:nt, 0:4, :], in1=bc(SM2.t[:nt, 8:12].unsqueeze(2), [nt, 4, 128]), op=ALU.mult),
                         reads=[KV, SM2], writes=[KVS])
                    if cut <= 10:
                        continue
                    pump()
                    pw = nps(0, 5); pwv = v4(pw)
                    for h in range(4):
                        S.op("pe", lambda e: e.matmul(pwv[:, h, :nt], lhsT=fr(KVS.t[:nt, 4 + h, :]), rhs=fr(XT.t[:nt, h, :nt]), start=True, stop=True), reads=[KVS, XT], writes=[pw])
                    S.op("act", lambda e: e.activation(out=fr(NWT.t[:, :, :nt]), in_=pwv[:, :, :nt], func=AF.Copy, scale=-1.0), reads=[pw], writes=[NWT])
                    pump()
                    pvn = nps(0, 5); pvnv = v4(pvn)
                    for h in range(4):
                        S.op("pe", lambda e: e.matmul(pvnv[:nt, h, :], lhsT=fr(XT.t[:nt, h, :nt]), rhs=fr(KVS.t[:nt, h, :]), start=True, stop=False), reads=[XT, KVS], writes=[pvn])
                        S.op("pe", lambda e: e.matmul(pvnv[:nt, h, :], lhsT=fr(NWT.t[:, h, :nt]), rhs=fr(Sst.t[:, h, :]), start=False, stop=True), reads=[NWT, Sst], writes=[pvn])
                    S.op("act", lambda e: e.copy(out=fr(VNEW.t[:nt, :, :]), in_=pvnv[:nt, :, :]), reads=[pvn], writes=[VNEW])
                    if cut <= 11:
                        continue
                    pump()
                    pa = nps(0, 5); pav = v4(pa)
                    for h in range(4):
                        S.op("pe", lambda e: e.matmul(pav[:nt, h, :nt], lhsT=fr(QKN.t[:, 4 + h, :nt]), rhs=fr(QKN.t[:, h, :nt]), start=True, stop=True), reads=[QKN], writes=[pa])
                    S.op("dve", lambda e: e.tensor_tensor(out=fr(ATT.t[:nt, :, :nt]), in0=pav[:nt, :, :nt], in1=DD.t[:nt, 4:8, :nt], op=ALU.mult), reads=[pa, DD], writes=[ATT])
                    S.op("dve", lambda e: e.tensor_tensor(out=fr(QD.t[:, :, :nt]), in0=QKN.t[:, 0:4, :nt], in1=EGB.t[:, :, :nt], op=ALU.mult), reads=[QKN, EGB], writes=[QD])
                    pump()
                    po = nps(0, 5); pov = v4(po)
                    for h in range(4):
                        S.op("pe", lambda e: e.matmul(pov[:, h, :nt], lhsT=fr(Sst.t[:, h, :]), rhs=fr(QD.t[:, h, :nt]), start=True, stop=False), reads=[Sst, QD], writes=[po])
                        S.op("pe", lambda e: e.matmul(pov[:, h, :nt], lhsT=fr(VNEW.t[:nt, h, :]), rhs=fr(ATT.t[:nt, h, :nt]), start=False, stop=True), reads=[VNEW, ATT], writes=[po])
                    if cut <= 12:
                        continue
                    pump()
                    pss = nps(0, 5); pssv = v4(pss)
                    for h in range(4):
                        S.op("pe", lambda e: e.matmul(pssv[:, h, :], lhsT=fr(KVS.t[:nt, 8 + h, :]), rhs=fr(VNEW.t[:nt, h, :]), start=True, stop=True), reads=[KVS, VNEW], writes=[pss])
                    S.op("dve", lambda e: e.tensor_tensor(out=fr(Sst.t[:, :, :]), in0=Sst.t[:, :, :], in1=bc(SM3.t[:, 0:4].unsqueeze(2), [128, 4, 128]), op=ALU.mult),
                         reads=[Sst, SM3], writes=[Sst])
                    S.op("dve", lambda e: e.tensor_tensor(out=fr(Sst.t[:, :, :]), in0=Sst.t[:, :, :], in1=pssv[:, :, :], op=ALU.add), reads=[Sst, pss], writes=[Sst])
                    if cut <= 13:
                        continue
                    pump()
                    S.op("act", lambda e: e.activation(out=fr(SQ.t[:, 0:4, :nt]), in_=pov[:, :, :nt], func=AF.Square), reads=[po], writes=[SQ])
                    pm = nps(0, 5); pmv = v4(pm)
                    S.op("pe", lambda e: e.matmul(pmv[:, :, :nt], lhsT=fr(onesr.t[:, :]), rhs=fr(SQ.t[:, 0:4, :nt]), start=True, stop=True), reads=[onesr, SQ], writes=[pm])
                    S.op("act", lambda e: e.activation(out=CT.t[:, 4:8, :nt], in_=pmv[:, :, :nt], func=AF.Ln, bias=1e-6, scale=1.0 / 128.0), reads=[pm], writes=[CT])
                    S.op("act", lambda e: e.activation(out=CT.t[:, 4:8, :nt], in_=CT.t[:, 4:8, :nt], func=AF.Exp, scale=-0.5), reads=[CT], writes=[CT])
                    S.op("act", lambda e: e.activation(out=ZS.t[:, :, :nt], in_=PJ.t[:, 12:16, :nt], func=AF.Silu), reads=[PJ], writes=[ZS])
                    S.op("dve", lambda e: e.scalar_tensor_tensor(out=T1.t[:, :, :nt], in0=pov[:, :, :nt], scalar=GNW.t[:, 0:1], in1=CT.t[:, 4:8, :nt],
                                                                 op0=ALU.mult, op1=ALU.mult), reads=[po, GNW, CT], writes=[T1])
                    S.op("dve", lambda e: e.tensor_tensor(out=YM.t[:, 4:8, :nt], in0=T1.t[:, :, :nt], in1=ZS.t[:, :, :nt], op=ALU.mult), reads=[T1, ZS], writes=[YM])
                    if cut <= 14:
                        continue
                    pump()
                    py = [nps(0, 5), nps(0, 5)]
                    for half in range(2):
                        for k in range(8):
                            S.op("pe", lambda e: e.matmul(py[half].t[:nt, :], lhsT=YM.t[:, k, :nt], rhs=WOUT.t[:, k, half * 512:(half + 1) * 512],
                                                          start=(k == 0), stop=(k == 7)), reads=[YM, WOUT], writes=[py[half]])
                    for half in range(2):
                        S.op("dve", lambda e: e.scalar_tensor_tensor(out=R.t[:nt, half * 512:(half + 1) * 512], in0=X.t[:nt, half * 512:(half + 1) * 512], scalar=ALPHA,
                                                                     in1=py[half].t[:nt, :], op0=ALU.mult, op1=ALU.add), reads=[X, py[half]], writes=[R])
                    layer_norm(lnst, R, nt, G1, B1)
                    S.dma("sp", lambda e: e.dma_start(out=hB[row0:row0 + nt, :], in_=R.t[:nt, :]), R, reads=[R])
                    if dbg and l == 0:
                        S.dma("sp", lambda e: e.dma_start(out=dbg_out["d_h2"][row0:row0 + nt, :], in_=R.t[:nt, :]), R, reads=[R])
                    if fnext is not None:
                        for _ in fnext:
                            pass
                S.barrier()

            with ExitStack() as st:
                if skip_p2:
                    break
                WQ = mk(st, "WQ", [128, 8, 2048], BF16)
                for k in range(8):
                    S.dma("pool", lambda e: e.dma_start(out=WQ.t[:, k, :], in_=peer_w_q[l, k * 128:(k + 1) * 128, :]), WQ, writes=[WQ])
                G2 = mk(st, "ln2g", [128, D]); B2 = mk(st, "ln2b", [128, D])
                S.dma("sp", lambda e: e.dma_start(out=G2.t[:], in_=ln2_g[l].partition_broadcast(128)), G2, writes=[G2])
                S.dma("sp", lambda e: e.dma_start(out=B2.t[:], in_=ln2_b[l].partition_broadcast(128)), B2, writes=[B2])
                KT = [mk(st, "KT%d" % i, [128, 128]) for i in range(2)]
                ktmp = mk(st, "ktmp", [128, 128])
                for i, kd in enumerate((peer_k1, peer_k2)):
                    S.dma("sp", lambda e: e.dma_start(out=ktmp.t[:], in_=kd[l]), ktmp, writes=[ktmp])
                    p = nps(0, 2)
                    S.op("pe", lambda e: e.transpose(out=p.t[:, 0:128], in_=ktmp.t[:, :], identity=ident.t[:, :]), reads=[ktmp, ident], writes=[p])
                    S.op("act", lambda e: e.copy(out=KT[i].t[:], in_=p.t[:, 0:128]), reads=[p], writes=[KT[i]])
                XIN = [mk(st, "x2in%d" % i, [128, D]) for i in range(2)]
                R2 = mk(st, "R2", [128, D])
                H2T = mk(st, "H2T", [128, 8, 128], BF16)
                H2B = mk(st, "H2B", [128, D], BF16)
                QT = mk(st, "QT", [128, 16, 128])
                SS = mk(st, "SS", [128, 16, 128])
                SR = mk(st, "SR", [128, 256])
                TV = mk(st, "TV", [128, 16, 16])
                TI = mk(st, "TI", [128, 16, 16], U32)
                TIF = mk(st, "TIF", [128, 16, 16])
                CAND = mk(st, "CAND", [128, 8, 256])
                OH = mk(st, "OH", [128, 8, 256])
                TOPV = mk(st, "TOPV", [128, 8, 16])
                CI = mk(st, "CI", [128, 8, 16], U32)
                II = mk(st, "II", [128, 8, 16], U32); JJ = mk(st, "JJ", [128, 8, 16], U32)
                IIF = mk(st, "IIF", [128, 8, 16]); JJF = mk(st, "JJF", [128, 8, 16])
                E1 = mk(st, "E1", [128, 8, 16]); E2 = mk(st, "E2", [128, 8, 16])
                GT = mk(st, "GT", [128, 8, 16]); GS = mk(st, "GS", [128, 8])
                GATE2 = [mk(st, "GATE%d" % i, [128, 128]) for i in range(2)]; IDX = mk(st, "IDX", [128, 128])
                IDXU2 = [mk(st, "IDXU%d" % i, [128, 128], U32) for i in range(2)]
                DG = [mk(st, "DG%d" % i, [128, 128], BF16) for i in range(NZ)]
                AALL = [mk(st, "AALL%d" % i, [128, 128]) for i in range(2)]
                ACTV = [mk(st, "ACTV%d" % i, [128, 128]) for i in range(2)]
                CC = [mk(st, "CC%d" % i, [128, 128]) for i in range(2)]
                ZB = [mk(st, "ZB%d" % i, [128, 255], BF16) for i in range(NZ)]
                JUNK = mk(st, "JUNK", [128, D], BF16)
                UV = [mk(st, "UV%d" % i, [128, 2 * D], BF16) for i in range(NSLOT)]
                lnst = (mk(st, "p2stats", [128, 2, 6]), mk(st, "p2mv", [128, 2]), mk(st, "p2rstd", [128, 1]))
                for z in ZB:
                    S.op("dve", lambda e: e.memset(z.t[:], 0.0), writes=[z])
                PSY = [PS[2], PS[3]]
                tokc = 0
                H2B2 = [H2B, mk(st, "H2Bb", [128, D], BF16)]
                fe_banks = [PS[0], PS[1], PS[4], PS[5], PS[6], PS[7]]
                fectr = [0]

                def fps():
                    p_ = fe_banks[fectr[0] % len(fe_banks)]
                    fectr[0] += 1
                    return p_

                def front(ti):
                    row0, nt = tiles[ti]
                    H2Bc = H2B2[ti % 2]
                    X = XIN[ti % 2]
                    IDXUc = IDXU2[ti % 2]; GATE = GATE2[ti % 2]
                    S.dma("sp", lambda e: e.dma_start(out=X.t[:nt, :], in_=hB[row0:row0 + nt, :]), X, writes=[X])
                    yield
                    for half in range(2):
                        p = fps(); pv = v4(p)
                        for c in range(4):
                            k = half * 4 + c
                            S.op("pe", lambda e: e.transpose(out=pv[:, c, :nt], in_=X.t[:nt, k * 128:(k + 1) * 128], identity=ident.t[:nt, :nt]), reads=[X, ident], writes=[p])
                        S.op("act", lambda e: e.copy(out=H2T.t[:, half * 4:(half + 1) * 4, :nt], in_=pv[:, :, :nt]), reads=[p], writes=[H2T])
                        yield
                    S.op("act", lambda e: e.copy(out=H2Bc.t[:nt, :], in_=X.t[:nt, :]), reads=[X], writes=[H2Bc])
                    yield
                    for g in range(4):
                        p = fps(); pv = v4(p)
                        for c in range(4):
                            blk = 4 * g + c
                            for k in range(8):
                                S.op("pe", lambda e: e.matmul(pv[:, c, :nt], lhsT=WQ.t[:, k, blk * 128:(blk + 1) * 128], rhs=H2T.t[:, k, :nt], start=(k == 0), stop=(k == 7)),
                                     reads=[WQ, H2T], writes=[p])
                            yield
                        S.op("act", lambda e: e.copy(out=QT.t[:, 4 * g:4 * g + 4, :nt], in_=pv[:, :, :nt]), reads=[p], writes=[QT])
                        yield
                    for g in range(4):
                        p = fps(); pv = v4(p)
                        for c in range(4):
                            blk = 4 * g + c
                            S.op("pe", lambda e: e.matmul(pv[:nt, c, :], lhsT=QT.t[:, blk, :nt], rhs=KT[blk % 2].t[:, :], start=True, stop=True), reads=[QT, KT[blk % 2]], writes=[p])
                        S.op("act", lambda e: e.copy(out=SS.t[:nt, 4 * g:4 * g + 4, :], in_=pv[:nt, :, :]), reads=[p], writes=[SS])
                        yield
                    for blk in range(16):
                        S.op("dve", lambda e: e.max(out=TV.t[:nt, blk, 0:8], in_=SS.t[:nt, blk, :]), reads=[SS], writes=[TV])
                        S.op("dve", lambda e: e.max_index(out=TI.t[:nt, blk, 0:8], in_max=TV.t[:nt, blk, 0:8], in_values=SS.t[:nt, blk, :]), reads=[SS, TV], writes=[TI])
                        S.op("dve", lambda e: e.match_replace(out=SR.t[:nt, 0:128], in_to_replace=TV.t[:nt, blk, 0:8], in_values=SS.t[:nt, blk, :], imm_value=-1e30),
                             reads=[SS, TV], writes=[SR])
                        S.op("dve", lambda e: e.max(out=TV.t[:nt, blk, 8:16], in_=SR.t[:nt, 0:128]), reads=[SR], writes=[TV])
                        S.op("dve", lambda e: e.max_index(out=TI.t[:nt, blk, 8:16], in_max=TV.t[:nt, blk, 8:16], in_values=SR.t[:nt, 0:128]), reads=[SR, TV], writes=[TI])
                        yield
                    tvv = TV.t[:].rearrange("p (h two) k -> p h two k", two=2)
                    candv = CAND.t[:].rearrange("p h (i j) -> p h i j", i=16)
                    S.op("dve", lambda e: e.tensor_tensor(out=candv[:nt], in0=bc(tvv[:nt, :, 0, :].unsqueeze(3), [nt, 8, 16, 16]),
                                                          in1=bc(tvv[:nt, :, 1, :].unsqueeze(2), [nt, 8, 16, 16]), op=ALU.add), reads=[TV], writes=[CAND])
                    yield
                    for h in range(8):
                        S.op("dve", lambda e: e.max(out=TOPV.t[:nt, h, 0:8], in_=CAND.t[:nt, h, :]), reads=[CAND], writes=[TOPV])
                        S.op("dve", lambda e: e.max_index(out=CI.t[:nt, h, 0:8], in_max=TOPV.t[:nt, h, 0:8], in_values=CAND.t[:nt, h, :]), reads=[CAND, TOPV], writes=[CI])
                        S.op("dve", lambda e: e.match_replace(out=SR.t[:nt, :], in_to_replace=TOPV.t[:nt, h, 0:8], in_values=CAND.t[:nt, h, :], imm_value=-1e30),
                             reads=[CAND, TOPV], writes=[SR])
                        S.op("dve", lambda e: e.max(out=TOPV.t[:nt, h, 8:16], in_=SR.t[:nt, :]), reads=[SR], writes=[TOPV])
                        S.op("dve", lambda e: e.max_index(out=CI.t[:nt, h, 8:16], in_max=TOPV.t[:nt, h, 8:16], in_values=SR.t[:nt, :]), reads=[SR, TOPV], writes=[CI])
                        yield
                    S.op("dve", lambda e: e.tensor_tensor(out=GT.t[:nt], in0=TOPV.t[:nt], in1=bc(TOPV.t[:nt, :, 0:1], [nt, 8, 16]), op=ALU.subtract), reads=[TOPV], writes=[GT])
                    S.op("act", lambda e: e.activation(out=GT.t[:nt], in_=GT.t[:nt], func=AF.Exp), reads=[GT], writes=[GT])
                    S.op("dve", lambda e: e.tensor_reduce(out=GS.t[:nt, :], in_=GT.t[:nt], axis=AX.X, op=ALU.add), reads=[GT], writes=[GS])
                    S.op("dve", lambda e: e.reciprocal(out=GS.t[:nt, :], in_=GS.t[:nt, :]), reads=[GS], writes=[GS])
                    S.op("dve", lambda e: e.tensor_tensor(out=GATE.t[:nt, :].rearrange("p (h k) -> p h k", h=8), in0=GT.t[:nt], in1=bc(GS.t[:nt, :].unsqueeze(2), [nt, 8, 16]), op=ALU.mult),
                         reads=[GT, GS], writes=[GATE])
                    yield
                    S.op("dve", lambda e: e.tensor_single_scalar(out=II.t[:nt], in_=CI.t[:nt], scalar=4, op=ALU.logical_shift_right), reads=[CI], writes=[II])
                    S.op("dve", lambda e: e.tensor_single_scalar(out=JJ.t[:nt], in_=CI.t[:nt], scalar=15, op=ALU.bitwise_and), reads=[CI], writes=[JJ])
                    S.op("dve", lambda e: e.tensor_copy(out=IIF.t[:nt], in_=II.t[:nt]), reads=[II], writes=[IIF])
                    S.op("dve", lambda e: e.tensor_copy(out=JJF.t[:nt], in_=JJ.t[:nt]), reads=[JJ], writes=[JJF])
                    S.op("dve", lambda e: e.tensor_copy(out=TIF.t[:nt], in_=TI.t[:nt]), reads=[TI], writes=[TIF])
                    yield
                    tif = TIF.t[:].rearrange("p (h two) k -> p h two k", two=2)
                    ohv = OH.t[:].rearrange("p h (r i) -> p h r i", r=16)
                    for which, (SEL, EOUT) in enumerate(((IIF, E1), (JJF, E2))):
                        S.op("dve", lambda e: e.tensor_tensor(out=ohv[:nt], in0=bc(SEL.t[:nt].unsqueeze(3), [nt, 8, 16, 16]),
                                                              in1=bc(iota16.t[:nt, :].unsqueeze(1).unsqueeze(1), [nt, 8, 16, 16]), op=ALU.is_equal), reads=[SEL, iota16], writes=[OH])
                        S.op("dve", lambda e: e.tensor_tensor(out=ohv[:nt], in0=ohv[:nt], in1=bc(tif[:nt, :, which, :].unsqueeze(2), [nt, 8, 16, 16]), op=ALU.mult),
                             reads=[OH, TIF], writes=[OH])
                        S.op("dve", lambda e: e.tensor_reduce(out=EOUT.t[:nt], in_=ohv[:nt], axis=AX.X, op=ALU.add), reads=[OH], writes=[EOUT])
                        yield
                    S.op("dve", lambda e: e.scalar_tensor_tensor(out=IDX.t[:nt, :], in0=E1.t[:nt].rearrange("p h k -> p (h k)"), scalar=128.0,
                                                                 in1=E2.t[:nt].rearrange("p h k -> p (h k)"), op0=ALU.mult, op1=ALU.add), reads=[E1, E2], writes=[IDX])
                    S.op("dve", lambda e: e.tensor_copy(out=IDXUc.t[:nt, :], in_=IDX.t[:nt, :]), reads=[IDX], writes=[IDXUc])
                    yield

                fe0 = front(0)
                for _ in fe0:
                    pass
                for ti, (row0, nt) in enumerate(tiles):
                    X = XIN[ti % 2]
                    H2Bc = H2B2[ti % 2]
                    IDXUc = IDXU2[ti % 2]; GATE = GATE2[ti % 2]
                    fe_next = front(ti + 1) if ti + 1 < len(tiles) else None
                    slots = {}
                    LAG = 6
                    PRE = 8

                    def emit_gather(j):
                        nonlocal tokc
                        sl = tokc % NSLOT
                        tokc += 1
                        slots[j] = sl
                        S.dma("pool", lambda e: e.indirect_dma_start(out=UV[sl].t[:], out_offset=None, in_=TAB,
                                                                     in_offset=bass.IndirectOffsetOnAxis(ap=IDXUc.t[:, j:j + 1], axis=0), element_offset=l * NEXP * 2 * D),
                              UV[sl], reads=[IDXUc], writes=[UV[sl]])

                    def emit_dot(j):
                        g0 = (j // G_TOK) * G_TOK
                        gi = (j // G_TOK) % 2
                        sl = slots[j]
                        edge = (j == g0) or (j == g0 + G_TOK - 1)
                        S.op("dve", lambda e: e.scalar_tensor_tensor(out=JUNK.t[:, :], in0=UV[sl].t[:, 0:D], scalar=1.0, in1=H2Bc.t[:, :], op0=ALU.mult, op1=ALU.mult,
                                                                     accum_out=AALL[gi].t[:, j:j + 1]),
                             reads=[UV[sl], H2Bc], writes=([AALL[gi]] if edge else []))
                        if j == g0 + G_TOK - 1:
                            gs = slice(g0, g0 + G_TOK)
                            S.op("act", lambda e: e.activation(out=ACTV[gi].t[:, gs], in_=AALL[gi].t[:, gs], func=AF.Gelu), reads=[AALL[gi]], writes=[ACTV[gi]])
                            S.op("dve", lambda e: e.tensor_tensor(out=CC[gi].t[:, gs], in0=ACTV[gi].t[:, gs], in1=GATE.t[:, gs], op=ALU.mult), reads=[ACTV[gi], GATE], writes=[CC[gi]])

                    def emit_y(j):
                        gi = (j // G_TOK) % 2
                        sl = slots[j]
                        dg = DG[j % NZ]
                        S.op("act", lambda e: e.activation(out=dg.t[:, :], in_=identb.t[:, :], func=AF.Copy, scale=CC[gi].t[:, j:j + 1]), reads=[CC[gi], identb], writes=[dg])
                        for half in range(2):
                            S.op("pe", lambda e: e.matmul(PSY[half].t[:, :], lhsT=dg.t[:, :], rhs=UV[sl].t[:, D + half * 512:D + (half + 1) * 512],
                                                          start=(j == 0), stop=(j == 127)), reads=[dg, UV[sl]], writes=[PSY[half]])

                    for j in range(PRE):
                        emit_gather(j)
                    for i in range(128 + LAG):
                        if i + PRE < 128:
                            emit_gather(i + PRE)
                        if i < 128:
                            emit_dot(i)
                        if i - LAG >= 0:
                            emit_y(i - LAG)
                        if fe_next is not None and i % 2 == 1:
                            next(fe_next, None)
                    for half in range(2):
                        S.op("dve", lambda e: e.scalar_tensor_tensor(out=R2.t[:nt, half * 512:(half + 1) * 512], in0=X.t[:nt, half * 512:(half + 1) * 512], scalar=ALPHA,
                                                                     in1=PSY[half].t[:nt, :], op0=ALU.mult, op1=ALU.add), reads=[X, PSY[half]], writes=[R2])
                    layer_norm(lnst, R2, nt, G2, B2, eng2="dve")
                    if ti == 0:
                        S.op("dve", lambda e: e.tensor_scalar(out=R2.t[:], in0=R2.t[:], scalar1=padmask.t[:, 0:1], scalar2=None, op0=ALU.mult), reads=[R2, padmask], writes=[R2])
                    if l < depth - 1:
                        S.dma("sp", lambda e: e.dma_start(out=hA[row0:row0 + nt, :], in_=R2.t[:nt, :]), R2, reads=[R2])
                    elif ti > 0:
                        S.dma("sp", lambda e: e.dma_start(out=out[(ti - 1) * 128:ti * 128, :], in_=R2.t[:nt, :]), R2, reads=[R2])
                    if dbg and l == 0:
                        S.dma("sp", lambda e: e.dma_start(out=dbg_out["d_h3"][row0:row0 + nt, :], in_=R2.t[:nt, :]), R2, reads=[R2])
                    if fe_next is not None:
                        for _ in fe_next:
                            pass
                S.barrier()
        print("instructions ~", S.ninst)
    return nc


_INPUT_NAMES = ["meta_tokens", "ln_in_g", "ln_in_b", "w_in", "conv_w", "gdn_conv_w", "a_log", "dt_bias", "gdn_norm_w", "w_out",
                "ln1_g", "ln1_b", "peer_w_q", "peer_k1", "peer_k2", "peer_u", "peer_v", "ln2_g", "ln2_b"]


def kernel(**inputs):
    x = np.asarray(inputs["x"], dtype=np.float32)
    B = x.shape[0]
    shared = {k: np.ascontiguousarray(np.asarray(inputs[k], dtype=np.float32)) for k in _INPUT_NAMES}
    nc = build(NT=SEQ // 128)
    in_maps = []
    for c in range(B):
        m = dict(shared)
        m["x"] = np.ascontiguousarray(x[c])
        in_maps.append(m)
    res = run_bass_kernel_spmd(nc, in_maps, core_ids=list(range(B)))
    outs = [np.asarray(res.results[c]["out"], dtype=np.float32) for c in range(B)]
    return np.stack(outs, axis=0)
```

```python
import numpy as np
from contextlib import ExitStack
import concourse.bass as bass
import concourse.mybir as mybir
from concourse.bass_utils import run_bass_kernel_spmd

F32 = mybir.dt.float32
F32R = mybir.dt.float32r
BF16 = mybir.dt.bfloat16
U32 = mybir.dt.uint32
I32 = mybir.dt.int32
AF = mybir.ActivationFunctionType
ALU = mybir.AluOpType
AX = mybir.AxisListType

D = 1024
PC = 3592
NMETA = 16
SEQ = 8192
DEPTH = 2
ALPHA = (2.0 * DEPTH) ** 0.25
BIG = 30000.0
NEXP = 16384
G_TOK = 4
NSLOT = 16
NZ = 4


class Buf:
    def __init__(self, name):
        self.name = name
        self.w = None
        self.r = []
        self.dsem = None
        self.dcnt = 0
        self.excl = False


class T:
    def __init__(self, t, name):
        self.t = t
        self.b = Buf(name)


class Sched:
    def __init__(self, nc, stack):
        self.nc = nc
        self.stack = stack
        self.eng = {}
        for nm, e in [("pe", nc.tensor), ("act", nc.scalar), ("dve", nc.vector), ("pool", nc.gpsimd), ("sp", nc.sync)]:
            sem = stack.enter_context(nc.semaphore("s_" + nm))
            self.eng[nm] = dict(e=e, sem=sem, cnt=0, waited={}, name=nm)
        self.pool = []
        self.ninst = 0

    def _wait(self, E, deps):
        for (sem, val) in deps:
            key = id(sem)
            if E["waited"].get(key, 0) < val:
                E["e"].wait_ge(sem, val)
                E["waited"][key] = val
                self.ninst += 1

    def _deps(self, E, reads, writes):
        deps = []
        for b in reads:
            if b.w is not None:
                deps.append(b.w)
        for b in writes:
            if b.w is not None:
                deps.append(b.w)
            deps.extend(b.r)
        if E["name"] == "pe":
            deps = [d for d in deps if d[0] is not E["sem"]]
        return deps

    def _mark(self, tok, reads, writes):
        for b in reads:
            b.r = [x for x in b.r if x[0] is not tok[0]]
            b.r.append(tok)
        for b in writes:
            b.w = tok
            b.r = []

    def op(self, en, fn, reads=(), writes=()):
        E = self.eng[en]
        reads = [x.b if isinstance(x, T) else x for x in reads]
        writes = [x.b if isinstance(x, T) else x for x in writes]
        writes = writes + [b for b in reads if b.excl and b not in writes]
        reads = [b for b in reads if not b.excl]
        self._wait(E, self._deps(E, reads, writes))
        inst = fn(E["e"])
        E["cnt"] += 1
        inst.then_inc(E["sem"], 1)
        self.ninst += 1
        self._mark((E["sem"], E["cnt"]), reads, writes)
        return inst

    def dma(self, en, fn, owner, reads=(), writes=()):
        E = self.eng[en]
        owner = owner.b if isinstance(owner, T) else owner
        reads = [x.b if isinstance(x, T) else x for x in reads]
        writes = [x.b if isinstance(x, T) else x for x in writes]
        self._wait(E, self._deps(E, reads, writes))
        kind = "sw" if en == "pool" else "hw"
        if owner.dsem is None:
            owner.dsem = {}
        if kind not in owner.dsem:
            free = [p for p in self.pool if p["owner"] is None and p["kind"] == kind]
            if free:
                ent = free[0]
            else:
                ent = dict(sem=self.stack.enter_context(self.nc.semaphore("dq%d" % len(self.pool))), cnt=0, owner=None, kind=kind)
                self.pool.append(ent)
            ent["owner"] = owner
            owner.dsem[kind] = ent
        ent = owner.dsem[kind]
        inst = fn(E["e"])
        ent["cnt"] += 16
        inst.then_inc(ent["sem"], 16)
        self.ninst += 1
        self._mark((ent["sem"], ent["cnt"]), reads, writes)
        return inst

    def barrier(self, release=True):
        toks = [(E["sem"], E["cnt"]) for E in self.eng.values() if E["cnt"] > 0]
        toks += [(p["sem"], p["cnt"]) for p in self.pool if p["cnt"] > 0]
        for E in self.eng.values():
            self._wait(E, [t for t in toks if t[0] is not E["sem"]])
        if release:
            for p in self.pool:
                if p["owner"] is not None:
                    p["owner"].dsem = None
                    p["owner"] = None


def fr(ap):
    return ap.bitcast(F32R)


def bc(ap, shape):
    return ap.to_broadcast(list(shape))


def build(NT=64, depth=DEPTH, dbg=False, skip_p2=False, skip_p1=False, cut=99):
    nc = bass.Bass("TRN2", target_bir_lowering=False)
    LTOK = 128 * (NT + 1)
    dr = {}

    def din(name, shape, dt=F32):
        dr[name] = nc.dram_tensor(name, list(shape), dt, kind="ExternalInput").ap()
        return dr[name]

    x = din("x", [128 * NT, D])
    meta = din("meta_tokens", [NMETA, D])
    ln_in_g = din("ln_in_g", [D]); ln_in_b = din("ln_in_b", [D])
    w_in = din("w_in", [DEPTH, D, PC])
    conv_w = din("conv_w", [DEPTH, 3, 512])
    gdn_conv_w = din("gdn_conv_w", [DEPTH, 4, 1536])
    a_log = din("a_log", [DEPTH, 4]); dt_bias = din("dt_bias", [DEPTH, 4])
    gdn_norm_w = din("gdn_norm_w", [DEPTH, 128])
    w_out = din("w_out", [DEPTH, D, D])
    ln1_g = din("ln1_g", [DEPTH, D]); ln1_b = din("ln1_b", [DEPTH, D])
    peer_w_q = din("peer_w_q", [DEPTH, D, 2048])
    peer_k1 = din("peer_k1", [DEPTH, 128, 128]); peer_k2 = din("peer_k2", [DEPTH, 128, 128])
    peer_u = din("peer_u", [DEPTH, NEXP, D]); peer_v = din("peer_v", [DEPTH, NEXP, D])
    ln2_g = din("ln2_g", [DEPTH, D]); ln2_b = din("ln2_b", [DEPTH, D])
    out = nc.dram_tensor("out", [128 * NT, D], F32, kind="ExternalOutput").ap()
    hA = nc.dram_tensor("hA", [LTOK, D], F32, kind="Internal").ap()
    hB = nc.dram_tensor("hB", [LTOK, D], F32, kind="Internal").ap()
    TAB = nc.dram_tensor("TAB", [DEPTH * NEXP, 2 * D], BF16, kind="Internal").ap()
    dbg_out = {}
    if dbg:
        dbg_out["d_h0"] = nc.dram_tensor("d_h0", [LTOK, D], F32, kind="ExternalOutput").ap()
        dbg_out["d_h2"] = nc.dram_tensor("d_h2", [LTOK, D], F32, kind="ExternalOutput").ap()
        dbg_out["d_h3"] = nc.dram_tensor("d_h3", [LTOK, D], F32, kind="ExternalOutput").ap()

    tiles = [(128 * i, 128) for i in range(NT + 1)]

    top = ExitStack()
    with top:
        S = Sched(nc, top)

        uniq = [0]

        def mk(st, name, shape, dt=F32):
            uniq[0] += 1
            name = "%s_%d" % (name, uniq[0])
            return T(st.enter_context(nc.sbuf_tensor(name, list(shape), dt)), name)

        PSB = top.enter_context(nc.psum_tensor("psb", [128, 4096], F32))
        PS = [T(PSB[:, 512 * i:512 * (i + 1)], "ps%d" % i) for i in range(8)]
        XBV = [PSB[:, 2048 + 1024 * j:2048 + 1024 * (j + 1)] for j in range(2)]
        for p_ in PS:
            p_.b.excl = True
        psctr = [0]

        def nps(lo=0, hi=8):
            i = lo + psctr[0] % (hi - lo)
            psctr[0] += 1
            return PS[i]

        def v4(p):
            return p.t[:].rearrange("p (a b) -> p a b", a=4)

        ident = mk(top, "ident", [128, 128])
        identb = mk(top, "identb", [128, 128], BF16)
        ones = mk(top, "ones", [128, 128])
        LT = mk(top, "LT", [128, 128])
        UTS = mk(top, "UTS", [128, 128])
        MPOS = mk(top, "MPOS", [128, 128])
        MNEG = mk(top, "MNEG", [128, 128])
        iot = mk(top, "iot", [128, 128], I32)
        iof = mk(top, "iof", [128, 128])
        iota16 = mk(top, "iota16", [128, 16])
        S.op("pool", lambda e: e.iota(iot.t[:], pattern=[[1, 128]], base=0, channel_multiplier=-1), writes=[iot])
        S.op("dve", lambda e: e.tensor_copy(out=iof.t[:], in_=iot.t[:]), reads=[iot], writes=[iof])
        S.op("dve", lambda e: e.tensor_scalar(out=ident.t[:], in0=iof.t[:], scalar1=0.0, scalar2=None, op0=ALU.is_equal), reads=[iof], writes=[ident])
        S.op("dve", lambda e: e.tensor_copy(out=identb.t[:], in_=ident.t[:]), reads=[ident], writes=[identb])
        S.op("dve", lambda e: e.memset(ones.t[:], 1.0), writes=[ones])
        onesr = mk(top, "onesr", [128, 128])
        S.op("dve", lambda e: e.tensor_scalar(out=fr(onesr.t[:]), in0=iof.t[:], scalar1=0.0, scalar2=1.0, op0=ALU.mult, op1=ALU.add), reads=[iof], writes=[onesr])
        S.op("dve", lambda e: e.tensor_scalar(out=LT.t[:], in0=iof.t[:], scalar1=0.0, scalar2=None, op0=ALU.is_ge), reads=[iof], writes=[LT])
        S.op("dve", lambda e: e.tensor_scalar(out=UTS.t[:], in0=iof.t[:], scalar1=0.0, scalar2=None, op0=ALU.is_lt), reads=[iof], writes=[UTS])
        S.op("dve", lambda e: e.tensor_scalar(out=MPOS.t[:], in0=iof.t[:], scalar1=0.0, scalar2=BIG, op0=ALU.is_ge, op1=ALU.mult), reads=[iof], writes=[MPOS])
        S.op("dve", lambda e: e.tensor_scalar(out=MNEG.t[:], in0=iof.t[:], scalar1=0.0, scalar2=-BIG, op0=ALU.is_lt, op1=ALU.mult), reads=[iof], writes=[MNEG])
        padmask = mk(top, "padmask", [128, 1])
        S.op("dve", lambda e: e.tensor_scalar(out=padmask.t[:], in0=iof.t[:, 0:1], scalar1=-111.5, scalar2=None, op0=ALU.is_lt), reads=[iof], writes=[padmask])
        iot2 = mk(top, "iot2", [128, 16], I32)
        S.op("pool", lambda e: e.iota(iot2.t[:], pattern=[[1, 16]], base=0, channel_multiplier=0), writes=[iot2])
        S.op("dve", lambda e: e.tensor_copy(out=iota16.t[:], in_=iot2.t[:]), reads=[iot2], writes=[iota16])

        def layer_norm(st_tiles, X, nt, Gt, Bt, eng2="pool"):
            stats, mv, rstd = st_tiles
            for c in range(2):
                S.op("dve", lambda e: e.bn_stats(out=stats.t[:nt, c, :], in_=X.t[:nt, c * 512:(c + 1) * 512]), reads=[X], writes=[stats])
            S.op("dve", lambda e: e.bn_aggr(out=mv.t[:nt, :], in_=stats.t[:nt].rearrange("p a b -> p (a b)")), reads=[stats], writes=[mv])
            S.op("act", lambda e: e.activation(out=rstd.t[:nt, :], in_=mv.t[:nt, 1:2], func=AF.Sqrt, bias=1e-5, scale=1.0), reads=[mv], writes=[rstd])
            S.op("dve", lambda e: e.reciprocal(out=rstd.t[:nt, :], in_=rstd.t[:nt, :]), reads=[rstd], writes=[rstd])
            S.op("dve", lambda e: e.tensor_scalar(out=X.t[:nt, :], in0=X.t[:nt, :], scalar1=mv.t[:nt, 0:1], scalar2=rstd.t[:nt, 0:1],
                                                  op0=ALU.subtract, op1=ALU.mult), reads=[X, mv, rstd], writes=[X])
            S.op(eng2, lambda e: e.tensor_tensor(out=X.t[:nt, :], in0=X.t[:nt, :], in1=Gt.t[:nt, :], op=ALU.mult), reads=[X, Gt], writes=[X])
            S.op(eng2, lambda e: e.tensor_tensor(out=X.t[:nt, :], in0=X.t[:nt, :], in1=Bt.t[:nt, :], op=ALU.add), reads=[X, Bt], writes=[X])

        def make_T(X, nt, HT, lo=0, hi=8):
            for half in range(2):
                p = nps(lo, hi)
                pv = v4(p)
                for c in range(4):
                    k = half * 4 + c
                    S.op("pe", lambda e: e.transpose(out=pv[:, c, :nt], in_=X.t[:nt, k * 128:(k + 1) * 128], identity=ident.t[:nt, :nt]),
                         reads=[X, ident], writes=[p])
                S.op("act", lambda e: e.copy(out=HT.t[:, half * 4:(half + 1) * 4, :nt], in_=pv[:, :, :nt]), reads=[p], writes=[HT])

        with ExitStack() as st:
            g0 = mk(st, "lnin_g", [128, D]); b0 = mk(st, "lnin_b", [128, D])
            S.dma("sp", lambda e: e.dma_start(out=g0.t[:], in_=ln_in_g.partition_broadcast(128)), g0, writes=[g0])
            S.dma("sp", lambda e: e.dma_start(out=b0.t[:], in_=ln_in_b.partition_broadcast(128)), b0, writes=[b0])
            XI = [mk(st, "p0x%d" % i, [128, D]) for i in range(3)]
            STG = [mk(st, "stg%d" % i, [128, 8, D], BF16) for i in range(3)]
            tabv = TAB.rearrange("(l p r) w -> l p r w", l=DEPTH, p=128)
            ci_ = 0
            for l in range(depth):
                for src, off in ((peer_u, 0), (peer_v, D)):
                    srcv = src[l].rearrange("(p r) d -> p r d", p=128)
                    for c in range(16):
                        sg = STG[ci_ % 3]
                        ci_ += 1
                        S.dma("pool", lambda e: e.dma_start(out=sg.t[:], in_=srcv[:, 8 * c:8 * c + 8, :]), sg, writes=[sg])
                        S.dma("act", lambda e: e.dma_start(out=tabv[l][:, 8 * c:8 * c + 8, off:off + D], in_=sg.t[:]), sg, reads=[sg])
            lnst = (mk(st, "p0stats", [128, 2, 6]), mk(st, "p0mv", [128, 2]), mk(st, "p0rstd", [128, 1]))
            for ti, (row0, nt) in enumerate(tiles):
                X = XI[ti % 3]
                if ti == 0:
                    S.op("dve", lambda e: e.memset(X.t[:], 0.0), writes=[X])
                    S.dma("sp", lambda e: e.dma_start(out=X.t[112:128, :], in_=meta[:, :]), X, writes=[X])
                else:
                    S.dma("sp", lambda e: e.dma_start(out=X.t[:nt, :], in_=x[(ti - 1) * 128:ti * 128, :]), X, writes=[X])
                layer_norm(lnst, X, nt, g0, b0)
                if ti == 0:
                    S.op("dve", lambda e: e.tensor_scalar(out=X.t[:], in0=X.t[:], scalar1=padmask.t[:, 0:1], scalar2=None, op0=ALU.mult), reads=[X, padmask], writes=[X])
                S.dma("sp", lambda e: e.dma_start(out=hA[row0:row0 + nt, :], in_=X.t[:nt, :]), X, reads=[X])
                if dbg:
                    S.dma("sp", lambda e: e.dma_start(out=dbg_out["d_h0"][row0:row0 + nt, :], in_=X.t[:nt, :]), X, reads=[X])
            S.barrier()

        for l in range(depth):
            with ExitStack() as st:
                if skip_p1:
                    break
                WIN = mk(st, "WIN", [128, 8, PC], BF16)
                WOUT = mk(st, "WOUT", [128, 8, D], BF16)
                for k in range(8):
                    S.dma("pool", lambda e: e.dma_start(out=WIN.t[:, k, :], in_=w_in[l, k * 128:(k + 1) * 128, :]), WIN, writes=[WIN])
                    S.dma("pool", lambda e: e.dma_start(out=WOUT.t[:, k, :], in_=w_out[l, k * 128:(k + 1) * 128, :]), WOUT, writes=[WOUT])
                G1 = mk(st, "ln1g", [128, D]); B1 = mk(st, "ln1b", [128, D])
                S.dma("sp", lambda e: e.dma_start(out=G1.t[:], in_=ln1_g[l].partition_broadcast(128)), G1, writes=[G1])
                S.dma("sp", lambda e: e.dma_start(out=B1.t[:], in_=ln1_b[l].partition_broadcast(128)), B1, writes=[B1])
                CW = mk(st, "CW", [128, 4, 3]); GW = mk(st, "GW", [128, 12, 4])
                for j in range(3):
                    S.dma("sp", lambda e: e.dma_start(out=CW.t[:, :, j], in_=conv_w[l, j].rearrange("(b p) -> p b", p=128), allow_slow_non_contiguous=True), CW, writes=[CW])
                for j in range(4):
                    S.dma("sp", lambda e: e.dma_start(out=GW.t[:, :, j], in_=gdn_conv_w[l, j].rearrange("(b p) -> p b", p=128), allow_slow_non_contiguous=True), GW, writes=[GW])
                GNW = mk(st, "GNW", [128, 1])
                S.dma("sp", lambda e: e.dma_start(out=GNW.t[:], in_=gdn_norm_w[l].rearrange("(p o) -> p o", o=1)), GNW, writes=[GNW])
                NEGA = mk(st, "NEGA", [128, 4]); DTB = mk(st, "DTB", [128, 4])
                S.dma("sp", lambda e: e.dma_start(out=NEGA.t[:], in_=a_log[l].partition_broadcast(128)), NEGA, writes=[NEGA])
                S.dma("sp", lambda e: e.dma_start(out=DTB.t[:], in_=dt_bias[l].partition_broadcast(128)), DTB, writes=[DTB])
                S.op("act", lambda e: e.activation(out=NEGA.t[:], in_=NEGA.t[:], func=AF.Exp), reads=[NEGA], writes=[NEGA])
                S.op("dve", lambda e: e.tensor_scalar(out=NEGA.t[:], in0=NEGA.t[:], scalar1=-1.0, scalar2=None, op0=ALU.mult), reads=[NEGA], writes=[NEGA])

                XIN = [mk(st, "xin%d" % i, [128, D]) for i in range(2)]
                R = mk(st, "R", [128, D])
                HT = mk(st, "HT", [128, 8, 128], BF16)
                PJ = mk(st, "PJ", [128, 16, 128])
                QKVP = mk(st, "QKVP", [128, 12, 131])
                UC = mk(st, "UC", [128, 4, 130])
                CV = mk(st, "CV", [128, 4, 128])
                CA = mk(st, "CA", [128, 12, 128])
                CT = mk(st, "CT", [128, 12, 128])
                SQ = mk(st, "SQ", [128, 8, 128])
                QKN = mk(st, "QKN", [128, 8, 128])
                KV = mk(st, "KV", [128, 8, 128])
                KVS = mk(st, "KVS", [128, 12, 128])
                DD = mk(st, "DD", [128, 8, 128])
                GH = mk(st, "GH", [128, 4, 128])
                EGB = mk(st, "EGB", [128, 4, 128])
                Pm = [mk(st, "Pm%d" % i, [128, 4, 128]) for i in range(2)]
                PTm = [mk(st, "PTm%d" % i, [128, 4, 128]) for i in range(2)]
                XT = mk(st, "XT", [128, 4, 128])
                NWT = mk(st, "NWT", [128, 4, 128])
                VNEW = mk(st, "VNEW", [128, 4, 128])
                ATT = mk(st, "ATT", [128, 4, 128])
                QD = mk(st, "QD", [128, 4, 128])
                Sst = mk(st, "Sst", [128, 4, 128])
                ZS = mk(st, "ZS", [128, 4, 128])
                T1 = mk(st, "T1", [128, 4, 128])
                YM = mk(st, "YM", [128, 8, 128], BF16)
                SM = mk(st, "SM", [128, 12])
                SM2 = mk(st, "SM2", [128, 16])
                SM3 = mk(st, "SM3", [128, 4])
                lnst = (mk(st, "p1stats", [128, 2, 6]), mk(st, "p1mv", [128, 2]), mk(st, "p1rstd", [128, 1]))
                for h in range(4):
                    S.op("dve", lambda e: e.tensor_scalar(out=fr(Sst.t[:, h, :]), in0=iof.t[:, :], scalar1=0.0, scalar2=None, op0=ALU.mult), reads=[iof], writes=[Sst])
                S.op("dve", lambda e: e.memset(QKVP.t[:], 0.0), writes=[QKVP])
                S.op("dve", lambda e: e.memset(UC.t[:], 0.0), writes=[UC])

                PJ2 = [PJ, mk(st, "PJb", [128, 16, 128])]
                QKVP2 = [QKVP, mk(st, "QKVPb", [128, 12, 131])]
                HT2 = [HT, mk(st, "HTb", [128, 8, 128], BF16)]
                PAB = [PS[7], PS[7]]
                S.op("dve", lambda e: e.memset(QKVP2[1].t[:], 0.0), writes=[QKVP2[1]])

                def front1(ti):
                    row0, nt = tiles[ti]
                    X = XIN[ti % 2]
                    HTc = HT2[ti % 2]; PJc = PJ2[ti % 2]; QKVPc = QKVP2[ti % 2]; pab = PAB[ti % 2]
                    S.dma("sp", lambda e: e.dma_start(out=X.t[:nt, :], in_=hA[row0:row0 + nt, :]), X, writes=[X])
                    yield
                    make_T(X, nt, HTc, lo=5, hi=7)
                    yield
                    for g in range(7):
                        p = nps(5, 7); pv = v4(p)
                        for c in range(4):
                            blk = g * 4 + c
                            for k in range(8):
                                S.op("pe", lambda e: e.matmul(pv[:, c, :nt], lhsT=WIN.t[:, k, blk * 128:(blk + 1) * 128], rhs=HTc.t[:, k, :nt],
                                                              start=(k == 0), stop=(k == 7)), reads=[WIN, HTc], writes=[p])
                            yield
                        if g < 3:
                            dst, dT = PJc.t[:, 4 * g:4 * g + 4, :nt], PJc
                        elif g < 6:
                            dst, dT = QKVPc.t[:, 4 * (g - 3):4 * (g - 3) + 4, 3:3 + nt], QKVPc
                        else:
                            dst, dT = PJc.t[:, 12:16, :nt], PJc
                        if g % 2 == 0:
                            S.op("act", lambda e: e.copy(out=dst, in_=pv[:, :, :nt]), reads=[p], writes=[dT])
                        else:
                            S.op("dve", lambda e: e.tensor_copy(out=dst, in_=pv[:, :, :nt]), reads=[p], writes=[dT])
                    for k in range(8):
                        S.op("pe", lambda e: e.matmul(pab.t[:nt, 0:8], lhsT=HTc.t[:, k, :nt], rhs=WIN.t[:, k, 3584:3592], start=(k == 0), stop=(k == 7)),
                             reads=[WIN, HTc], writes=[pab])
                    if ti > 0:
                        QKVPp = QKVP2[(ti - 1) % 2]
                        S.op("act", lambda e: e.copy(out=QKVPc.t[:, :, 0:3], in_=QKVPp.t[:, :, nt:nt + 3]), reads=[QKVPp], writes=[QKVPc])
                    yield

                f0_ = front1(0)
                for _ in f0_:
                    pass
                for ti, (row0, nt) in enumerate(tiles):
                    X = XIN[ti % 2]
                    PJ = PJ2[ti % 2]; QKVP = QKVP2[ti % 2]; pab = PAB[ti % 2]
                    fnext = front1(ti + 1) if ti + 1 < len(tiles) else None

                    def pump(n=2):
                        if fnext is not None:
                            for _ in range(n):
                                next(fnext, None)
                    S.op("dve", lambda e: e.tensor_tensor(out=UC.t[:, :, 2:2 + nt], in0=PJ.t[:, 4:8, :nt], in1=PJ.t[:, 8:12, :nt], op=ALU.mult), reads=[PJ], writes=[UC])
                    for j in range(3):
                        if j == 0:
                            S.op("dve", lambda e: e.tensor_tensor(out=CV.t[:, :, :nt], in0=UC.t[:, :, 0:nt], in1=bc(CW.t[:, :, 0:1], [128, 4, nt]), op=ALU.mult),
                                 reads=[UC, CW], writes=[CV])
                        else:
                            S.op("dve", lambda e: e.tensor_tensor(out=CT.t[:, 0:4, :nt], in0=UC.t[:, :, j:j + nt], in1=bc(CW.t[:, :, j:j + 1], [128, 4, nt]), op=ALU.mult),
                                 reads=[UC, CW], writes=[CT])
                            S.op("dve", lambda e: e.tensor_tensor(out=CV.t[:, :, :nt], in0=CV.t[:, :, :nt], in1=CT.t[:, 0:4, :nt], op=ALU.add), reads=[CV, CT], writes=[CV])
                    S.op("dve", lambda e: e.tensor_tensor(out=YM.t[:, 0:4, :nt], in0=PJ.t[:, 0:4, :nt], in1=CV.t[:, :, :nt], op=ALU.mult), reads=[PJ, CV], writes=[YM])
                    S.op("dve", lambda e: e.tensor_copy(out=UC.t[:, :, 0:2], in_=UC.t[:, :, nt:nt + 2]), reads=[UC], writes=[UC])
                    if cut <= 3:
                        continue
                    pump()
                    for j in range(4):
                        if j == 0:
                            S.op("dve", lambda e: e.tensor_tensor(out=CA.t[:, :, :nt], in0=QKVP.t[:, :, 0:nt], in1=bc(GW.t[:, :, 0:1], [128, 12, nt]), op=ALU.mult),
                                 reads=[QKVP, GW], writes=[CA])
                        else:
                            S.op("dve", lambda e: e.tensor_tensor(out=CT.t[:, :, :nt], in0=QKVP.t[:, :, j:j + nt], in1=bc(GW.t[:, :, j:j + 1], [128, 12, nt]), op=ALU.mult),
                                 reads=[QKVP, GW], writes=[CT])
                            S.op("dve", lambda e: e.tensor_tensor(out=CA.t[:, :, :nt], in0=CA.t[:, :, :nt], in1=CT.t[:, :, :nt], op=ALU.add), reads=[CA, CT], writes=[CA])
                    S.op("act", lambda e: e.activation(out=CA.t[:, :, :nt], in_=CA.t[:, :, :nt], func=AF.Silu), reads=[CA], writes=[CA])
                    if cut <= 4:
                        continue
                    pump()
                    S.op("act", lambda e: e.activation(out=fr(SQ.t[:, 0:8, :nt]), in_=CA.t[:, 0:8, :nt], func=AF.Square), reads=[CA], writes=[SQ])
                    for half in range(2):
                        p = nps(0, 5); pv = v4(p)
                        S.op("pe", lambda e: e.matmul(pv[:, :, :nt], lhsT=fr(onesr.t[:, :]), rhs=fr(SQ.t[:, 4 * half:4 * half + 4, :nt]), start=True, stop=True),
                             reads=[onesr, SQ], writes=[p])
                        S.op("act", lambda e: e.activation(out=CT.t[:, 4 * half:4 * half + 4, :nt], in_=pv[:, :, :nt], func=AF.Ln, bias=1e-6, scale=1.0),
                             reads=[p], writes=[CT])
                    S.op("act", lambda e: e.activation(out=CT.t[:, 0:8, :nt], in_=CT.t[:, 0:8, :nt], func=AF.Exp, scale=-0.5), reads=[CT], writes=[CT])
                    S.op("dve", lambda e: e.scalar_tensor_tensor(out=fr(QKN.t[:, 0:4, :nt]), in0=CA.t[:, 0:4, :nt], scalar=128.0 ** -0.5, in1=CT.t[:, 0:4, :nt],
                                                                 op0=ALU.mult, op1=ALU.mult), reads=[CA, CT], writes=[QKN])
                    S.op("dve", lambda e: e.tensor_tensor(out=fr(QKN.t[:, 4:8, :nt]), in0=CA.t[:, 4:8, :nt], in1=CT.t[:, 4:8, :nt], op=ALU.mult), reads=[CA, CT], writes=[QKN])
                    if cut <= 5:
                        continue
                    pump()
                    for which in range(2):
                        p = nps(0, 5); pv = v4(p)
                        for h in range(4):
                            src = QKN.t[:, 4 + h, :nt] if which == 0 else CA.t[:, 8 + h, :nt]
                            S.op("pe", lambda e: e.transpose(out=pv[:nt, h, :], in_=src, identity=ident.t[:, :]), reads=[QKN, CA, ident], writes=[p])
                        S.op("act", lambda e: e.copy(out=KV.t[:nt, 4 * which:4 * which + 4, :], in_=pv[:nt, :, :]), reads=[p], writes=[KV])
                    if cut <= 6:
                        continue
                    pump()
                    S.op("dve", lambda e: e.tensor_tensor(out=SM.t[:nt, 0:4], in0=pab.t[:nt, 0:4], in1=DTB.t[:nt, :], op=ALU.add), reads=[pab, DTB], writes=[SM])
                    S.op("act", lambda e: e.activation(out=SM.t[:nt, 0:4], in_=SM.t[:nt, 0:4], func=AF.Exp), reads=[SM], writes=[SM])
                    S.op("act", lambda e: e.activation(out=SM.t[:nt, 0:4], in_=SM.t[:nt, 0:4], func=AF.Ln, bias=1.0, scale=1.0), reads=[SM], writes=[SM])
                    S.op("dve", lambda e: e.tensor_tensor(out=SM.t[:nt, 4:8], in0=SM.t[:nt, 0:4], in1=NEGA.t[:nt, :], op=ALU.mult), reads=[SM, NEGA], writes=[SM])
                    S.op("act", lambda e: e.activation(out=SM.t[:nt, 8:12], in_=pab.t[:nt, 4:8], func=AF.Sigmoid), reads=[pab], writes=[SM])
                    pg = nps(0, 5)
                    S.op("pe", lambda e: e.matmul(pg.t[:nt, 0:4], lhsT=LT.t[:nt, :nt], rhs=SM.t[:nt, 4:8], start=True, stop=True), reads=[LT, SM], writes=[pg])
                    S.op("pe", lambda e: e.matmul(pg.t[:nt, 4:8], lhsT=UTS.t[:nt, :nt], rhs=SM.t[:nt, 4:8], start=True, stop=True), reads=[UTS, SM], writes=[pg])
                    S.op("pe", lambda e: e.matmul(pg.t[:, 8:12], lhsT=ones.t[:nt, :], rhs=SM.t[:nt, 4:8], start=True, stop=True), reads=[ones, SM], writes=[pg])
                    S.op("dve", lambda e: e.tensor_copy(out=SM2.t[:nt, 0:4], in_=pg.t[:nt, 0:4]), reads=[pg], writes=[SM2])
                    S.op("act", lambda e: e.activation(out=SM2.t[:nt, 4:12], in_=pg.t[:nt, 0:8], func=AF.Exp), reads=[pg], writes=[SM2])
                    S.op("act", lambda e: e.activation(out=SM3.t[:, 0:4], in_=pg.t[:, 8:12], func=AF.Exp), reads=[pg], writes=[SM3])
                    S.op("dve", lambda e: e.tensor_tensor(out=SM2.t[:nt, 12:16], in0=SM.t[:nt, 8:12], in1=SM2.t[:nt, 4:8], op=ALU.mult), reads=[SM, SM2], writes=[SM2])
                    if cut <= 7:
                        continue
                    S.op("dve", lambda e: e.tensor_tensor(out=GH.t[:nt, :, :nt], in0=bc(LT.t[:nt, :nt].unsqueeze(1), [nt, 4, nt]),
                                                          in1=bc(SM.t[:nt, 4:8].unsqueeze(2), [nt, 4, nt]), op=ALU.mult), reads=[LT, SM], writes=[GH])
                    pc_ = nps(0, 5); pcv = v4(pc_)
                    S.op("pe", lambda e: e.matmul(pcv[:, :, :nt], lhsT=ones.t[:nt, :], rhs=GH.t[:nt, :, :nt], start=True, stop=True), reads=[ones, GH], writes=[pc_])
                    S.op("act", lambda e: e.activation(out=EGB.t[:, :, :nt], in_=pcv[:, :, :nt], func=AF.Exp), reads=[pc_], writes=[EGB])
                    for h in range(4):
                        S.op("dve", lambda e: e.scalar_tensor_tensor(out=DD.t[:nt, h, :nt], in0=pcv[:nt, h, :nt], scalar=SM2.t[:nt, h:h + 1], in1=MPOS.t[:nt, :nt],
                                                                     op0=ALU.subtract, op1=ALU.add), reads=[pc_, SM2, MPOS], writes=[DD])
                        S.op("dve", lambda e: e.scalar_tensor_tensor(out=DD.t[:nt, 4 + h, :nt], in0=pcv[:nt, h, :nt], scalar=SM2.t[:nt, h:h + 1], in1=MNEG.t[:nt, :nt],
                                                                     op0=ALU.subtract, op1=ALU.add), reads=[pc_, SM2, MNEG], writes=[DD])
                    S.op("act", lambda e: e.activation(out=DD.t[:nt, 0:4, :nt], in_=DD.t[:nt, 0:4, :nt], func=AF.Exp, scale=-1.0), reads=[DD], writes=[DD])
                    S.op("act", lambda e: e.activation(out=DD.t[:nt, 4:8, :nt], in_=DD.t[:nt, 4:8, :nt], func=AF.Exp), reads=[DD], writes=[DD])
                    if cut <= 8:
                        continue
                    pump()
                    pk = nps(0, 5); pkv = v4(pk)
                    for h in range(4):
                        S.op("pe", lambda e: e.matmul(pkv[:nt, h, :nt], lhsT=fr(QKN.t[:, 4 + h, :nt]), rhs=fr(QKN.t[:, 4 + h, :nt]), start=True, stop=True), reads=[QKN], writes=[pk])
                    for h in range(4):
                        S.op("dve", lambda e: e.scalar_tensor_tensor(out=fr(Pm[0].t[:nt, h, :nt]), in0=pkv[:nt, h, :nt], scalar=SM.t[:nt, 8 + h:9 + h], in1=DD.t[:nt, h, :nt],
                                                                     op0=ALU.mult, op1=ALU.mult), reads=[pk, SM, DD], writes=[Pm[0]])
                    pt = nps(0, 5); ptv = v4(pt)
                    for h in range(4):
                        S.op("pe", lambda e: e.transpose(out=ptv[:nt, h, :nt], in_=Pm[0].t[:nt, h, :nt], identity=ident.t[:nt, :nt]), reads=[Pm[0], ident], writes=[pt])
                    S.op("act", lambda e: e.copy(out=fr(PTm[0].t[:nt, :, :nt]), in_=ptv[:nt, :, :nt]), reads=[pt], writes=[PTm[0]])
                    S.op("dve", lambda e: e.tensor_tensor(out=fr(XT.t[:nt, :, :nt]), in0=bc(ident.t[:nt, :nt].unsqueeze(1), [nt, 4, nt]), in1=ptv[:nt, :, :nt], op=ALU.subtract),
                         reads=[ident, pt], writes=[XT])
                    smax = 6 if nt == 128 else 3
                    for s in range(1, smax + 1):
                        cur, nxt = (s - 1) % 2, s % 2
                        pump(2)
                        pp = nps(0, 5); ppv = v4(pp)
                        for h in range(4):
                            S.op("pe", lambda e: e.matmul(ppv[:nt, h, :nt], lhsT=fr(PTm[cur].t[:nt, h, :nt]), rhs=fr(Pm[cur].t[:nt, h, :nt]), start=True, stop=True),
                                 reads=[PTm[cur], Pm[cur]], writes=[pp])
                        S.op("act", lambda e: e.copy(out=fr(Pm[nxt].t[:nt, :, :nt]), in_=ppv[:nt, :, :nt]), reads=[pp], writes=[Pm[nxt]])
                        if s < smax:
                            pq = nps(0, 5); pqv = v4(pq)
                            for h in range(4):
                                S.op("pe", lambda e: e.matmul(pqv[:nt, h, :nt], lhsT=fr(Pm[cur].t[:nt, h, :nt]), rhs=fr(PTm[cur].t[:nt, h, :nt]), start=True, stop=True),
                                     reads=[PTm[cur], Pm[cur]], writes=[pq])
                            S.op("dve", lambda e: e.tensor_copy(out=fr(PTm[nxt].t[:nt, :, :nt]), in_=pqv[:nt, :, :nt]), reads=[pq], writes=[PTm[nxt]])
                        px = nps(0, 5); pxv = v4(px)
                        for h in range(4):
                            S.op("pe", lambda e: e.matmul(pxv[:nt, h, :nt], lhsT=fr(Pm[nxt].t[:nt, h, :nt]), rhs=fr(XT.t[:nt, h, :nt]), start=True, stop=True),
                                 reads=[Pm[nxt], XT], writes=[px])
                        S.op("dve", lambda e: e.tensor_tensor(out=fr(XT.t[:nt, :, :nt]), in0=pxv[:nt, :, :nt], in1=XT.t[:nt, :, :nt], op=ALU.add), reads=[px, XT], writes=[XT])
                    if cut <= 9:
                        continue
                    pump()
                    S.op("dve", lambda e: e.tensor_tensor(out=fr(KVS.t[:nt, 0:4, :]), in0=KV.t[:nt, 4:8, :], in1=bc(SM.t[:nt, 8:12].unsqueeze(2), [nt, 4, 128]), op=ALU.mult),
                         reads=[KV, SM], writes=[KVS])
                    S.op("dve", lambda e: e.tensor_tensor(out=fr(KVS.t[:nt, 4:8, :]), in0=KV.t[:nt, 0:4, :], in1=bc(SM2.t[:nt, 12:16].unsqueeze(2), [nt, 4, 128]), op=ALU.mult),
                         reads=[KV, SM2], writes=[KVS])
                    S.op("dve", lambda e: e.tensor_tensor(out=fr(KVS.t[:nt, 8:12, :]), in0=KV.t[:nt, 0:4, :], in1=bc(SM2.t[:nt, 8:12].unsqueeze(2), [nt, 4, 128]), op=ALU.mult),
                         reads=[KV, SM2], writes=[KVS])
                    if cut <= 10:
                        continue
                    pump()
                    pw = nps(0, 5); pwv = v4(pw)
                    for h in range(4):
                        S.op("pe", lambda e: e.matmul(pwv[:, h, :nt], lhsT=fr(KVS.t[:nt, 4 + h, :]), rhs=fr(XT.t[:nt, h, :nt]), start=True, stop=True), reads=[KVS, XT], writes=[pw])
                    S.op("act", lambda e: e.activation(out=fr(NWT.t[:, :, :nt]), in_=pwv[:, :, :nt], func=AF.Copy, scale=-1.0), reads=[pw], writes=[NWT])
                    pump()
                    pvn = nps(0, 5); pvnv = v4(pvn)
                    for h in range(4):
                        S.op("pe", lambda e: e.matmul(pvnv[:nt, h, :], lhsT=fr(XT.t[:nt, h, :nt]), rhs=fr(KVS.t[:nt, h, :]), start=True, stop=False), reads=[XT, KVS], writes=[pvn])
                        S.op("pe", lambda e: e.matmul(pvnv[:nt, h, :], lhsT=fr(NWT.t[:, h, :nt]), rhs=fr(Sst.t[:, h, :]), start=False, stop=True), reads=[NWT, Sst], writes=[pvn])
                    S.op("act", lambda e: e.copy(out=fr(VNEW.t[:nt, :, :]), in_=pvnv[:nt, :, :]), reads=[pvn], writes=[VNEW])
                    if cut <= 11:
                        continue
                    pump()
                    pa = nps(0, 5); pav = v4(pa)
                    for h in range(4):
                        S.op("pe", lambda e: e.matmul(pav[:nt, h, :nt], lhsT=fr(QKN.t[:, 4 + h, :nt]), rhs=fr(QKN.t[:, h, :nt]), start=True, stop=True), reads=[QKN], writes=[pa])
                    S.op("dve", lambda e: e.tensor_tensor(out=fr(ATT.t[:nt, :, :nt]), in0=pav[:nt, :, :nt], in1=DD.t[:nt, 4:8, :nt], op=ALU.mult), reads=[pa, DD], writes=[ATT])
                    S.op("dve", lambda e: e.tensor_tensor(out=fr(QD.t[:, :, :nt]), in0=QKN.t[:, 0:4, :nt], in1=EGB.t[:, :, :nt], op=ALU.mult), reads=[QKN, EGB], writes=[QD])
                    pump()
                    po = nps(0, 5); pov = v4(po)
                    for h in range(4):
                        S.op("pe", lambda e: e.matmul(pov[:, h, :nt], lhsT=fr(Sst.t[:, h, :]), rhs=fr(QD.t[:, h, :nt]), start=True, stop=False), reads=[Sst, QD], writes=[po])
                        S.op("pe", lambda e: e.matmul(pov[:, h, :nt], lhsT=fr(VNEW.t[:nt, h, :]), rhs=fr(ATT.t[:nt, h, :nt]), start=False, stop=True), reads=[VNEW, ATT], writes=[po])
                    if cut <= 12:
                        continue
                    pump()
                    pss = nps(0, 5); pssv = v4(pss)
                    for h in range(4):
                        S.op("pe", lambda e: e.matmul(pssv[:, h, :], lhsT=fr(KVS.t[:nt, 8 + h, :]), rhs=fr(VNEW.t[:nt, h, :]), start=True, stop=True), reads=[KVS, VNEW], writes=[pss])
                    S.op("dve", lambda e: e.tensor_tensor(out=fr(Sst.t[:, :, :]), in0=Sst.t[:, :, :], in1=bc(SM3.t[:, 0:4].unsqueeze(2), [128, 4, 128]), op=ALU.mult),
                         reads=[Sst, SM3], writes=[Sst])
                    S.op("dve", lambda e: e.tensor_tensor(out=fr(Sst.t[:, :, :]), in0=Sst.t[:, :, :], in1=pssv[:, :, :], op=ALU.add), reads=[Sst, pss], writes=[Sst])
                    if cut <= 13:
                        continue
                    pump()
                    S.op("act", lambda e: e.activation(out=fr(SQ.t[:, 0:4, :nt]), in_=pov[:, :, :nt], func=AF.Square), reads=[po], writes=[SQ])
                    pm = nps(0, 5); pmv = v4(pm)
                    S.op("pe", lambda e: e.matmul(pmv[:, :, :nt], lhsT=fr(onesr.t[:, :]), rhs=fr(SQ.t[:, 0:4, :nt]), start=True, stop=True), reads=[onesr, SQ], writes=[pm])
                    S.op("act", lambda e: e.activation(out=CT.t[:, 4:8, :nt], in_=pmv[:, :, :nt], func=AF.Ln, bias=1e-6, scale=1.0 / 128.0), reads=[pm], writes=[CT])
                    S.op("act", lambda e: e.activation(out=CT.t[:, 4:8, :nt], in_=CT.t[:, 4:8, :nt], func=AF.Exp, scale=-0.5), reads=[CT], writes=[CT])
                    S.op("act", lambda e: e.activation(out=ZS.t[:, :, :nt], in_=PJ.t[:, 12:16, :nt], func=AF.Silu), reads=[PJ], writes=[ZS])
                    S.op("dve", lambda e: e.scalar_tensor_tensor(out=T1.t[:, :, :nt], in0=pov[:, :, :nt], scalar=GNW.t[:, 0:1], in1=CT.t[:, 4:8, :nt],
                                                                 op0=ALU.mult, op1=ALU.mult), reads=[po, GNW, CT], writes=[T1])
                    S.op("dve", lambda e: e.tensor_tensor(out=YM.t[:, 4:8, :nt], in0=T1.t[:, :, :nt], in1=ZS.t[:, :, :nt], op=ALU.mult), reads=[T1, ZS], writes=[YM])
                    if cut <= 14:
                        continue
                    pump()
                    py = [nps(0, 5), nps(0, 5)]
                    for half in range(2):
                        for k in range(8):
                            S.op("pe", lambda e: e.matmul(py[half].t[:nt, :], lhsT=YM.t[:, k, :nt], rhs=WOUT.t[:, k, half * 512:(half + 1) * 512],
                                                          start=(k == 0), stop=(k == 7)), reads=[YM, WOUT], writes=[py[half]])
                    for half in range(2):
                        S.op("dve", lambda e: e.scalar_tensor_tensor(out=R.t[:nt, half * 512:(half + 1) * 512], in0=X.t[:nt, half * 512:(half + 1) * 512], scalar=ALPHA,
                                                                     in1=py[half].t[:nt, :], op0=ALU.mult, op1=ALU.add), reads=[X, py[half]], writes=[R])
                    layer_norm(lnst, R, nt, G1, B1)
                    S.dma("sp", lambda e: e.dma_start(out=hB[row0:row0 + nt, :], in_=R.t[:nt, :]), R, reads=[R])
                    if dbg and l == 0:
                        S.dma("sp", lambda e: e.dma_start(out=dbg_out["d_h2"][row0:row0 + nt, :], in_=R.t[:nt, :]), R, reads=[R])
                    if fnext is not None:
                        for _ in fnext:
                            pass
                S.barrier()

            with ExitStack() as st:
                if skip_p2:
                    break
                WQ = mk(st, "WQ", [128, 8, 2048], BF16)
                for k in range(8):
                    S.dma("pool", lambda e: e.dma_start(out=WQ.t[:, k, :], in_=peer_w_q[l, k * 128:(k + 1) * 128, :]), WQ, writes=[WQ])
                G2 = mk(st, "ln2g", [128, D]); B2 = mk(st, "ln2b", [128, D])
                S.dma("sp", lambda e: e.dma_start(out=G2.t[:], in_=ln2_g[l].partition_broadcast(128)), G2, writes=[G2])
                S.dma("sp", lambda e: e.dma_start(out=B2.t[:], in_=ln2_b[l].partition_broadcast(128)), B2, writes=[B2])
                KT = [mk(st, "KT%d" % i, [128, 128]) for i in range(2)]
                ktmp = mk(st, "ktmp", [128, 128])
                for i, kd in enumerate((peer_k1, peer_k2)):
                    S.dma("sp", lambda e: e.dma_start(out=ktmp.t[:], in_=kd[l]), ktmp, writes=[ktmp])
                    p = nps(0, 2)
                    S.op("pe", lambda e: e.transpose(out=p.t[:, 0:128], in_=ktmp.t[:, :], identity=ident.t[:, :]), reads=[ktmp, ident], writes=[p])
                    S.op("act", lambda e: e.copy(out=KT[i].t[:], in_=p.t[:, 0:128]), reads=[p], writes=[KT[i]])
                XIN = [mk(st, "x2in%d" % i, [128, D]) for i in range(2)]
                R2 = mk(st, "R2", [128, D])
                H2T = mk(st, "H2T", [128, 8, 128], BF16)
                H2B = mk(st, "H2B", [128, D], BF16)
                QT = mk(st, "QT", [128, 16, 128])
                SS = mk(st, "SS", [128, 16, 128])
                SR = mk(st, "SR", [128, 256])
                TV = mk(st, "TV", [128, 16, 16])
                TI = mk(st, "TI", [128, 16, 16], U32)
                TIF = mk(st, "TIF", [128, 16, 16])
                CAND = mk(st, "CAND", [128, 8, 256])
                OH = mk(st, "OH", [128, 8, 256])
                TOPV = mk(st, "TOPV", [128, 8, 16])
                CI = mk(st, "CI", [128, 8, 16], U32)
                II = mk(st, "II", [128, 8, 16], U32); JJ = mk(st, "JJ", [128, 8, 16], U32)
                IIF = mk(st, "IIF", [128, 8, 16]); JJF = mk(st, "JJF", [128, 8, 16])
                E1 = mk(st, "E1", [128, 8, 16]); E2 = mk(st, "E2", [128, 8, 16])
                GT = mk(st, "GT", [128, 8, 16]); GS = mk(st, "GS", [128, 8])
                GATE2 = [mk(st, "GATE%d" % i, [128, 128]) for i in range(2)]; IDX = mk(st, "IDX", [128, 128])
                IDXU2 = [mk(st, "IDXU%d" % i, [128, 128], U32) for i in range(2)]
                DG = [mk(st, "DG%d" % i, [128, 128], BF16) for i in range(NZ)]
                AALL = [mk(st, "AALL%d" % i, [128, 128]) for i in range(2)]
                ACTV = [mk(st, "ACTV%d" % i, [128, 128]) for i in range(2)]
                CC = [mk(st, "CC%d" % i, [128, 128]) for i in range(2)]
                ZB = [mk(st, "ZB%d" % i, [128, 255], BF16) for i in range(NZ)]
                JUNK = mk(st, "JUNK", [128, D], BF16)
                UV = [mk(st, "UV%d" % i, [128, 2 * D], BF16) for i in range(NSLOT)]
                lnst = (mk(st, "p2stats", [128, 2, 6]), mk(st, "p2mv", [128, 2]), mk(st, "p2rstd", [128, 1]))
                for z in ZB:
                    S.op("dve", lambda e: e.memset(z.t[:], 0.0), writes=[z])
                PSY = [PS[2], PS[3]]
                tokc = 0
                H2B2 = [H2B, mk(st, "H2Bb", [128, D], BF16)]
                fe_banks = [PS[0], PS[1], PS[4], PS[5], PS[6], PS[7]]
                fectr = [0]

                def fps():
                    p_ = fe_banks[fectr[0] % len(fe_banks)]
                    fectr[0] += 1
                    return p_

                def front(ti):
                    row0, nt = tiles[ti]
                    H2Bc = H2B2[ti % 2]
                    X = XIN[ti % 2]
                    IDXUc = IDXU2[ti % 2]; GATE = GATE2[ti % 2]
                    S.dma("sp", lambda e: e.dma_start(out=X.t[:nt, :], in_=hB[row0:row0 + nt, :]), X, writes=[X])
                    yield
                    for half in range(2):
                        p = fps(); pv = v4(p)
                        for c in range(4):
                            k = half * 4 + c
                            S.op("pe", lambda e: e.transpose(out=pv[:, c, :nt], in_=X.t[:nt, k * 128:(k + 1) * 128], identity=ident.t[:nt, :nt]), reads=[X, ident], writes=[p])
                        S.op("act", lambda e: e.copy(out=H2T.t[:, half * 4:(half + 1) * 4, :nt], in_=pv[:, :, :nt]), reads=[p], writes=[H2T])
                        yield
                    S.op("act", lambda e: e.copy(out=H2Bc.t[:nt, :], in_=X.t[:nt, :]), reads=[X], writes=[H2Bc])
                    yield
                    for g in range(4):
                        p = fps(); pv = v4(p)
                        for c in range(4):
                            blk = 4 * g + c
                            for k in range(8):
                                S.op("pe", lambda e: e.matmul(pv[:, c, :nt], lhsT=WQ.t[:, k, blk * 128:(blk + 1) * 128], rhs=H2T.t[:, k, :nt], start=(k == 0), stop=(k == 7)),
                                     reads=[WQ, H2T], writes=[p])
                            yield
                        S.op("act", lambda e: e.copy(out=QT.t[:, 4 * g:4 * g + 4, :nt], in_=pv[:, :, :nt]), reads=[p], writes=[QT])
                        yield
                    for g in range(4):
                        p = fps(); pv = v4(p)
                        for c in range(4):
                            blk = 4 * g + c
                            S.op("pe", lambda e: e.matmul(pv[:nt, c, :], lhsT=QT.t[:, blk, :nt], rhs=KT[blk % 2].t[:, :], start=True, stop=True), reads=[QT, KT[blk % 2]], writes=[p])
                        S.op("act", lambda e: e.copy(out=SS.t[:nt, 4 * g:4 * g + 4, :], in_=pv[:nt, :, :]), reads=[p], writes=[SS])
                        yield
                    for blk in range(16):
                        S.op("dve", lambda e: e.max(out=TV.t[:nt, blk, 0:8], in_=SS.t[:nt, blk, :]), reads=[SS], writes=[TV])
                        S.op("dve", lambda e: e.max_index(out=TI.t[:nt, blk, 0:8], in_max=TV.t[:nt, blk, 0:8], in_values=SS.t[:nt, blk, :]), reads=[SS, TV], writes=[TI])
                        S.op("dve", lambda e: e.match_replace(out=SR.t[:nt, 0:128], in_to_replace=TV.t[:nt, blk, 0:8], in_values=SS.t[:nt, blk, :], imm_value=-1e30),
                             reads=[SS, TV], writes=[SR])
                        S.op("dve", lambda e: e.max(out=TV.t[:nt, blk, 8:16], in_=SR.t[:nt, 0:128]), reads=[SR], writes=[TV])
                        S.op("dve", lambda e: e.max_index(out=TI.t[:nt, blk, 8:16], in_max=TV.t[:nt, blk, 8:16], in_values=SR.t[:nt, 0:128]), reads=[SR, TV], writes=[TI])
                        yield
                    tvv = TV.t[:].rearrange("p (h two) k -> p h two k", two=2)
                    candv = CAND.t[:].rearrange("p h (i j) -> p h i j", i=16)
                    S.op("dve", lambda e: e.tensor_tensor(out=candv[:nt], in0=bc(tvv[:nt, :, 0, :].unsqueeze(3), [nt, 8, 16, 16]),
                                                          in1=bc(tvv[:nt, :, 1, :].unsqueeze(2), [nt, 8, 16, 16]), op=ALU.add), reads=[TV], writes=[CAND])
                    yield
                    for h in range(8):
                        S.op("dve", lambda e: e.max(out=TOPV.t[:nt, h, 0:8], in_=CAND.t[:nt, h, :]), reads=[CAND], writes=[TOPV])
                        S.op("dve", lambda e: e.max_index(out=CI.t[:nt, h, 0:8], in_max=TOPV.t[:nt, h, 0:8], in_values=CAND.t[:nt, h, :]), reads=[CAND, TOPV], writes=[CI])
                        S.op("dve", lambda e: e.match_replace(out=SR.t[:nt, :], in_to_replace=TOPV.t[:nt, h, 0:8], in_values=CAND.t[:nt, h, :], imm_value=-1e30),
                             reads=[CAND, TOPV], writes=[SR])
                        S.op("dve", lambda e: e.max(out=TOPV.t[:nt, h, 8:16], in_=SR.t[:nt, :]), reads=[SR], writes=[TOPV])
                        S.op("dve", lambda e: e.max_index(out=CI.t[:nt, h, 8:16], in_max=TOPV.t[:nt, h, 8:16], in_values=SR.t[:nt, :]), reads=[SR, TOPV], writes=[CI])
                        yield
                    S.op("dve", lambda e: e.tensor_tensor(out=GT.t[:nt], in0=TOPV.t[:nt], in1=bc(TOPV.t[:nt, :, 0:1], [nt, 8, 16]), op=ALU.subtract), reads=[TOPV], writes=[GT])
                    S.op("act", lambda e: e.activation(out=GT.t[:nt], in_=GT.t[:nt], func=AF.Exp), reads=[GT], writes=[GT])
                    S.op("dve", lambda e: e.tensor_reduce(out=GS.t[:nt, :], in_=GT.t[:nt], axis=AX.X, op=ALU.add), reads=[GT], writes=[GS])
                    S.op("dve", lambda e: e.reciprocal(out=GS.t[:nt, :], in_=GS.t[:nt, :]), reads=[GS], writes=[GS])
                    S.op("dve", lambda e: e.tensor_tensor(out=GATE.t[:nt, :].rearrange("p (h k) -> p h k", h=8), in0=GT.t[:nt], in1=bc(GS.t[:nt, :].unsqueeze(2), [nt, 8, 16]), op=ALU.mult),
                         reads=[GT, GS], writes=[GATE])
                    yield
                    S.op("dve", lambda e: e.tensor_single_scalar(out=II.t[:nt], in_=CI.t[:nt], scalar=4, op=ALU.logical_shift_right), reads=[CI], writes=[II])
                    S.op("dve", lambda e: e.tensor_single_scalar(out=JJ.t[:nt], in_=CI.t[:nt], scalar=15, op=ALU.bitwise_and), reads=[CI], writes=[JJ])
                    S.op("dve", lambda e: e.tensor_copy(out=IIF.t[:nt], in_=II.t[:nt]), reads=[II], writes=[IIF])
                    S.op("dve", lambda e: e.tensor_copy(out=JJF.t[:nt], in_=JJ.t[:nt]), reads=[JJ], writes=[JJF])
                    S.op("dve", lambda e: e.tensor_copy(out=TIF.t[:nt], in_=TI.t[:nt]), reads=[TI], writes=[TIF])
                    yield
                    tif = TIF.t[:].rearrange("p (h two) k -> p h two k", two=2)
                    ohv = OH.t[:].rearrange("p h (r i) -> p h r i", r=16)
                    for which, (SEL, EOUT) in enumerate(((IIF, E1), (JJF, E2))):
                        S.op("dve", lambda e: e.tensor_tensor(out=ohv[:nt], in0=bc(SEL.t[:nt].unsqueeze(3), [nt, 8, 16, 16]),
                                                              in1=bc(iota16.t[:nt, :].unsqueeze(1).unsqueeze(1), [nt, 8, 16, 16]), op=ALU.is_equal), reads=[SEL, iota16], writes=[OH])
                        S.op("dve", lambda e: e.tensor_tensor(out=ohv[:nt], in0=ohv[:nt], in1=bc(tif[:nt, :, which, :].unsqueeze(2), [nt, 8, 16, 16]), op=ALU.mult),
                             reads=[OH, TIF], writes=[OH])
                        S.op("dve", lambda e: e.tensor_reduce(out=EOUT.t[:nt], in_=ohv[:nt], axis=AX.X, op=ALU.add), reads=[OH], writes=[EOUT])
                        yield
                    S.op("dve", lambda e: e.scalar_tensor_tensor(out=IDX.t[:nt, :], in0=E1.t[:nt].rearrange("p h k -> p (h k)"), scalar=128.0,
                                                                 in1=E2.t[:nt].rearrange("p h k -> p (h k)"), op0=ALU.mult, op1=ALU.add), reads=[E1, E2], writes=[IDX])
                    S.op("dve", lambda e: e.tensor_copy(out=IDXUc.t[:nt, :], in_=IDX.t[:nt, :]), reads=[IDX], writes=[IDXUc])
                    yield

                fe0 = front(0)
                for _ in fe0:
                    pass
                for ti, (row0, nt) in enumerate(tiles):
                    X = XIN[ti % 2]
                    H2Bc = H2B2[ti % 2]
                    IDXUc = IDXU2[ti % 2]; GATE = GATE2[ti % 2]
                    fe_next = front(ti + 1) if ti + 1 < len(tiles) else None
                    slots = {}
                    LAG = 6
                    PRE = 8

                    def emit_gather(j):
                        nonlocal tokc
                        sl = tokc % NSLOT
                        tokc += 1
                        slots[j] = sl
                        S.dma("pool", lambda e: e.indirect_dma_start(out=UV[sl].t[:], out_offset=None, in_=TAB,
                                                                     in_offset=bass.IndirectOffsetOnAxis(ap=IDXUc.t[:, j:j + 1], axis=0), element_offset=l * NEXP * 2 * D),
                              UV[sl], reads=[IDXUc], writes=[UV[sl]])

                    def emit_dot(j):
                        g0 = (j // G_TOK) * G_TOK
                        gi = (j // G_TOK) % 2
                        sl = slots[j]
                        edge = (j == g0) or (j == g0 + G_TOK - 1)
                        S.op("dve", lambda e: e.scalar_tensor_tensor(out=JUNK.t[:, :], in0=UV[sl].t[:, 0:D], scalar=1.0, in1=H2Bc.t[:, :], op0=ALU.mult, op1=ALU.mult,
                                                                     accum_out=AALL[gi].t[:, j:j + 1]),
                             reads=[UV[sl], H2Bc], writes=([AALL[gi], JUNK] if edge else [JUNK]))
                        if j == g0 + G_TOK - 1:
                            gs = slice(g0, g0 + G_TOK)
                            S.op("act", lambda e: e.activation(out=ACTV[gi].t[:, gs], in_=AALL[gi].t[:, gs], func=AF.Gelu), reads=[AALL[gi]], writes=[ACTV[gi]])
                            S.op("dve", lambda e: e.tensor_tensor(out=CC[gi].t[:, gs], in0=ACTV[gi].t[:, gs], in1=GATE.t[:, gs], op=ALU.mult), reads=[ACTV[gi], GATE], writes=[CC[gi]])

                    def emit_y(j):
                        gi = (j // G_TOK) % 2
                        sl = slots[j]
                        dg = DG[j % NZ]
                        S.op("act", lambda e: e.activation(out=dg.t[:, :], in_=identb.t[:, :], func=AF.Copy, scale=CC[gi].t[:, j:j + 1]), reads=[CC[gi], identb], writes=[dg])
                        for half in range(2):
                            S.op("pe", lambda e: e.matmul(PSY[half].t[:, :], lhsT=dg.t[:, :], rhs=UV[sl].t[:, D + half * 512:D + (half + 1) * 512],
                                                          start=(j == 0), stop=(j == 127)), reads=[dg, UV[sl]], writes=[PSY[half]])

                    for j in range(PRE):
                        emit_gather(j)
                    for i in range(128 + LAG):
                        if i + PRE < 128:
                            emit_gather(i + PRE)
                        if i < 128:
                            emit_dot(i)
                        if i - LAG >= 0:
                            emit_y(i - LAG)
                        if fe_next is not None and i % 2 == 1:
                            next(fe_next, None)
                    for half in range(2):
                        S.op("dve", lambda e: e.scalar_tensor_tensor(out=R2.t[:nt, half * 512:(half + 1) * 512], in0=X.t[:nt, half * 512:(half + 1) * 512], scalar=ALPHA,
                                                                     in1=PSY[half].t[:nt, :], op0=ALU.mult, op1=ALU.add), reads=[X, PSY[half]], writes=[R2])
                    layer_norm(lnst, R2, nt, G2, B2, eng2="dve")
                    if ti == 0:
                        S.op("dve", lambda e: e.tensor_scalar(out=R2.t[:], in0=R2.t[:], scalar1=padmask.t[:, 0:1], scalar2=None, op0=ALU.mult), reads=[R2, padmask], writes=[R2])
                    if l < depth - 1:
                        S.dma("sp", lambda e: e.dma_start(out=hA[row0:row0 + nt, :], in_=R2.t[:nt, :]), R2, reads=[R2])
                    elif ti > 0:
                        S.dma("sp", lambda e: e.dma_start(out=out[(ti - 1) * 128:ti * 128, :], in_=R2.t[:nt, :]), R2, reads=[R2])
                    if dbg and l == 0:
                        S.dma("sp", lambda e: e.dma_start(out=dbg_out["d_h3"][row0:row0 + nt, :], in_=R2.t[:nt, :]), R2, reads=[R2])
                    if fe_next is not None:
                        for _ in fe_next:
                            pass
                S.barrier()
        print("instructions ~", S.ninst)
    return nc


_INPUT_NAMES = ["meta_tokens", "ln_in_g", "ln_in_b", "w_in", "conv_w", "gdn_conv_w", "a_log", "dt_bias", "gdn_norm_w", "w_out",
                "ln1_g", "ln1_b", "peer_w_q", "peer_k1", "peer_k2", "peer_u", "peer_v", "ln2_g", "ln2_b"]


def kernel(**inputs):
    x = np.asarray(inputs["x"], dtype=np.float32)
    B = x.shape[0]
    shared = {k: np.ascontiguousarray(np.asarray(inputs[k], dtype=np.float32)) for k in _INPUT_NAMES}
    nc = build(NT=SEQ // 128)
    in_maps = []
    for c in range(B):
        m = dict(shared)
        m["x"] = np.ascontiguousarray(x[c])
        in_maps.append(m)
    res = run_bass_kernel_spmd(nc, in_maps, core_ids=list(range(B)))
    outs = [np.asarray(res.results[c]["out"], dtype=np.float32) for c in range(B)]
    return np.stack(outs, axis=0)
```

```python
import numpy as np
from contextlib import ExitStack
import concourse.bass as bass
import concourse.mybir as mybir
from concourse.bass_utils import run_bass_kernel_spmd

F32 = mybir.dt.float32
F32R = mybir.dt.float32r
BF16 = mybir.dt.bfloat16
U32 = mybir.dt.uint32
I32 = mybir.dt.int32
AF = mybir.ActivationFunctionType
ALU = mybir.AluOpType
AX = mybir.AxisListType

D = 1024
PC = 3592
NMETA = 16
SEQ = 8192
DEPTH = 2
ALPHA = (2.0 * DEPTH) ** 0.25
BIG = 30000.0
NEXP = 16384
G_TOK = 8
NSLOT = 20
NZ = 4


class Buf:
    def __init__(self, name):
        self.name = name
        self.w = None
        self.r = []
        self.dsem = None
        self.dcnt = 0
        self.excl = False


class T:
    def __init__(self, t, name):
        self.t = t
        self.b = Buf(name)


class Sched:
    def __init__(self, nc, stack):
        self.nc = nc
        self.stack = stack
        self.eng = {}
        for nm, e in [("pe", nc.tensor), ("act", nc.scalar), ("dve", nc.vector), ("pool", nc.gpsimd), ("sp", nc.sync)]:
            sem = stack.enter_context(nc.semaphore("s_" + nm))
            self.eng[nm] = dict(e=e, sem=sem, cnt=0, waited={}, name=nm)
        self.pool = []
        self.ninst = 0

    def _wait(self, E, deps):
        for (sem, val) in deps:
            key = id(sem)
            if E["waited"].get(key, 0) < val:
                E["e"].wait_ge(sem, val)
                E["waited"][key] = val
                self.ninst += 1

    def _deps(self, E, reads, writes):
        deps = []
        for b in reads:
            if b.w is not None:
                deps.append(b.w)
        for b in writes:
            if b.w is not None:
                deps.append(b.w)
            deps.extend(b.r)
        if E["name"] == "pe":
            deps = [d for d in deps if d[0] is not E["sem"]]
        return deps

    def _mark(self, tok, reads, writes):
        for b in reads:
            b.r = [x for x in b.r if x[0] is not tok[0]]
            b.r.append(tok)
        for b in writes:
            b.w = tok
            b.r = []

    def op(self, en, fn, reads=(), writes=()):
        E = self.eng[en]
        reads = [x.b if isinstance(x, T) else x for x in reads]
        writes = [x.b if isinstance(x, T) else x for x in writes]
        writes = writes + [b for b in reads if b.excl and b not in writes]
        reads = [b for b in reads if not b.excl]
        self._wait(E, self._deps(E, reads, writes))
        inst = fn(E["e"])
        E["cnt"] += 1
        inst.then_inc(E["sem"], 1)
        self.ninst += 1
        self._mark((E["sem"], E["cnt"]), reads, writes)
        return inst

    def dma(self, en, fn, owner, reads=(), writes=()):
        E = self.eng[en]
        owner = owner.b if isinstance(owner, T) else owner
        reads = [x.b if isinstance(x, T) else x for x in reads]
        writes = [x.b if isinstance(x, T) else x for x in writes]
        self._wait(E, self._deps(E, reads, writes))
        kind = "sw" if en == "pool" else "hw"
        if owner.dsem is None:
            owner.dsem = {}
        if kind not in owner.dsem:
            free = [p for p in self.pool if p["owner"] is None and p["kind"] == kind]
            if free:
                ent = free[0]
            else:
                ent = dict(sem=self.stack.enter_context(self.nc.semaphore("dq%d" % len(self.pool))), cnt=0, owner=None, kind=kind)
                self.pool.append(ent)
            ent["owner"] = owner
            owner.dsem[kind] = ent
        ent = owner.dsem[kind]
        inst = fn(E["e"])
        ent["cnt"] += 16
        inst.then_inc(ent["sem"], 16)
        self.ninst += 1
        self._mark((ent["sem"], ent["cnt"]), reads, writes)
        return inst

    def barrier(self, release=True):
        toks = [(E["sem"], E["cnt"]) for E in self.eng.values() if E["cnt"] > 0]
        toks += [(p["sem"], p["cnt"]) for p in self.pool if p["cnt"] > 0]
        for E in self.eng.values():
            self._wait(E, [t for t in toks if t[0] is not E["sem"]])
        if release:
            for p in self.pool:
                if p["owner"] is not None:
                    p["owner"].dsem = None
                    p["owner"] = None


def fr(ap):
    return ap.bitcast(F32R)


def bc(ap, shape):
    return ap.to_broadcast(list(shape))


def build(NT=64, depth=DEPTH, dbg=False, skip_p2=False, skip_p1=False, cut=99):
    nc = bass.Bass("TRN2", target_bir_lowering=False)
    LTOK = 128 * (NT + 1)
    dr = {}

    def din(name, shape, dt=F32):
        dr[name] = nc.dram_tensor(name, list(shape), dt, kind="ExternalInput").ap()
        return dr[name]

    x = din("x", [128 * NT, D])
    meta = din("meta_tokens", [NMETA, D])
    ln_in_g = din("ln_in_g", [D]); ln_in_b = din("ln_in_b", [D])
    w_in = din("w_in", [DEPTH, D, PC])
    conv_w = din("conv_w", [DEPTH, 3, 512])
    gdn_conv_w = din("gdn_conv_w", [DEPTH, 4, 1536])
    a_log = din("a_log", [DEPTH, 4]); dt_bias = din("dt_bias", [DEPTH, 4])
    gdn_norm_w = din("gdn_norm_w", [DEPTH, 128])
    w_out = din("w_out", [DEPTH, D, D])
    ln1_g = din("ln1_g", [DEPTH, D]); ln1_b = din("ln1_b", [DEPTH, D])
    peer_w_q = din("peer_w_q", [DEPTH, D, 2048])
    peer_k1 = din("peer_k1", [DEPTH, 128, 128]); peer_k2 = din("peer_k2", [DEPTH, 128, 128])
    peer_u = din("peer_u", [DEPTH, NEXP, D]); peer_v = din("peer_v", [DEPTH, NEXP, D])
    ln2_g = din("ln2_g", [DEPTH, D]); ln2_b = din("ln2_b", [DEPTH, D])
    out = nc.dram_tensor("out", [128 * NT, D], F32, kind="ExternalOutput").ap()
    hA = nc.dram_tensor("hA", [LTOK, D], F32, kind="Internal").ap()
    hB = nc.dram_tensor("hB", [LTOK, D], F32, kind="Internal").ap()
    TAB = nc.dram_tensor("TAB", [DEPTH * NEXP, 2 * D], BF16, kind="Internal").ap()
    dbg_out = {}
    if dbg:
        dbg_out["d_h0"] = nc.dram_tensor("d_h0", [LTOK, D], F32, kind="ExternalOutput").ap()
        dbg_out["d_h2"] = nc.dram_tensor("d_h2", [LTOK, D], F32, kind="ExternalOutput").ap()
        dbg_out["d_h3"] = nc.dram_tensor("d_h3", [LTOK, D], F32, kind="ExternalOutput").ap()

    tiles = [(128 * i, 128) for i in range(NT + 1)]

    top = ExitStack()
    with top:
        S = Sched(nc, top)

        uniq = [0]

        def mk(st, name, shape, dt=F32):
            uniq[0] += 1
            name = "%s_%d" % (name, uniq[0])
            return T(st.enter_context(nc.sbuf_tensor(name, list(shape), dt)), name)

        PSB = top.enter_context(nc.psum_tensor("psb", [128, 4096], F32))
        PS = [T(PSB[:, 512 * i:512 * (i + 1)], "ps%d" % i) for i in range(8)]
        XBV = [PSB[:, 2048 + 1024 * j:2048 + 1024 * (j + 1)] for j in range(2)]
        for p_ in PS:
            p_.b.excl = True
        psctr = [0]

        def nps(lo=0, hi=8):
            i = lo + psctr[0] % (hi - lo)
            psctr[0] += 1
            return PS[i]

        def v4(p):
            return p.t[:].rearrange("p (a b) -> p a b", a=4)

        ident = mk(top, "ident", [128, 128])
        identb = mk(top, "identb", [128, 128], BF16)
        ones = mk(top, "ones", [128, 128])
        LT = mk(top, "LT", [128, 128])
        UTS = mk(top, "UTS", [128, 128])
        MPOS = mk(top, "MPOS", [128, 128])
        MNEG = mk(top, "MNEG", [128, 128])
        iot = mk(top, "iot", [128, 128], I32)
        iof = mk(top, "iof", [128, 128])
        iota16 = mk(top, "iota16", [128, 16])
        S.op("pool", lambda e: e.iota(iot.t[:], pattern=[[1, 128]], base=0, channel_multiplier=-1), writes=[iot])
        S.op("dve", lambda e: e.tensor_copy(out=iof.t[:], in_=iot.t[:]), reads=[iot], writes=[iof])
        S.op("dve", lambda e: e.tensor_scalar(out=ident.t[:], in0=iof.t[:], scalar1=0.0, scalar2=None, op0=ALU.is_equal), reads=[iof], writes=[ident])
        S.op("dve", lambda e: e.tensor_copy(out=identb.t[:], in_=ident.t[:]), reads=[ident], writes=[identb])
        S.op("dve", lambda e: e.memset(ones.t[:], 1.0), writes=[ones])
        onesr = mk(top, "onesr", [128, 128])
        S.op("dve", lambda e: e.tensor_scalar(out=fr(onesr.t[:]), in0=iof.t[:], scalar1=0.0, scalar2=1.0, op0=ALU.mult, op1=ALU.add), reads=[iof], writes=[onesr])
        S.op("dve", lambda e: e.tensor_scalar(out=LT.t[:], in0=iof.t[:], scalar1=0.0, scalar2=None, op0=ALU.is_ge), reads=[iof], writes=[LT])
        S.op("dve", lambda e: e.tensor_scalar(out=UTS.t[:], in0=iof.t[:], scalar1=0.0, scalar2=None, op0=ALU.is_lt), reads=[iof], writes=[UTS])
        S.op("dve", lambda e: e.tensor_scalar(out=MPOS.t[:], in0=iof.t[:], scalar1=0.0, scalar2=BIG, op0=ALU.is_ge, op1=ALU.mult), reads=[iof], writes=[MPOS])
        S.op("dve", lambda e: e.tensor_scalar(out=MNEG.t[:], in0=iof.t[:], scalar1=0.0, scalar2=-BIG, op0=ALU.is_lt, op1=ALU.mult), reads=[iof], writes=[MNEG])
        padmask = mk(top, "padmask", [128, 1])
        S.op("dve", lambda e: e.tensor_scalar(out=padmask.t[:], in0=iof.t[:, 0:1], scalar1=-111.5, scalar2=None, op0=ALU.is_lt), reads=[iof], writes=[padmask])
        iot2 = mk(top, "iot2", [128, 16], I32)
        S.op("pool", lambda e: e.iota(iot2.t[:], pattern=[[1, 16]], base=0, channel_multiplier=0), writes=[iot2])
        S.op("dve", lambda e: e.tensor_copy(out=iota16.t[:], in_=iot2.t[:]), reads=[iot2], writes=[iota16])

        def layer_norm(st_tiles, X, nt, Gt, Bt, eng2="pool"):
            stats, mv, rstd = st_tiles
            for c in range(2):
                S.op("dve", lambda e: e.bn_stats(out=stats.t[:nt, c, :], in_=X.t[:nt, c * 512:(c + 1) * 512]), reads=[X], writes=[stats])
            S.op("dve", lambda e: e.bn_aggr(out=mv.t[:nt, :], in_=stats.t[:nt].rearrange("p a b -> p (a b)")), reads=[stats], writes=[mv])
            S.op("act", lambda e: e.activation(out=rstd.t[:nt, :], in_=mv.t[:nt, 1:2], func=AF.Sqrt, bias=1e-5, scale=1.0), reads=[mv], writes=[rstd])
            S.op("dve", lambda e: e.reciprocal(out=rstd.t[:nt, :], in_=rstd.t[:nt, :]), reads=[rstd], writes=[rstd])
            S.op("dve", lambda e: e.tensor_scalar(out=X.t[:nt, :], in0=X.t[:nt, :], scalar1=mv.t[:nt, 0:1], scalar2=rstd.t[:nt, 0:1],
                                                  op0=ALU.subtract, op1=ALU.mult), reads=[X, mv, rstd], writes=[X])
            S.op(eng2, lambda e: e.tensor_tensor(out=X.t[:nt, :], in0=X.t[:nt, :], in1=Gt.t[:nt, :], op=ALU.mult), reads=[X, Gt], writes=[X])
            S.op(eng2, lambda e: e.tensor_tensor(out=X.t[:nt, :], in0=X.t[:nt, :], in1=Bt.t[:nt, :], op=ALU.add), reads=[X, Bt], writes=[X])

        def make_T(X, nt, HT, lo=0, hi=8):
            for half in range(2):
                p = nps(lo, hi)
                pv = v4(p)
                for c in range(4):
                    k = half * 4 + c
                    S.op("pe", lambda e: e.transpose(out=pv[:, c, :nt], in_=X.t[:nt, k * 128:(k + 1) * 128], identity=ident.t[:nt, :nt]),
                         reads=[X, ident], writes=[p])
                S.op("act", lambda e: e.copy(out=HT.t[:, half * 4:(half + 1) * 4, :nt], in_=pv[:, :, :nt]), reads=[p], writes=[HT])

        with ExitStack() as st:
            g0 = mk(st, "lnin_g", [128, D]); b0 = mk(st, "lnin_b", [128, D])
            S.dma("sp", lambda e: e.dma_start(out=g0.t[:], in_=ln_in_g.partition_broadcast(128)), g0, writes=[g0])
            S.dma("sp", lambda e: e.dma_start(out=b0.t[:], in_=ln_in_b.partition_broadcast(128)), b0, writes=[b0])
            XI = [mk(st, "p0x%d" % i, [128, D]) for i in range(3)]
            STG = [mk(st, "stg%d" % i, [128, 8, D], BF16) for i in range(3)]
            tabv = TAB.rearrange("(l p r) w -> l p r w", l=DEPTH, p=128)
            ci_ = 0
            for l in range(depth):
                for src, off in ((peer_u, 0), (peer_v, D)):
                    srcv = src[l].rearrange("(p r) d -> p r d", p=128)
                    for c in range(16):
                        sg = STG[ci_ % 3]
                        ci_ += 1
                        S.dma("pool", lambda e: e.dma_start(out=sg.t[:], in_=srcv[:, 8 * c:8 * c + 8, :]), sg, writes=[sg])
                        S.dma("act", lambda e: e.dma_start(out=tabv[l][:, 8 * c:8 * c + 8, off:off + D], in_=sg.t[:]), sg, reads=[sg])
            lnst = (mk(st, "p0stats", [128, 2, 6]), mk(st, "p0mv", [128, 2]), mk(st, "p0rstd", [128, 1]))
            for ti, (row0, nt) in enumerate(tiles):
                X = XI[ti % 3]
                if ti == 0:
                    S.op("dve", lambda e: e.memset(X.t[:], 0.0), writes=[X])
                    S.dma("sp", lambda e: e.dma_start(out=X.t[112:128, :], in_=meta[:, :]), X, writes=[X])
                else:
                    S.dma("sp", lambda e: e.dma_start(out=X.t[:nt, :], in_=x[(ti - 1) * 128:ti * 128, :]), X, writes=[X])
                layer_norm(lnst, X, nt, g0, b0)
                if ti == 0:
                    S.op("dve", lambda e: e.tensor_scalar(out=X.t[:], in0=X.t[:], scalar1=padmask.t[:, 0:1], scalar2=None, op0=ALU.mult), reads=[X, padmask], writes=[X])
                S.dma("sp", lambda e: e.dma_start(out=hA[row0:row0 + nt, :], in_=X.t[:nt, :]), X, reads=[X])
                if dbg:
                    S.dma("sp", lambda e: e.dma_start(out=dbg_out["d_h0"][row0:row0 + nt, :], in_=X.t[:nt, :]), X, reads=[X])
            S.barrier()

        for l in range(depth):
            with ExitStack() as st:
                if skip_p1:
                    break
                WIN = mk(st, "WIN", [128, 8, PC], BF16)
                WOUT = mk(st, "WOUT", [128, 8, D], BF16)
                for k in range(8):
                    S.dma("pool", lambda e: e.dma_start(out=WIN.t[:, k, :], in_=w_in[l, k * 128:(k + 1) * 128, :]), WIN, writes=[WIN])
                    S.dma("pool", lambda e: e.dma_start(out=WOUT.t[:, k, :], in_=w_out[l, k * 128:(k + 1) * 128, :]), WOUT, writes=[WOUT])
                G1 = mk(st, "ln1g", [128, D]); B1 = mk(st, "ln1b", [128, D])
                S.dma("sp", lambda e: e.dma_start(out=G1.t[:], in_=ln1_g[l].partition_broadcast(128)), G1, writes=[G1])
                S.dma("sp", lambda e: e.dma_start(out=B1.t[:], in_=ln1_b[l].partition_broadcast(128)), B1, writes=[B1])
                CW = mk(st, "CW", [128, 4, 3]); GW = mk(st, "GW", [128, 12, 4])
                for j in range(3):
                    S.dma("sp", lambda e: e.dma_start(out=CW.t[:, :, j], in_=conv_w[l, j].rearrange("(b p) -> p b", p=128), allow_slow_non_contiguous=True), CW, writes=[CW])
                for j in range(4):
                    S.dma("sp", lambda e: e.dma_start(out=GW.t[:, :, j], in_=gdn_conv_w[l, j].rearrange("(b p) -> p b", p=128), allow_slow_non_contiguous=True), GW, writes=[GW])
                GNW = mk(st, "GNW", [128, 1])
                S.dma("sp", lambda e: e.dma_start(out=GNW.t[:], in_=gdn_norm_w[l].rearrange("(p o) -> p o", o=1)), GNW, writes=[GNW])
                NEGA = mk(st, "NEGA", [128, 4]); DTB = mk(st, "DTB", [128, 4])
                S.dma("sp", lambda e: e.dma_start(out=NEGA.t[:], in_=a_log[l].partition_broadcast(128)), NEGA, writes=[NEGA])
                S.dma("sp", lambda e: e.dma_start(out=DTB.t[:], in_=dt_bias[l].partition_broadcast(128)), DTB, writes=[DTB])
                S.op("act", lambda e: e.activation(out=NEGA.t[:], in_=NEGA.t[:], func=AF.Exp), reads=[NEGA], writes=[NEGA])
                S.op("dve", lambda e: e.tensor_scalar(out=NEGA.t[:], in0=NEGA.t[:], scalar1=-1.0, scalar2=None, op0=ALU.mult), reads=[NEGA], writes=[NEGA])

                XIN = [mk(st, "xin%d" % i, [128, D]) for i in range(2)]
                R = mk(st, "R", [128, D])
                HT = mk(st, "HT", [128, 8, 128], BF16)
                PJ = mk(st, "PJ", [128, 16, 128])
                QKVP = mk(st, "QKVP", [128, 12, 131])
                UC = mk(st, "UC", [128, 4, 130])
                CV = mk(st, "CV", [128, 4, 128])
                CA = mk(st, "CA", [128, 12, 128])
                CT = mk(st, "CT", [128, 12, 128])
                SQ = mk(st, "SQ", [128, 8, 128])
                QKN = mk(st, "QKN", [128, 8, 128])
                KV = mk(st, "KV", [128, 8, 128])
                KVS = mk(st, "KVS", [128, 12, 128])
                DD = mk(st, "DD", [128, 8, 128])
                GH = mk(st, "GH", [128, 4, 128])
                EGB = mk(st, "EGB", [128, 4, 128])
                Pm = [mk(st, "Pm%d" % i, [128, 4, 128]) for i in range(2)]
                PTm = [mk(st, "PTm%d" % i, [128, 4, 128]) for i in range(2)]
                XT = mk(st, "XT", [128, 4, 128])
                NWT = mk(st, "NWT", [128, 4, 128])
                VNEW = mk(st, "VNEW", [128, 4, 128])
                ATT = mk(st, "ATT", [128, 4, 128])
                QD = mk(st, "QD", [128, 4, 128])
                Sst = mk(st, "Sst", [128, 4, 128])
                ZS = mk(st, "ZS", [128, 4, 128])
                T1 = mk(st, "T1", [128, 4, 128])
                YM = mk(st, "YM", [128, 8, 128], BF16)
                SM = mk(st, "SM", [128, 12])
                SM2 = mk(st, "SM2", [128, 16])
                SM3 = mk(st, "SM3", [128, 4])
                lnst = (mk(st, "p1stats", [128, 2, 6]), mk(st, "p1mv", [128, 2]), mk(st, "p1rstd", [128, 1]))
                for h in range(4):
                    S.op("dve", lambda e: e.tensor_scalar(out=fr(Sst.t[:, h, :]), in0=iof.t[:, :], scalar1=0.0, scalar2=None, op0=ALU.mult), reads=[iof], writes=[Sst])
                S.op("dve", lambda e: e.memset(QKVP.t[:], 0.0), writes=[QKVP])
                S.op("dve", lambda e: e.memset(UC.t[:], 0.0), writes=[UC])

                PJ2 = [PJ, mk(st, "PJb", [128, 16, 128])]
                QKVP2 = [QKVP, mk(st, "QKVPb", [128, 12, 131])]
                HT2 = [HT, mk(st, "HTb", [128, 8, 128], BF16)]
                PAB = [PS[7], PS[7]]
                S.op("dve", lambda e: e.memset(QKVP2[1].t[:], 0.0), writes=[QKVP2[1]])

                def front1(ti):
                    row0, nt = tiles[ti]
                    X = XIN[ti % 2]
                    HTc = HT2[ti % 2]; PJc = PJ2[ti % 2]; QKVPc = QKVP2[ti % 2]; pab = PAB[ti % 2]
                    S.dma("sp", lambda e: e.dma_start(out=X.t[:nt, :], in_=hA[row0:row0 + nt, :]), X, writes=[X])
                    yield
                    make_T(X, nt, HTc, lo=5, hi=7)
                    yield
                    for g in range(7):
                        p = nps(5, 7); pv = v4(p)
                        for c in range(4):
                            blk = g * 4 + c
                            for k in range(8):
                                S.op("pe", lambda e: e.matmul(pv[:, c, :nt], lhsT=WIN.t[:, k, blk * 128:(blk + 1) * 128], rhs=HTc.t[:, k, :nt],
                                                              start=(k == 0), stop=(k == 7)), reads=[WIN, HTc], writes=[p])
                            yield
                        if g < 3:
                            dst, dT = PJc.t[:, 4 * g:4 * g + 4, :nt], PJc
                        elif g < 6:
                            dst, dT = QKVPc.t[:, 4 * (g - 3):4 * (g - 3) + 4, 3:3 + nt], QKVPc
                        else:
                            dst, dT = PJc.t[:, 12:16, :nt], PJc
                        if g % 2 == 0:
                            S.op("act", lambda e: e.copy(out=dst, in_=pv[:, :, :nt]), reads=[p], writes=[dT])
                        else:
                            S.op("dve", lambda e: e.tensor_copy(out=dst, in_=pv[:, :, :nt]), reads=[p], writes=[dT])
                    for k in range(8):
                        S.op("pe", lambda e: e.matmul(pab.t[:nt, 0:8], lhsT=HTc.t[:, k, :nt], rhs=WIN.t[:, k, 3584:3592], start=(k == 0), stop=(k == 7)),
                             reads=[WIN, HTc], writes=[pab])
                    if ti > 0:
                        QKVPp = QKVP2[(ti - 1) % 2]
                        S.op("act", lambda e: e.copy(out=QKVPc.t[:, :, 0:3], in_=QKVPp.t[:, :, nt:nt + 3]), reads=[QKVPp], writes=[QKVPc])
                    yield

                f0_ = front1(0)
                for _ in f0_:
                    pass
                for ti, (row0, nt) in enumerate(tiles):
                    X = XIN[ti % 2]
                    PJ = PJ2[ti % 2]; QKVP = QKVP2[ti % 2]; pab = PAB[ti % 2]
                    fnext = front1(ti + 1) if ti + 1 < len(tiles) else None

                    def pump(n=2):
                        if fnext is not None:
                            for _ in range(n):
                                next(fnext, None)
                    S.op("dve", lambda e: e.tensor_tensor(out=UC.t[:, :, 2:2 + nt], in0=PJ.t[:, 4:8, :nt], in1=PJ.t[:, 8:12, :nt], op=ALU.mult), reads=[PJ], writes=[UC])
                    for j in range(3):
                        if j == 0:
                            S.op("dve", lambda e: e.tensor_tensor(out=CV.t[:, :, :nt], in0=UC.t[:, :, 0:nt], in1=bc(CW.t[:, :, 0:1], [128, 4, nt]), op=ALU.mult),
                                 reads=[UC, CW], writes=[CV])
                        else:
                            S.op("dve", lambda e: e.tensor_tensor(out=CT.t[:, 0:4, :nt], in0=UC.t[:, :, j:j + nt], in1=bc(CW.t[:, :, j:j + 1], [128, 4, nt]), op=ALU.mult),
                                 reads=[UC, CW], writes=[CT])
                            S.op("dve", lambda e: e.tensor_tensor(out=CV.t[:, :, :nt], in0=CV.t[:, :, :nt], in1=CT.t[:, 0:4, :nt], op=ALU.add), reads=[CV, CT], writes=[CV])
                    S.op("dve", lambda e: e.tensor_tensor(out=YM.t[:, 0:4, :nt], in0=PJ.t[:, 0:4, :nt], in1=CV.t[:, :, :nt], op=ALU.mult), reads=[PJ, CV], writes=[YM])
                    S.op("dve", lambda e: e.tensor_copy(out=UC.t[:, :, 0:2], in_=UC.t[:, :, nt:nt + 2]), reads=[UC], writes=[UC])
                    if cut <= 3:
                        continue
                    pump()
                    for j in range(4):
                        if j == 0:
                            S.op("dve", lambda e: e.tensor_tensor(out=CA.t[:, :, :nt], in0=QKVP.t[:, :, 0:nt], in1=bc(GW.t[:, :, 0:1], [128, 12, nt]), op=ALU.mult),
                                 reads=[QKVP, GW], writes=[CA])
                        else:
                            S.op("dve", lambda e: e.tensor_tensor(out=CT.t[:, :, :nt], in0=QKVP.t[:, :, j:j + nt], in1=bc(GW.t[:, :, j:j + 1], [128, 12, nt]), op=ALU.mult),
                                 reads=[QKVP, GW], writes=[CT])
                            S.op("dve", lambda e: e.tensor_tensor(out=CA.t[:, :, :nt], in0=CA.t[:, :, :nt], in1=CT.t[:, :, :nt], op=ALU.add), reads=[CA, CT], writes=[CA])
                    S.op("act", lambda e: e.activation(out=CA.t[:, :, :nt], in_=CA.t[:, :, :nt], func=AF.Silu), reads=[CA], writes=[CA])
                    if cut <= 4:
                        continue
                    pump()
                    S.op("act", lambda e: e.activation(out=fr(SQ.t[:, 0:8, :nt]), in_=CA.t[:, 0:8, :nt], func=AF.Square), reads=[CA], writes=[SQ])
                    for half in range(2):
                        p = nps(0, 5); pv = v4(p)
                        S.op("pe", lambda e: e.matmul(pv[:, :, :nt], lhsT=fr(onesr.t[:, :]), rhs=fr(SQ.t[:, 4 * half:4 * half + 4, :nt]), start=True, stop=True),
                             reads=[onesr, SQ], writes=[p])
                        S.op("act", lambda e: e.activation(out=CT.t[:, 4 * half:4 * half + 4, :nt], in_=pv[:, :, :nt], func=AF.Ln, bias=1e-6, scale=1.0),
                             reads=[p], writes=[CT])
                    S.op("act", lambda e: e.activation(out=CT.t[:, 0:8, :nt], in_=CT.t[:, 0:8, :nt], func=AF.Exp, scale=-0.5), reads=[CT], writes=[CT])
                    S.op("dve", lambda e: e.scalar_tensor_tensor(out=fr(QKN.t[:, 0:4, :nt]), in0=CA.t[:, 0:4, :nt], scalar=128.0 ** -0.5, in1=CT.t[:, 0:4, :nt],
                                                                 op0=ALU.mult, op1=ALU.mult), reads=[CA, CT], writes=[QKN])
                    S.op("dve", lambda e: e.tensor_tensor(out=fr(QKN.t[:, 4:8, :nt]), in0=CA.t[:, 4:8, :nt], in1=CT.t[:, 4:8, :nt], op=ALU.mult), reads=[CA, CT], writes=[QKN])
                    if cut <= 5:
                        continue
                    pump()
                    for which in range(2):
                        p = nps(0, 5); pv = v4(p)
                        for h in range(4):
                            src = QKN.t[:, 4 + h, :nt] if which == 0 else CA.t[:, 8 + h, :nt]
                            S.op("pe", lambda e: e.transpose(out=pv[:nt, h, :], in_=src, identity=ident.t[:, :]), reads=[QKN, CA, ident], writes=[p])
                        S.op("act", lambda e: e.copy(out=KV.t[:nt, 4 * which:4 * which + 4, :], in_=pv[:nt, :, :]), reads=[p], writes=[KV])
                    if cut <= 6:
                        continue
                    pump()
                    S.op("dve", lambda e: e.tensor_tensor(out=SM.t[:nt, 0:4], in0=pab.t[:nt, 0:4], in1=DTB.t[:nt, :], op=ALU.add), reads=[pab, DTB], writes=[SM])
                    S.op("act", lambda e: e.activation(out=SM.t[:nt, 0:4], in_=SM.t[:nt, 0:4], func=AF.Exp), reads=[SM], writes=[SM])
                    S.op("act", lambda e: e.activation(out=SM.t[:nt, 0:4], in_=SM.t[:nt, 0:4], func=AF.Ln, bias=1.0, scale=1.0), reads=[SM], writes=[SM])
                    S.op("dve", lambda e: e.tensor_tensor(out=SM.t[:nt, 4:8], in0=SM.t[:nt, 0:4], in1=NEGA.t[:nt, :], op=ALU.mult), reads=[SM, NEGA], writes=[SM])
                    S.op("act", lambda e: e.activation(out=SM.t[:nt, 8:12], in_=pab.t[:nt, 4:8], func=AF.Sigmoid), reads=[pab], writes=[SM])
                    pg = nps(0, 5)
                    S.op("pe", lambda e: e.matmul(pg.t[:nt, 0:4], lhsT=LT.t[:nt, :nt], rhs=SM.t[:nt, 4:8], start=True, stop=True), reads=[LT, SM], writes=[pg])
                    S.op("pe", lambda e: e.matmul(pg.t[:nt, 4:8], lhsT=UTS.t[:nt, :nt], rhs=SM.t[:nt, 4:8], start=True, stop=True), reads=[UTS, SM], writes=[pg])
                    S.op("pe", lambda e: e.matmul(pg.t[:, 8:12], lhsT=ones.t[:nt, :], rhs=SM.t[:nt, 4:8], start=True, stop=True), reads=[ones, SM], writes=[pg])
                    S.op("dve", lambda e: e.tensor_copy(out=SM2.t[:nt, 0:4], in_=pg.t[:nt, 0:4]), reads=[pg], writes=[SM2])
                    S.op("act", lambda e: e.activation(out=SM2.t[:nt, 4:12], in_=pg.t[:nt, 0:8], func=AF.Exp), reads=[pg], writes=[SM2])
                    S.op("act", lambda e: e.activation(out=SM3.t[:, 0:4], in_=pg.t[:, 8:12], func=AF.Exp), reads=[pg], writes=[SM3])
                    S.op("dve", lambda e: e.tensor_tensor(out=SM2.t[:nt, 12:16], in0=SM.t[:nt, 8:12], in1=SM2.t[:nt, 4:8], op=ALU.mult), reads=[SM, SM2], writes=[SM2])
                    if cut <= 7:
                        continue
                    S.op("dve", lambda e: e.tensor_tensor(out=GH.t[:nt, :, :nt], in0=bc(LT.t[:nt, :nt].unsqueeze(1), [nt, 4, nt]),
                                                          in1=bc(SM.t[:nt, 4:8].unsqueeze(2), [nt, 4, nt]), op=ALU.mult), reads=[LT, SM], writes=[GH])
                    pc_ = nps(0, 5); pcv = v4(pc_)
                    S.op("pe", lambda e: e.matmul(pcv[:, :, :nt], lhsT=ones.t[:nt, :], rhs=GH.t[:nt, :, :nt], start=True, stop=True), reads=[ones, GH], writes=[pc_])
                    S.op("act", lambda e: e.activation(out=EGB.t[:, :, :nt], in_=pcv[:, :, :nt], func=AF.Exp), reads=[pc_], writes=[EGB])
                    for h in range(4):
                        S.op("dve", lambda e: e.scalar_tensor_tensor(out=DD.t[:nt, h, :nt], in0=pcv[:nt, h, :nt], scalar=SM2.t[:nt, h:h + 1], in1=MPOS.t[:nt, :nt],
                                                                     op0=ALU.subtract, op1=ALU.add), reads=[pc_, SM2, MPOS], writes=[DD])
                        S.op("dve", lambda e: e.scalar_tensor_tensor(out=DD.t[:nt, 4 + h, :nt], in0=pcv[:nt, h, :nt], scalar=SM2.t[:nt, h:h + 1], in1=MNEG.t[:nt, :nt],
                                                                     op0=ALU.subtract, op1=ALU.add), reads=[pc_, SM2, MNEG], writes=[DD])
                    S.op("act", lambda e: e.activation(out=DD.t[:nt, 0:4, :nt], in_=DD.t[:nt, 0:4, :nt], func=AF.Exp, scale=-1.0), reads=[DD], writes=[DD])
                    S.op("act", lambda e: e.activation(out=DD.t[:nt, 4:8, :nt], in_=DD.t[:nt, 4:8, :nt], func=AF.Exp), reads=[DD], writes=[DD])
                    if cut <= 8:
                        continue
                    pump()
                    pk = nps(0, 5); pkv = v4(pk)
                    for h in range(4):
                        S.op("pe", lambda e: e.matmul(pkv[:nt, h, :nt], lhsT=fr(QKN.t[:, 4 + h, :nt]), rhs=fr(QKN.t[:, 4 + h, :nt]), start=True, stop=True), reads=[QKN], writes=[pk])
                    for h in range(4):
                        S.op("dve", lambda e: e.scalar_tensor_tensor(out=fr(Pm[0].t[:nt, h, :nt]), in0=pkv[:nt, h, :nt], scalar=SM.t[:nt, 8 + h:9 + h], in1=DD.t[:nt, h, :nt],
                                                                     op0=ALU.mult, op1=ALU.mult), reads=[pk, SM, DD], writes=[Pm[0]])
                    pt = nps(0, 5); ptv = v4(pt)
                    for h in range(4):
                        S.op("pe", lambda e: e.transpose(out=ptv[:nt, h, :nt], in_=Pm[0].t[:nt, h, :nt], identity=ident.t[:nt, :nt]), reads=[Pm[0], ident], writes=[pt])
                    S.op("act", lambda e: e.copy(out=fr(PTm[0].t[:nt, :, :nt]), in_=ptv[:nt, :, :nt]), reads=[pt], writes=[PTm[0]])
                    S.op("dve", lambda e: e.tensor_tensor(out=fr(XT.t[:nt, :, :nt]), in0=bc(ident.t[:nt, :nt].unsqueeze(1), [nt, 4, nt]), in1=ptv[:nt, :, :nt], op=ALU.subtract),
                         reads=[ident, pt], writes=[XT])
                    smax = 6 if nt == 128 else 3
                    for s in range(1, smax + 1):
                        cur, nxt = (s - 1) % 2, s % 2
                        pump(2)
                        pp = nps(0, 5); ppv = v4(pp)
                        for h in range(4):
                            S.op("pe", lambda e: e.matmul(ppv[:nt, h, :nt], lhsT=fr(PTm[cur].t[:nt, h, :nt]), rhs=fr(Pm[cur].t[:nt, h, :nt]), start=True, stop=True),
                                 reads=[PTm[cur], Pm[cur]], writes=[pp])
                        S.op("act", lambda e: e.copy(out=fr(Pm[nxt].t[:nt, :, :nt]), in_=ppv[:nt, :, :nt]), reads=[pp], writes=[Pm[nxt]])
                        if s < smax:
                            pq = nps(0, 5); pqv = v4(pq)
                            for h in range(4):
                                S.op("pe", lambda e: e.matmul(pqv[:nt, h, :nt], lhsT=fr(Pm[cur].t[:nt, h, :nt]), rhs=fr(PTm[cur].t[:nt, h, :nt]), start=True, stop=True),
                                     reads=[PTm[cur], Pm[cur]], writes=[pq])
                            S.op("dve", lambda e: e.tensor_copy(out=fr(PTm[nxt].t[:nt, :, :nt]), in_=pqv[:nt, :, :nt]), reads=[pq], writes=[PTm[nxt]])
                        px = nps(0, 5); pxv = v4(px)
                        for h in range(4):
                            S.op("pe", lambda e: e.matmul(pxv[:nt, h, :nt], lhsT=fr(Pm[nxt].t[:nt, h, :nt]), rhs=fr(XT.t[:nt, h, :nt]), start=True, stop=True),
                                 reads=[Pm[nxt], XT], writes=[px])
                        S.op("dve", lambda e: e.tensor_tensor(out=fr(XT.t[:nt, :, :nt]), in0=pxv[:nt, :, :nt], in1=XT.t[:nt, :, :nt], op=ALU.add), reads=[px, XT], writes=[XT])
                    if cut <= 9:
                        continue
                    pump()
                    S.op("dve", lambda e: e.tensor_tensor(out=fr(KVS.t[:nt, 0:4, :]), in0=KV.t[:nt, 4:8, :], in1=bc(SM.t[:nt, 8:12].unsqueeze(2), [nt, 4, 128]), op=ALU.mult),
                         reads=[KV, SM], writes=[KVS])
                    S.op("dve", lambda e: e.tensor_tensor(out=fr(KVS.t[:nt, 4:8, :]), in0=KV.t[:nt, 0:4, :], in1=bc(SM2.t[:nt, 12:16].unsqueeze(2), [nt, 4, 128]), op=ALU.mult),
                         reads=[KV, SM2], writes=[KVS])
                    S.op("dve", lambda e: e.tensor_tensor(out=fr(KVS.t[:nt, 8:12, :]), in0=KV.t[:nt, 0:4, :], in1=bc(SM2.t[:nt, 8:12].unsqueeze(2), [nt, 4, 128]), op=ALU.mult),
                         reads=[KV, SM2], writes=[KVS])
                    if cut <= 10:
                        continue
                    pump()
                    pw = nps(0, 5); pwv = v4(pw)
                    for h in range(4):
                        S.op("pe", lambda e: e.matmul(pwv[:, h, :nt], lhsT=fr(KVS.t[:nt, 4 + h, :]), rhs=fr(XT.t[:nt, h, :nt]), start=True, stop=True), reads=[KVS, XT], writes=[pw])
                    S.op("act", lambda e: e.activation(out=fr(NWT.t[:, :, :nt]), in_=pwv[:, :, :nt], func=AF.Copy, scale=-1.0), reads=[pw], writes=[NWT])
                    pump()
                    pvn = nps(0, 5); pvnv = v4(pvn)
                    for h in range(4):
                        S.op("pe", lambda e: e.matmul(pvnv[:nt, h, :], lhsT=fr(XT.t[:nt, h, :nt]), rhs=fr(KVS.t[:nt, h, :]), start=True, stop=False), reads=[XT, KVS], writes=[pvn])
                        S.op("pe", lambda e: e.matmul(pvnv[:nt, h, :], lhsT=fr(NWT.t[:, h, :nt]), rhs=fr(Sst.t[:, h, :]), start=False, stop=True), reads=[NWT, Sst], writes=[pvn])
                    S.op("act", lambda e: e.copy(out=fr(VNEW.t[:nt, :, :]), in_=pvnv[:nt, :, :]), reads=[pvn], writes=[VNEW])
                    if cut <= 11:
                        continue
                    pump()
                    pa = nps(0, 5); pav = v4(pa)
                    for h in range(4):
                        S.op("pe", lambda e: e.matmul(pav[:nt, h, :nt], lhsT=fr(QKN.t[:, 4 + h, :nt]), rhs=fr(QKN.t[:, h, :nt]), start=True, stop=True), reads=[QKN], writes=[pa])
                    S.op("dve", lambda e: e.tensor_tensor(out=fr(ATT.t[:nt, :, :nt]), in0=pav[:nt, :, :nt], in1=DD.t[:nt, 4:8, :nt], op=ALU.mult), reads=[pa, DD], writes=[ATT])
                    S.op("dve", lambda e: e.tensor_tensor(out=fr(QD.t[:, :, :nt]), in0=QKN.t[:, 0:4, :nt], in1=EGB.t[:, :, :nt], op=ALU.mult), reads=[QKN, EGB], writes=[QD])
                    pump()
                    po = nps(0, 5); pov = v4(po)
                    for h in range(4):
                        S.op("pe", lambda e: e.matmul(pov[:, h, :nt], lhsT=fr(Sst.t[:, h, :]), rhs=fr(QD.t[:, h, :nt]), start=True, stop=False), reads=[Sst, QD], writes=[po])
                        S.op("pe", lambda e: e.matmul(pov[:, h, :nt], lhsT=fr(VNEW.t[:nt, h, :]), rhs=fr(ATT.t[:nt, h, :nt]), start=False, stop=True), reads=[VNEW, ATT], writes=[po])
                    if cut <= 12:
                        continue
                    pump()
                    pss = nps(0, 5); pssv = v4(pss)
                    for h in range(4):
                        S.op("pe", lambda e: e.matmul(pssv[:, h, :], lhsT=fr(KVS.t[:nt, 8 + h, :]), rhs=fr(VNEW.t[:nt, h, :]), start=True, stop=True), reads=[KVS, VNEW], writes=[pss])
                    S.op("dve", lambda e: e.tensor_tensor(out=fr(Sst.t[:, :, :]), in0=Sst.t[:, :, :], in1=bc(SM3.t[:, 0:4].unsqueeze(2), [128, 4, 128]), op=ALU.mult),
                         reads=[Sst, SM3], writes=[Sst])
                    S.op("dve", lambda e: e.tensor_tensor(out=fr(Sst.t[:, :, :]), in0=Sst.t[:, :, :], in1=pssv[:, :, :], op=ALU.add), reads=[Sst, pss], writes=[Sst])
                    if cut <= 13:
                        continue
                    pump()
                    S.op("act", lambda e: e.activation(out=fr(SQ.t[:, 0:4, :nt]), in_=pov[:, :, :nt], func=AF.Square), reads=[po], writes=[SQ])
                    pm = nps(0, 5); pmv = v4(pm)
                    S.op("pe", lambda e: e.matmul(pmv[:, :, :nt], lhsT=fr(onesr.t[:, :]), rhs=fr(SQ.t[:, 0:4, :nt]), start=True, stop=True), reads=[onesr, SQ], writes=[pm])
                    S.op("act", lambda e: e.activation(out=CT.t[:, 4:8, :nt], in_=pmv[:, :, :nt], func=AF.Ln, bias=1e-6, scale=1.0 / 128.0), reads=[pm], writes=[CT])
                    S.op("act", lambda e: e.activation(out=CT.t[:, 4:8, :nt], in_=CT.t[:, 4:8, :nt], func=AF.Exp, scale=-0.5), reads=[CT], writes=[CT])
                    S.op("act", lambda e: e.activation(out=ZS.t[:, :, :nt], in_=PJ.t[:, 12:16, :nt], func=AF.Silu), reads=[PJ], writes=[ZS])
                    S.op("dve", lambda e: e.scalar_tensor_tensor(out=T1.t[:, :, :nt], in0=pov[:, :, :nt], scalar=GNW.t[:, 0:1], in1=CT.t[:, 4:8, :nt],
                                                                 op0=ALU.mult, op1=ALU.mult), reads=[po, GNW, CT], writes=[T1])
                    S.op("dve", lambda e: e.tensor_tensor(out=YM.t[:, 4:8, :nt], in0=T1.t[:, :, :nt], in1=ZS.t[:, :, :nt], op=ALU.mult), reads=[T1, ZS], writes=[YM])
                    if cut <= 14:
                        continue
                    pump()
                    py = [nps(0, 5), nps(0, 5)]
                    for half in range(2):
                        for k in range(8):
                            S.op("pe", lambda e: e.matmul(py[half].t[:nt, :], lhsT=YM.t[:, k, :nt], rhs=WOUT.t[:, k, half * 512:(half + 1) * 512],
                                                          start=(k == 0), stop=(k == 7)), reads=[YM, WOUT], writes=[py[half]])
                    for half in range(2):
                        S.op("dve", lambda e: e.scalar_tensor_tensor(out=R.t[:nt, half * 512:(half + 1) * 512], in0=X.t[:nt, half * 512:(half + 1) * 512], scalar=ALPHA,
                                                                     in1=py[half].t[:nt, :], op0=ALU.mult, op1=ALU.add), reads=[X, py[half]], writes=[R])
                    layer_norm(lnst, R, nt, G1, B1)
                    S.dma("sp", lambda e: e.dma_start(out=hB[row0:row0 + nt, :], in_=R.t[:nt, :]), R, reads=[R])
                    if dbg and l == 0:
                        S.dma("sp", lambda e: e.dma_start(out=dbg_out["d_h2"][row0:row0 + nt, :], in_=R.t[:nt, :]), R, reads=[R])
                    if fnext is not None:
                        for _ in fnext:
                            pass
                S.barrier()

            with ExitStack() as st:
                if skip_p2:
                    break
                WQ = mk(st, "WQ", [128, 8, 2048], BF16)
                for k in range(8):
                    S.dma("pool", lambda e: e.dma_start(out=WQ.t[:, k, :], in_=peer_w_q[l, k * 128:(k + 1) * 128, :]), WQ, writes=[WQ])
                G2 = mk(st, "ln2g", [128, D]); B2 = mk(st, "ln2b", [128, D])
                S.dma("sp", lambda e: e.dma_start(out=G2.t[:], in_=ln2_g[l].partition_broadcast(128)), G2, writes=[G2])
                S.dma("sp", lambda e: e.dma_start(out=B2.t[:], in_=ln2_b[l].partition_broadcast(128)), B2, writes=[B2])
                KT = [mk(st, "KT%d" % i, [128, 128]) for i in range(2)]
                ktmp = mk(st, "ktmp", [128, 128])
                for i, kd in enumerate((peer_k1, peer_k2)):
                    S.dma("sp", lambda e: e.dma_start(out=ktmp.t[:], in_=kd[l]), ktmp, writes=[ktmp])
                    p = nps(0, 2)
                    S.op("pe", lambda e: e.transpose(out=p.t[:, 0:128], in_=ktmp.t[:, :], identity=ident.t[:, :]), reads=[ktmp, ident], writes=[p])
                    S.op("act", lambda e: e.copy(out=KT[i].t[:], in_=p.t[:, 0:128]), reads=[p], writes=[KT[i]])
                XIN = [mk(st, "x2in%d" % i, [128, D]) for i in range(2)]
                R2 = mk(st, "R2", [128, D])
                H2T = mk(st, "H2T", [128, 8, 128], BF16)
                H2B = mk(st, "H2B", [128, D], BF16)
                QT = mk(st, "QT", [128, 16, 128])
                SS = mk(st, "SS", [128, 16, 128])
                SR = mk(st, "SR", [128, 256])
                TV = mk(st, "TV", [128, 16, 16])
                TI = mk(st, "TI", [128, 16, 16], U32)
                TIF = mk(st, "TIF", [128, 16, 16])
                CAND = mk(st, "CAND", [128, 8, 256])
                OH = mk(st, "OH", [128, 8, 256])
                TOPV = mk(st, "TOPV", [128, 8, 16])
                CI = mk(st, "CI", [128, 8, 16], U32)
                II = mk(st, "II", [128, 8, 16], U32); JJ = mk(st, "JJ", [128, 8, 16], U32)
                IIF = mk(st, "IIF", [128, 8, 16]); JJF = mk(st, "JJF", [128, 8, 16])
                E1 = mk(st, "E1", [128, 8, 16]); E2 = mk(st, "E2", [128, 8, 16])
                GT = mk(st, "GT", [128, 8, 16]); GS = mk(st, "GS", [128, 8])
                GATE2 = [mk(st, "GATE%d" % i, [128, 128]) for i in range(2)]; IDX = mk(st, "IDX", [128, 128])
                IDXU2 = [mk(st, "IDXU%d" % i, [128, 128], U32) for i in range(2)]
                DG = [mk(st, "DG%d" % i, [128, 128], BF16) for i in range(NZ)]
                AALL = [mk(st, "AALL%d" % i, [128, 128]) for i in range(2)]
                ACTV = [mk(st, "ACTV%d" % i, [128, 128]) for i in range(2)]
                CC = [mk(st, "CC%d" % i, [128, 128]) for i in range(2)]
                ZB = [mk(st, "ZB%d" % i, [128, 255], BF16) for i in range(NZ)]
                JUNK = mk(st, "JUNK", [128, D], BF16)
                UV = [mk(st, "UV%d" % i, [128, 2 * D], BF16) for i in range(NSLOT)]
                lnst = (mk(st, "p2stats", [128, 2, 6]), mk(st, "p2mv", [128, 2]), mk(st, "p2rstd", [128, 1]))
                for z in ZB:
                    S.op("dve", lambda e: e.memset(z.t[:], 0.0), writes=[z])
                PSY = [PS[2], PS[3]]
                tokc = 0
                H2B2 = [H2B, mk(st, "H2Bb", [128, D], BF16)]
                fe_banks = [PS[0], PS[1], PS[4], PS[5], PS[6], PS[7]]
                fectr = [0]

                def fps():
                    p_ = fe_banks[fectr[0] % len(fe_banks)]
                    fectr[0] += 1
                    return p_

                def front(ti):
                    row0, nt = tiles[ti]
                    H2Bc = H2B2[ti % 2]
                    X = XIN[ti % 2]
                    IDXUc = IDXU2[ti % 2]; GATE = GATE2[ti % 2]
                    S.dma("sp", lambda e: e.dma_start(out=X.t[:nt, :], in_=hB[row0:row0 + nt, :]), X, writes=[X])
                    yield
                    for half in range(2):
                        p = fps(); pv = v4(p)
                        for c in range(4):
                            k = half * 4 + c
                            S.op("pe", lambda e: e.transpose(out=pv[:, c, :nt], in_=X.t[:nt, k * 128:(k + 1) * 128], identity=ident.t[:nt, :nt]), reads=[X, ident], writes=[p])
                        S.op("act", lambda e: e.copy(out=H2T.t[:, half * 4:(half + 1) * 4, :nt], in_=pv[:, :, :nt]), reads=[p], writes=[H2T])
                        yield
                    S.op("act", lambda e: e.copy(out=H2Bc.t[:nt, :], in_=X.t[:nt, :]), reads=[X], writes=[H2Bc])
                    yield
                    for g in range(4):
                        p = fps(); pv = v4(p)
                        for c in range(4):
                            blk = 4 * g + c
                            for k in range(8):
                                S.op("pe", lambda e: e.matmul(pv[:, c, :nt], lhsT=WQ.t[:, k, blk * 128:(blk + 1) * 128], rhs=H2T.t[:, k, :nt], start=(k == 0), stop=(k == 7)),
                                     reads=[WQ, H2T], writes=[p])
                            yield
                        S.op("act", lambda e: e.copy(out=QT.t[:, 4 * g:4 * g + 4, :nt], in_=pv[:, :, :nt]), reads=[p], writes=[QT])
                        yield
                    for g in range(4):
                        p = fps(); pv = v4(p)
                        for c in range(4):
                            blk = 4 * g + c
                            S.op("pe", lambda e: e.matmul(pv[:nt, c, :], lhsT=QT.t[:, blk, :nt], rhs=KT[blk % 2].t[:, :], start=True, stop=True), reads=[QT, KT[blk % 2]], writes=[p])
                        S.op("act", lambda e: e.copy(out=SS.t[:nt, 4 * g:4 * g + 4, :], in_=pv[:nt, :, :]), reads=[p], writes=[SS])
                        yield
                    for blk in range(16):
                        S.op("dve", lambda e: e.max(out=TV.t[:nt, blk, 0:8], in_=SS.t[:nt, blk, :]), reads=[SS], writes=[TV])
                        S.op("dve", lambda e: e.max_index(out=TI.t[:nt, blk, 0:8], in_max=TV.t[:nt, blk, 0:8], in_values=SS.t[:nt, blk, :]), reads=[SS, TV], writes=[TI])
                        S.op("dve", lambda e: e.match_replace(out=SR.t[:nt, 0:128], in_to_replace=TV.t[:nt, blk, 0:8], in_values=SS.t[:nt, blk, :], imm_value=-1e30),
                             reads=[SS, TV], writes=[SR])
                        S.op("dve", lambda e: e.max(out=TV.t[:nt, blk, 8:16], in_=SR.t[:nt, 0:128]), reads=[SR], writes=[TV])
                        S.op("dve", lambda e: e.max_index(out=TI.t[:nt, blk, 8:16], in_max=TV.t[:nt, blk, 8:16], in_values=SR.t[:nt, 0:128]), reads=[SR, TV], writes=[TI])
                        yield
                    tvv = TV.t[:].rearrange("p (h two) k -> p h two k", two=2)
                    candv = CAND.t[:].rearrange("p h (i j) -> p h i j", i=16)
                    S.op("dve", lambda e: e.tensor_tensor(out=candv[:nt], in0=bc(tvv[:nt, :, 0, :].unsqueeze(3), [nt, 8, 16, 16]),
                                                          in1=bc(tvv[:nt, :, 1, :].unsqueeze(2), [nt, 8, 16, 16]), op=ALU.add), reads=[TV], writes=[CAND])
                    yield
                    for h in range(8):
                        S.op("dve", lambda e: e.max(out=TOPV.t[:nt, h, 0:8], in_=CAND.t[:nt, h, :]), reads=[CAND], writes=[TOPV])
                        S.op("dve", lambda e: e.max_index(out=CI.t[:nt, h, 0:8], in_max=TOPV.t[:nt, h, 0:8], in_values=CAND.t[:nt, h, :]), reads=[CAND, TOPV], writes=[CI])
                        S.op("dve", lambda e: e.match_replace(out=SR.t[:nt, :], in_to_replace=TOPV.t[:nt, h, 0:8], in_values=CAND.t[:nt, h, :], imm_value=-1e30),
                             reads=[CAND, TOPV], writes=[SR])
                        S.op("dve", lambda e: e.max(out=TOPV.t[:nt, h, 8:16], in_=SR.t[:nt, :]), reads=[SR], writes=[TOPV])
                        S.op("dve", lambda e: e.max_index(out=CI.t[:nt, h, 8:16], in_max=TOPV.t[:nt, h, 8:16], in_values=SR.t[:nt, :]), reads=[SR, TOPV], writes=[CI])
                        yield
                    S.op("dve", lambda e: e.tensor_tensor(out=GT.t[:nt], in0=TOPV.t[:nt], in1=bc(TOPV.t[:nt, :, 0:1], [nt, 8, 16]), op=ALU.subtract), reads=[TOPV], writes=[GT])
                    S.op("act", lambda e: e.activation(out=GT.t[:nt], in_=GT.t[:nt], func=AF.Exp), reads=[GT], writes=[GT])
                    S.op("dve", lambda e: e.tensor_reduce(out=GS.t[:nt, :], in_=GT.t[:nt], axis=AX.X, op=ALU.add), reads=[GT], writes=[GS])
                    S.op("dve", lambda e: e.reciprocal(out=GS.t[:nt, :], in_=GS.t[:nt, :]), reads=[GS], writes=[GS])
                    S.op("dve", lambda e: e.tensor_tensor(out=GATE.t[:nt, :].rearrange("p (h k) -> p h k", h=8), in0=GT.t[:nt], in1=bc(GS.t[:nt, :].unsqueeze(2), [nt, 8, 16]), op=ALU.mult),
                         reads=[GT, GS], writes=[GATE])
                    yield
                    S.op("dve", lambda e: e.tensor_single_scalar(out=II.t[:nt], in_=CI.t[:nt], scalar=4, op=ALU.logical_shift_right), reads=[CI], writes=[II])
                    S.op("dve", lambda e: e.tensor_single_scalar(out=JJ.t[:nt], in_=CI.t[:nt], scalar=15, op=ALU.bitwise_and), reads=[CI], writes=[JJ])
                    S.op("dve", lambda e: e.tensor_copy(out=IIF.t[:nt], in_=II.t[:nt]), reads=[II], writes=[IIF])
                    S.op("dve", lambda e: e.tensor_copy(out=JJF.t[:nt], in_=JJ.t[:nt]), reads=[JJ], writes=[JJF])
                    S.op("dve", lambda e: e.tensor_copy(out=TIF.t[:nt], in_=TI.t[:nt]), reads=[TI], writes=[TIF])
                    yield
                    tif = TIF.t[:].rearrange("p (h two) k -> p h two k", two=2)
                    ohv = OH.t[:].rearrange("p h (r i) -> p h r i", r=16)
                    for which, (SEL, EOUT) in enumerate(((IIF, E1), (JJF, E2))):
                        S.op("dve", lambda e: e.tensor_tensor(out=ohv[:nt], in0=bc(SEL.t[:nt].unsqueeze(3), [nt, 8, 16, 16]),
                                                              in1=bc(iota16.t[:nt, :].unsqueeze(1).unsqueeze(1), [nt, 8, 16, 16]), op=ALU.is_equal), reads=[SEL, iota16], writes=[OH])
                        S.op("dve", lambda e: e.tensor_tensor(out=ohv[:nt], in0=ohv[:nt], in1=bc(tif[:nt, :, which, :].unsqueeze(2), [nt, 8, 16, 16]), op=ALU.mult),
                             reads=[OH, TIF], writes=[OH])
                        S.op("dve", lambda e: e.tensor_reduce(out=EOUT.t[:nt], in_=ohv[:nt], axis=AX.X, op=ALU.add), reads=[OH], writes=[EOUT])
                        yield
                    S.op("dve", lambda e: e.scalar_tensor_tensor(out=IDX.t[:nt, :], in0=E1.t[:nt].rearrange("p h k -> p (h k)"), scalar=128.0,
                                                                 in1=E2.t[:nt].rearrange("p h k -> p (h k)"), op0=ALU.mult, op1=ALU.add), reads=[E1, E2], writes=[IDX])
                    S.op("dve", lambda e: e.tensor_copy(out=IDXUc.t[:nt, :], in_=IDX.t[:nt, :]), reads=[IDX], writes=[IDXUc])
                    yield

                fe0 = front(0)
                for _ in fe0:
                    pass
                for ti, (row0, nt) in enumerate(tiles):
                    X = XIN[ti % 2]
                    H2Bc = H2B2[ti % 2]
                    IDXUc = IDXU2[ti % 2]; GATE = GATE2[ti % 2]
                    fe_next = front(ti + 1) if ti + 1 < len(tiles) else None
                    slots = {}
                    LAG = 10
                    PRE = 8

                    def emit_gather(j):
                        nonlocal tokc
                        sl = tokc % NSLOT
                        tokc += 1
                        slots[j] = sl
                        S.dma("pool", lambda e: e.indirect_dma_start(out=UV[sl].t[:], out_offset=None, in_=TAB,
                                                                     in_offset=bass.IndirectOffsetOnAxis(ap=IDXUc.t[:, j:j + 1], axis=0), element_offset=l * NEXP * 2 * D),
                              UV[sl], reads=[IDXUc], writes=[UV[sl]])

                    def emit_dot(j):
                        g0 = (j // G_TOK) * G_TOK
                        gi = (j // G_TOK) % 2
                        sl = slots[j]
                        edge = (j == g0) or (j == g0 + G_TOK - 1)
                        S.op("dve", lambda e: e.scalar_tensor_tensor(out=JUNK.t[:, :], in0=UV[sl].t[:, 0:D], scalar=1.0, in1=H2Bc.t[:, :], op0=ALU.mult, op1=ALU.mult,
                                                                     accum_out=AALL[gi].t[:, j:j + 1]),
                             reads=[UV[sl], H2Bc], writes=([AALL[gi], JUNK] if edge else [JUNK]))
                        if j == g0 + G_TOK - 1:
                            gs = slice(g0, g0 + G_TOK)
                            S.op("act", lambda e: e.activation(out=ACTV[gi].t[:, gs], in_=AALL[gi].t[:, gs], func=AF.Gelu), reads=[AALL[gi]], writes=[ACTV[gi]])
                            S.op("dve", lambda e: e.tensor_tensor(out=CC[gi].t[:, gs], in0=ACTV[gi].t[:, gs], in1=GATE.t[:, gs], op=ALU.mult), reads=[ACTV[gi], GATE], writes=[CC[gi]])

                    def emit_y(j):
                        gi = (j // G_TOK) % 2
                        sl = slots[j]
                        dg = DG[j % NZ]
                        S.op("act", lambda e: e.activation(out=dg.t[:, :], in_=identb.t[:, :], func=AF.Copy, scale=CC[gi].t[:, j:j + 1]), reads=[CC[gi], identb], writes=[dg])
                        for half in range(2):
                            S.op("pe", lambda e: e.matmul(PSY[half].t[:, :], lhsT=dg.t[:, :], rhs=UV[sl].t[:, D + half * 512:D + (half + 1) * 512],
                                                          start=(j == 0), stop=(j == 127)), reads=[dg, UV[sl]], writes=[PSY[half]])

                    for j in range(PRE):
                        emit_gather(j)
                    for i in range(128 + LAG):
                        if i + PRE < 128:
                            emit_gather(i + PRE)
                        if i < 128:
                            emit_dot(i)
                        if i - LAG >= 0:
                            emit_y(i - LAG)
                        if fe_next is not None and i % 2 == 1:
                            next(fe_next, None)
                    for half in range(2):
                        S.op("dve", lambda e: e.scalar_tensor_tensor(out=R2.t[:nt, half * 512:(half + 1) * 512], in0=X.t[:nt, half * 512:(half + 1) * 512], scalar=ALPHA,
                                                                     in1=PSY[half].t[:nt, :], op0=ALU.mult, op1=ALU.add), reads=[X, PSY[half]], writes=[R2])
                    layer_norm(lnst, R2, nt, G2, B2, eng2="dve")
                    if ti == 0:
                        S.op("dve", lambda e: e.tensor_scalar(out=R2.t[:], in0=R2.t[:], scalar1=padmask.t[:, 0:1], scalar2=None, op0=ALU.mult), reads=[R2, padmask], writes=[R2])
                    if l < depth - 1:
                        S.dma("sp", lambda e: e.dma_start(out=hA[row0:row0 + nt, :], in_=R2.t[:nt, :]), R2, reads=[R2])
                    elif ti > 0:
                        S.dma("sp", lambda e: e.dma_start(out=out[(ti - 1) * 128:ti * 128, :], in_=R2.t[:nt, :]), R2, reads=[R2])
                    if dbg and l == 0:
                        S.dma("sp", lambda e: e.dma_start(out=dbg_out["d_h3"][row0:row0 + nt, :], in_=R2.t[:nt, :]), R2, reads=[R2])
                    if fe_next is not None:
                        for _ in fe_next:
                            pass
                S.barrier()
        print("instructions ~", S.ninst)
    return nc


_INPUT_NAMES = ["meta_tokens", "ln_in_g", "ln_in_b", "w_in", "conv_w", "gdn_conv_w", "a_log", "dt_bias", "gdn_norm_w", "w_out",
                "ln1_g", "ln1_b", "peer_w_q", "peer_k1", "peer_k2", "peer_u", "peer_v", "ln2_g", "ln2_b"]


def kernel(**inputs):
    x = np.asarray(inputs["x"], dtype=np.float32)
    B = x.shape[0]
    shared = {k: np.ascontiguousarray(np.asarray(inputs[k], dtype=np.float32)) for k in _INPUT_NAMES}
    nc = build(NT=SEQ // 128)
    in_maps = []
    for c in range(B):
        m = dict(shared)
        m["x"] = np.ascontiguousarray(x[c])
        in_maps.append(m)
    res = run_bass_kernel_spmd(nc, in_maps, core_ids=list(range(B)))
    outs = [np.asarray(res.results[c]["out"], dtype=np.float32) for c in range(B)]
    return np.stack(outs, axis=0)
```
